# Optimizing a Trainium2 kernel written in Bass

```python
import math
import jax, jax.numpy as jnp
from jax import lax
import numpy as np

D_MODEL = 1024
BATCH = 4
SEQ = 4096
DEPTH = 1
DEC_BATCH = 32
DEC_SEQ = 8
PAST_LEN = 8192
PAGE_SIZE = 128

D_MIX = 2 * D_MODEL
HEAD_DIM = 64
D_ATT = D_MIX // 2
N_ATT_HEADS = D_ATT // HEAD_DIM
DILATED_BRANCHES = ((128, 1), (512, 4), (2048, 16))
MAX_WINDOW = 2048
Q_BLOCK = 128
D_SSM = D_MIX - D_ATT
N_SSM_HEADS = D_SSM // HEAD_DIM
SSM_STATE = 128
N_SSM_GROUPS = 4
CONV_WIDTH = 4
SSD_CHUNK = 128
D_CONV = D_SSM + 2 * N_SSM_GROUPS * SSM_STATE
D_FF = ((-(-8 * D_MODEL // 3) + 255) // 256) * 256
D_IN_PROJ = 3 * D_ATT + D_SSM + D_CONV + N_SSM_HEADS
DEEPNORM_ALPHA = (2.0 * DEPTH) ** 0.25
DEEPNORM_BETA = (8.0 * DEPTH) ** -0.25
LN_EPS = 1e-5
RMS_EPS = 1e-5

kernel_name = "hymba_longnet_ssd_deepnorm_step"


def _alibi_slopes():
    n = N_ATT_HEADS
    return jnp.asarray(2.0 ** (-8.0 * np.arange(1, n + 1) / n), dtype=jnp.float32)


def _layer_norm(x, g, b):
    xf = x.astype(jnp.float32)
    mu = jnp.mean(xf, -1, keepdims=True)
    var = jnp.mean(jnp.square(xf - mu), -1, keepdims=True)
    y = (xf - mu) * lax.rsqrt(var + LN_EPS) * g.astype(jnp.float32) + b.astype(jnp.float32)
    return y.astype(x.dtype)


def _rms_norm(xf, g):
    return xf * lax.rsqrt(jnp.mean(jnp.square(xf), -1, keepdims=True) + RMS_EPS) * g.astype(jnp.float32)


def _dilated_block(q, qpos, k, v, slopes):
    qf = q.astype(jnp.float32) * (HEAD_DIM ** -0.5)
    outs, lses = [], []
    for window, dil in DILATED_BRANCHES:
        dist = jnp.arange(window // dil + 1, dtype=jnp.int32) * dil
        idx = qpos[:, None] - dist[None, :]
        valid = idx >= 0
        idx = jnp.clip(idx, 0, k.shape[1] - 1)
        kg = jnp.take(k, idx, axis=1).astype(jnp.float32)
        vg = jnp.take(v, idx, axis=1).astype(jnp.float32)
        s = jnp.einsum('bqhd,bqjhd->bhqj', qf, kg)
        s = s - slopes[:, None, None] * dist.astype(jnp.float32)[None, None, :]
        s = jnp.where(valid[None, None], s, -jnp.inf)
        m = jnp.max(s, -1, keepdims=True)
        p = jnp.exp(s - m)
        den = jnp.sum(p, -1, keepdims=True)
        outs.append(jnp.einsum('bhqj,bqjhd->bqhd', p / den, vg))
        lses.append((m + jnp.log(den))[..., 0])
    w = jax.nn.softmax(jnp.stack(lses, 0), axis=0)
    w = jnp.transpose(w, (0, 1, 3, 2))[..., None]
    return jnp.sum(w * jnp.stack(outs, 0), axis=0)


def _dilated_attention(q, k, v, q_start, slopes):
    b, t, h, dh = q.shape
    if t % Q_BLOCK == 0 and t > Q_BLOCK:
        nb = t // Q_BLOCK
        qb = jnp.moveaxis(q.reshape(b, nb, Q_BLOCK, h, dh), 1, 0)

        def one(args):
            q_blk, i = args
            qpos = q_start + i * Q_BLOCK + jnp.arange(Q_BLOCK, dtype=jnp.int32)
            return _dilated_block(q_blk, qpos, k, v, slopes)

        out = lax.map(one, (qb, jnp.arange(nb, dtype=jnp.int32)))
        return jnp.moveaxis(out, 0, 1).reshape(b, t, h, dh)
    qpos = q_start + jnp.arange(t, dtype=jnp.int32)
    return _dilated_block(q, qpos, k, v, slopes)


def _ssd_chunked(x, dt, a, b_in, c_in, h0):
    bsz, t, h, p = x.shape
    g, n = b_in.shape[2], b_in.shape[3]
    r = h // g
    L = SSD_CHUNK if t % SSD_CHUNK == 0 else t
    nc = t // L
    xdt = (x * dt[..., None]).reshape(bsz, nc, L, g, r, p)
    la = (dt * a).reshape(bsz, nc, L, g, r)
    bc = b_in.reshape(bsz, nc, L, g, n)
    cc = c_in.reshape(bsz, nc, L, g, n)
    cs = jnp.cumsum(la, axis=2)
    seg = cs[:, :, :, None] - cs[:, :, None, :]
    causal = jnp.tril(jnp.ones((L, L), dtype=bool))[None, None, :, :, None, None]
    decay = jnp.exp(jnp.where(causal, seg, -jnp.inf))
    cb = jnp.einsum('bclgn,bcsgn->bclsg', cc, bc)
    y_diag = jnp.einsum('bclsg,bclsgr,bcsgrp->bclgrp', cb, decay, xdt)
    to_end = jnp.exp(cs[:, :, -1:] - cs)
    chunk_states = jnp.einsum('bclgn,bclgr,bclgrp->bcgrpn', bc, to_end, xdt)
    chunk_decay = jnp.exp(cs[:, :, -1])

    def step(hc, inp):
        st, dec = inp
        return hc * dec[..., None, None] + st, hc

    h_fin, h_prev = lax.scan(step, h0.reshape(bsz, g, r, p, n),
                             (jnp.moveaxis(chunk_states, 1, 0), jnp.moveaxis(chunk_decay, 1, 0)))
    y_off = jnp.einsum('bclgn,cbgrpn,bclgr->bclgrp', cc, h_prev, jnp.exp(cs))
    y = (y_diag + y_off).reshape(bsz, t, h, p)
    return y, h_fin.reshape(bsz, h, p, n)


def _hybrid_layer(x, k_past, v_past, conv_prefix, h0, w_in, conv_w, conv_b, dt_bias, a_log,
                  d_skip, attn_norm_g, ssm_norm_g, w_out, ln1_g, ln1_b, w_gate, w_up, w_down,
                  ln2_g, ln2_b):
    f32 = jnp.float32
    bsz, t, _ = x.shape
    proj = x @ w_in
    cuts = np.cumsum([D_ATT, D_ATT, D_ATT, D_SSM, D_CONV])
    q, k, v, z, xbc_raw, dt_raw = jnp.split(proj, cuts, axis=-1)
    q = q.reshape(bsz, t, N_ATT_HEADS, HEAD_DIM)
    k = k.reshape(bsz, t, N_ATT_HEADS, HEAD_DIM)
    v = v.reshape(bsz, t, N_ATT_HEADS, HEAD_DIM)

    k_all = jnp.concatenate([k_past.astype(k.dtype), k], axis=1)
    v_all = jnp.concatenate([v_past.astype(v.dtype), v], axis=1)
    attn = _dilated_attention(q, k_all, v_all, k_past.shape[1], _alibi_slopes())
    attn = _rms_norm(attn.reshape(bsz, t, D_ATT), attn_norm_g)

    xpad = jnp.concatenate([conv_prefix.astype(f32), xbc_raw.astype(f32)], axis=1)
    conv_new = xpad[:, -(CONV_WIDTH - 1):]
    xbc = lax.conv_general_dilated(xpad, conv_w.astype(f32)[:, None, :], window_strides=(1,),
                                   padding='VALID', dimension_numbers=('NWC', 'WIO', 'NWC'),
                                   feature_group_count=D_CONV) + conv_b.astype(f32)
    xbc = jax.nn.silu(xbc)
    xs, bs, cs_ = jnp.split(xbc, [D_SSM, D_SSM + N_SSM_GROUPS * SSM_STATE], axis=-1)
    xs = xs.reshape(bsz, t, N_SSM_HEADS, HEAD_DIM)
    bs = bs.reshape(bsz, t, N_SSM_GROUPS, SSM_STATE)
    cs_ = cs_.reshape(bsz, t, N_SSM_GROUPS, SSM_STATE)
    dt = jax.nn.softplus(dt_raw.astype(f32) + dt_bias.astype(f32))
    a = -jnp.exp(a_log.astype(f32))
    y_ssm, h_fin = _ssd_chunked(xs, dt, a, bs, cs_, h0.astype(f32))
    y_ssm = (y_ssm + d_skip.astype(f32)[:, None] * xs).reshape(bsz, t, D_SSM)
    y_ssm = _rms_norm(y_ssm * jax.nn.silu(z.astype(f32)), ssm_norm_g)

    mix = jnp.concatenate([attn, y_ssm], axis=-1).astype(x.dtype) @ w_out
    hdn = _layer_norm(DEEPNORM_ALPHA * x + mix, ln1_g, ln1_b)
    ffn = (jax.nn.silu(hdn @ w_gate) * (hdn @ w_up)) @ w_down
    y = _layer_norm(DEEPNORM_ALPHA * hdn + ffn, ln2_g, ln2_b)
    return y, k, v, h_fin.astype(x.dtype), conv_new.astype(x.dtype)


def setup_inputs(seed: int = 0) -> dict:
    key = jax.random.key(seed)
    ks = jax.random.split(key, 24)
    nrm = jax.random.normal
    wb = min(MAX_WINDOW, PAST_LEN)
    dt0 = jnp.exp(jax.random.uniform(ks[10], (DEPTH, N_SSM_HEADS), minval=math.log(1e-3), maxval=math.log(1e-1)))
    return {
        "x_prompt": nrm(ks[0], (BATCH, SEQ, D_MODEL), jnp.float32),
        "x_sample": nrm(ks[1], (DEC_BATCH, DEC_SEQ, D_MODEL), jnp.float32),
        "cache_k_win": nrm(ks[2], (DEPTH, DEC_BATCH, wb, N_ATT_HEADS, HEAD_DIM), jnp.float32),
        "cache_v_win": nrm(ks[3], (DEPTH, DEC_BATCH, wb, N_ATT_HEADS, HEAD_DIM), jnp.float32),
        "state_ssm": 0.1 * nrm(ks[4], (DEPTH, DEC_BATCH, N_SSM_HEADS, HEAD_DIM, SSM_STATE), jnp.float32),
        "state_conv": nrm(ks[5], (DEPTH, DEC_BATCH, CONV_WIDTH - 1, D_CONV), jnp.float32),
        "w_in": nrm(ks[6], (DEPTH, D_MODEL, D_IN_PROJ), jnp.float32) * D_MODEL ** -0.5,
        "conv_w": nrm(ks[7], (DEPTH, CONV_WIDTH, D_CONV), jnp.float32) * CONV_WIDTH ** -0.5,
        "conv_b": 0.01 * nrm(ks[8], (DEPTH, D_CONV), jnp.float32),
        "dt_bias": dt0 + jnp.log(-jnp.expm1(-dt0)),
        "a_log": jnp.log(jax.random.uniform(ks[11], (DEPTH, N_SSM_HEADS), minval=1.0, maxval=16.0)),
        "d_skip": 1.0 + 0.1 * nrm(ks[12], (DEPTH, N_SSM_HEADS), jnp.float32),
        "attn_norm_g": 1.0 + 0.05 * nrm(ks[13], (DEPTH, D_ATT), jnp.float32),
        "ssm_norm_g": 1.0 + 0.05 * nrm(ks[14], (DEPTH, D_SSM), jnp.float32),
        "w_out": nrm(ks[15], (DEPTH, D_MIX, D_MODEL), jnp.float32) * (D_MIX ** -0.5 * DEEPNORM_BETA),
        "ln1_g": 1.0 + 0.05 * nrm(ks[16], (DEPTH, D_MODEL), jnp.float32),
        "ln1_b": 0.01 * nrm(ks[17], (DEPTH, D_MODEL), jnp.float32),
        "w_gate": nrm(ks[18], (DEPTH, D_MODEL, D_FF), jnp.float32) * D_MODEL ** -0.5,
        "w_up": nrm(ks[19], (DEPTH, D_MODEL, D_FF), jnp.float32) * D_MODEL ** -0.5,
        "w_down": nrm(ks[20], (DEPTH, D_FF, D_MODEL), jnp.float32) * (D_FF ** -0.5 * DEEPNORM_BETA),
        "ln2_g": 1.0 + 0.05 * nrm(ks[21], (DEPTH, D_MODEL), jnp.float32),
        "ln2_b": 0.01 * nrm(ks[22], (DEPTH, D_MODEL), jnp.float32),
    }


def reference(x_prompt, x_sample, cache_k_win, cache_v_win, state_ssm, state_conv, w_in, conv_w,
              conv_b, dt_bias, a_log, d_skip, attn_norm_g, ssm_norm_g, w_out, ln1_g, ln1_b,
              w_gate, w_up, w_down, ln2_g, ln2_b):
    bp, tp, _ = x_prompt.shape
    keep = min(MAX_WINDOW, tp)
    hp, hs = x_prompt, x_sample
    kp_l, vp_l, sp_l, cp_l, ks_l, vs_l, ss_l, cs_l = [], [], [], [], [], [], [], []
    for l in range(DEPTH):
        w = (w_in[l], conv_w[l], conv_b[l], dt_bias[l], a_log[l], d_skip[l], attn_norm_g[l],
             ssm_norm_g[l], w_out[l], ln1_g[l], ln1_b[l], w_gate[l], w_up[l], w_down[l],
             ln2_g[l], ln2_b[l])
        k0 = jnp.zeros((bp, 0, N_ATT_HEADS, HEAD_DIM), hp.dtype)
        c0 = jnp.zeros((bp, CONV_WIDTH - 1, D_CONV), hp.dtype)
        s0 = jnp.zeros((bp, N_SSM_HEADS, HEAD_DIM, SSM_STATE), hp.dtype)
        hp, kp, vp, sp, cp = _hybrid_layer(hp, k0, k0, c0, s0, *w)
        kp_l.append(kp[:, tp - keep:]); vp_l.append(vp[:, tp - keep:]); sp_l.append(sp); cp_l.append(cp)
        hs, kn, vn, sn, cn = _hybrid_layer(hs, cache_k_win[l], cache_v_win[l], state_conv[l],
                                           state_ssm[l], *w)
        ks_l.append(kn); vs_l.append(vn); ss_l.append(sn); cs_l.append(cn)
    return (hp, hs, jnp.stack(kp_l), jnp.stack(vp_l), jnp.stack(sp_l), jnp.stack(cp_l),
            jnp.stack(ks_l), jnp.stack(vs_l), jnp.stack(ss_l), jnp.stack(cs_l))
```

```python
import numpy as np
from contextlib import ExitStack
import concourse.bass as bass
import concourse.mybir as mybir
from concourse.bass_utils import run_bass_kernel_spmd

F32 = mybir.dt.float32
BF16 = mybir.dt.bfloat16
AF = mybir.ActivationFunctionType
ALU = mybir.AluOpType

D = 1024
T = 2048
NB = T // 128
DIN = 6160
DFF = 2816
NF = DFF // 128
ALPHA = 2.0 ** 0.25
LN_EPS = 1e-5
RMS_EPS = 1e-5
COL_Q, COL_K, COL_V, COL_Z, COL_XBC, COL_DT = 0, 1024, 2048, 3072, 4096, 6144
NPAIR = 8
SKIP_SSD = False


class Prog:
    NDMA = 24

    def __init__(self, nc, es):
        self.nc = nc
        self.eng = {"pe": nc.tensor, "act": nc.scalar, "dve": nc.vector, "pool": nc.gpsimd, "sp": nc.sync}
        self.ops = {e: [] for e in self.eng}
        self.sem = {e: es.enter_context(nc.semaphore("s_" + e)) for e in ("pe", "act", "dve", "pool")}
        self.cnt = {e: 0 for e in self.sem}
        self.seen = {e: {} for e in self.eng}
        self.lastw = {}
        self.readers = {}
        self.dsem = {q: [es.enter_context(nc.semaphore(f"d_{q}{i}")) for i in range(self.NDMA)] for q in ("sp", "pool")}
        self.dcnt = {q: [0] * self.NDMA for q in ("sp", "pool")}
        self.drr = {q: 0 for q in ("sp", "pool")}
        self.semobj = {}
        for e, s in self.sem.items():
            self.semobj[("c", e)] = s
        for q in self.dsem:
            for i, s in enumerate(self.dsem[q]):
                self.semobj[("d", q, i)] = s
        self.nops = 0

    def _deps(self, e, reads, writes):
        deps = {}

        def add(d):
            if d is None:
                return
            k, v = d
            if k == ("c", "pe") and e == "pe":
                return
            if deps.get(k, 0) < v:
                deps[k] = v

        for r in reads:
            add(self.lastw.get(r))
        for w in writes:
            add(self.lastw.get(w))
            for k, v in self.readers.get(w, {}).items():
                add((k, v))
        out = []
        for k, v in deps.items():
            if self.seen[e].get(k, 0) < v:
                self.seen[e][k] = v
                out.append((k, v))
        return out

    def _commit(self, dep, reads, writes):
        for w in writes:
            self.lastw[w] = dep
            self.readers[w] = {}
        for r in reads:
            if r in writes:
                continue
            d = self.readers.setdefault(r, {})
            if d.get(dep[0], 0) < dep[1]:
                d[dep[0]] = dep[1]

    def op(self, e, fn, reads=(), writes=(), inc=True):
        psr = [r for r in reads if isinstance(r, tuple) and r[0] == "ps"]
        if psr:
            reads = [r for r in reads if r not in psr]
            writes = list(writes) + psr
        waits = self._deps(e, reads, writes)
        if inc:
            self.cnt[e] += 1
            dep = (("c", e), self.cnt[e])
        else:
            assert e == "pe"
            dep = (("c", e), self.cnt[e] + 1)
        self.ops[e].append((waits, fn, (("c", e), 1) if inc else None))
        self._commit(dep, reads, writes)
        self.nops += 1

    def dma(self, q, out, in_, reads=(), writes=(), **kw):
        i = self.drr[q]
        self.drr[q] = (i + 1) % self.NDMA
        key = ("d", q, i)
        waits = self._deps(q, reads, writes)
        prev = self.dcnt[q][i]
        if prev and self.seen[q].get(key, 0) < prev:
            self.seen[q][key] = prev
            waits.append((key, prev))
        self.dcnt[q][i] += 16
        dep = (key, self.dcnt[q][i])

        def fn(eng, out=out, in_=in_, kw=kw):
            return eng.dma_start(out=out, in_=in_, **kw)

        self.ops[q].append((waits, fn, (key, 16)))
        self._commit(dep, reads, writes)
        self.nops += 1

    def flush(self):
        fin = []
        for q in self.dsem:
            for i in range(self.NDMA):
                if self.dcnt[q][i]:
                    fin.append((("d", q, i), self.dcnt[q][i]))
        for e in self.sem:
            if self.cnt[e]:
                fin.append((("c", e), self.cnt[e]))
        for e in self.eng:
            w = [(k, v) for k, v in fin if self.seen[e].get(k, 0) < v]
            for k, v in w:
                self.seen[e][k] = v
            self.ops[e].append((w, None, None))
        ops = self.ops
        self.ops = {e: [] for e in self.eng}
        semobj = self.semobj
        with self.nc.Block() as block:
            def mk(e):
                def body(eng):
                    for waits, fn, inc in ops[e]:
                        for k, v in waits:
                            eng.wait_ge(semobj[k], v)
                        if fn is None:
                            continue
                        ins = fn(eng)
                        if inc is not None:
                            ins.then_inc(semobj[inc[0]], inc[1])
                return body
            block.sync(mk("sp"))
            block.tensor(mk("pe"))
            block.scalar(mk("act"))
            block.vector(mk("dve"))
            block.gpsimd(mk("pool"))


class _Stop(Exception):
    pass


STOP = None


def build(dbg=None):
    try:
        return _build(dbg)
    except _Stop as ex:
        return ex.args[0]


def _build(dbg=None):
    dbg = dbg or {}
    nc = bass.Bass("TRN2", target_bir_lowering=False)

    def din(name, shape):
        return nc.dram_tensor(name, list(shape), F32, kind="ExternalInput").ap()

    def dout(name, shape):
        return nc.dram_tensor(name, list(shape), F32, kind="ExternalOutput").ap()

    xo = din("xo", [T, D]); xp = din("xp", [T, D]); flag_d = din("flag", [128, 1])
    xs_d = din("xs", [32, D])
    ck = din("ck", [4, 2048, D]); cv = din("cv", [4, 2048, D])
    sssm = din("sssm", [4, 1024, 128]); sconv = din("sconv", [12, 2048])
    w_in = din("w_in", [D, DIN]); conv_w = din("conv_w", [4, 2048]); conv_b = din("conv_b", [2048])
    dt_bias = din("dt_bias", [16]); a_log = din("a_log", [16]); d_skip = din("d_skip", [16])
    attn_g = din("attn_g", [1024]); ssm_g = din("ssm_g", [1024])
    w_out = din("w_out", [2048, D]); ln1_g = din("ln1_g", [D]); ln1_b = din("ln1_b", [D])
    w_gate = din("w_gate", [D, DFF]); w_up = din("w_up", [D, DFF]); w_down = din("w_down", [DFF, D])
    ln2_g = din("ln2_g", [D]); ln2_b = din("ln2_b", [D])

    y_o = dout("y_o", [T, D]); y_s = dout("y_s", [32, D])
    k_o = dout("k_o", [T, D]); v_o = dout("v_o", [T, D])
    ssm_o = dout("ssm_o", [1024, 128]); conv_o = dout("conv_o", [3, 2048])
    kn_s = dout("kn_s", [32, D]); vn_s = dout("vn_s", [32, D])
    ssm_s = dout("ssm_s", [4, 1024, 128]); conv_s = dout("conv_s", [12, 2048])
    scr_cs = nc.dram_tensor("scr_cs", [512, 128], F32, kind="Internal").ap()
    dbg_out = {k: dout(k, shp) for k, shp in dbg.items()}

    w_in_v = w_in.rearrange("(kc p) n -> p kc n", p=128)

    with ExitStack() as es:
        P = Prog(nc, es)

        def stop_at(name):
            if STOP == name:
                P.flush()
                raise _Stop(nc)

        def sb(st, name, shape, dt, side=None):
            if side is None:
                return st.enter_context(nc.sbuf_tensor("t_" + name, list(shape), dt))
            return st.enter_context(nc.sbuf_tensor("t_" + name, list(shape), dt, side=side))

        ps = [es.enter_context(nc.psum_tensor(f"ps{i}", [128, 512], F32)) for i in range(8)]
        psb = [p[:, :].bitcast(BF16) for p in ps]

        def PSK(i):
            return ("ps", i)

        ident = sb(es, "ident", [128, 128], F32)
        identb = sb(es, "identb", [128, 128], BF16)
        U = sb(es, "U", [128, 128], F32)
        Ub = sb(es, "Ub", [128, 128], BF16)
        onesf = sb(es, "onesf", [128, 128], F32)
        flag = sb(es, "flag", [128, 1], F32)
        P.op("pool", lambda e: e.memset(ident[:], 0.0), writes=["ident"])
        P.op("pool", lambda e: e.affine_select(out=ident[:], in_=ident[:], pattern=[[-1, 128]], compare_op=ALU.not_equal,
                                               fill=1.0, base=0, channel_multiplier=1), reads=["ident"], writes=["ident"])
        P.op("dve", lambda e: e.tensor_copy(out=identb[:], in_=ident[:]), reads=["ident"], writes=["identb"])
        P.op("pool", lambda e: e.memset(onesf[:], 1.0), writes=["onesf"])
        P.op("pool", lambda e: e.memset(U[:], 1.0), writes=["U"])
        P.op("pool", lambda e: e.affine_select(out=U[:], in_=U[:], pattern=[[1, 128]], compare_op=ALU.is_ge,
                                               fill=0.0, base=0, channel_multiplier=-1), reads=["U"], writes=["U"])
        P.op("dve", lambda e: e.tensor_copy(out=Ub[:], in_=U[:]), reads=["U"], writes=["Ub"])
        P.dma("sp", flag[:], flag_d, writes=["flag"])

        xsT = sb(es, "xsT", [128, 8, 32], BF16)
        smixT_a = sb(es, "smixT_a", [128, 8, 32], BF16)
        smixT_s = sb(es, "smixT_s", [128, 8, 32], BF16)
        sssq_a = sb(es, "sssq_a", [32, 1], F32)
        sssq_s = sb(es, "sssq_s", [32, 4], F32)
        U32 = sb(es, "U32", [32, 4, 8], F32)
        B32 = sb(es, "B32", [32, 4, 8], F32)
        seqm = sb(es, "seqm", [32, 4], F32)
        sel7 = sb(es, "sel7", [32, 4, 128], F32)
        P.op("pool", lambda e: e.memset(B32[:], 1.0), writes=["B32"])
        P.op("pool", lambda e: e.affine_select(out=B32[:], in_=B32[:], pattern=[[8, 4], [0, 8]], compare_op=ALU.is_ge, fill=0.0, base=7, channel_multiplier=-1),
             reads=["B32"], writes=["B32"])
        P.op("pool", lambda e: e.affine_select(out=B32[:], in_=B32[:], pattern=[[-8, 4], [0, 8]], compare_op=ALU.is_ge, fill=0.0, base=0, channel_multiplier=1),
             reads=["B32"], writes=["B32"])
        P.op("pool", lambda e: e.affine_select(out=U32[:], in_=B32[:], pattern=[[8, 4], [1, 8]], compare_op=ALU.is_ge, fill=0.0, base=0, channel_multiplier=-1),
             reads=["B32"], writes=["U32"])
        P.op("pool", lambda e: e.memset(seqm[:], 1.0), writes=["seqm"])
        P.op("pool", lambda e: e.affine_select(out=seqm[:], in_=seqm[:], pattern=[[8, 4]], compare_op=ALU.is_ge, fill=0.0, base=7, channel_multiplier=-1),
             reads=["seqm"], writes=["seqm"])
        P.op("pool", lambda e: e.affine_select(out=seqm[:], in_=seqm[:], pattern=[[-8, 4]], compare_op=ALU.is_ge, fill=0.0, base=0, channel_multiplier=1),
             reads=["seqm"], writes=["seqm"])
        P.op("pool", lambda e: e.memset(sel7[:], 1.0), writes=["sel7"])
        P.op("pool", lambda e: e.affine_select(out=sel7[:], in_=sel7[:], pattern=[[-8, 4], [0, 128]], compare_op=ALU.is_equal, fill=0.0, base=-7, channel_multiplier=1),
             reads=["sel7"], writes=["sel7"])
        U32f = U32[:].rearrange("p s t -> p (s t)")
        B32f = B32[:].rearrange("p s t -> p (s t)")

        xsc = ExitStack()
        xT_o = sb(xsc, "xT_o", [128, 8, T], BF16)
        xT_p = sb(xsc, "xT_p", [128, 8, T], BF16)
        msc = ExitStack()
        mixT_s = sb(msc, "mixT_s", [128, 8, T], BF16, side="right")
        ssq_s = sb(msc, "ssq_s", [128, NB, 4], F32, side="right")


        with ExitStack() as ph:
            xin = [sb(ph, f"xin{i}", [128, D], F32) for i in range(2)]
            n = 0
            for src, dst, dkey in ((xp, xT_p, "xT_p"), (xo, xT_o, "xT_o")):
                for blk in range(NB):
                    xi = xin[n % 2]; xk = ("xin", n % 2)
                    pa, pb = (0, 1) if n % 2 == 0 else (2, 3)
                    P.dma("sp", xi[:], src[blk * 128:(blk + 1) * 128, :], writes=[xk])
                    for kc in range(8):
                        pt = ps[pa] if kc < 4 else ps[pb]
                        kk = PSK(pa) if kc < 4 else PSK(pb)
                        c4 = kc % 4
                        P.op("pe", lambda e, pt=pt, c4=c4, xi=xi, kc=kc: e.transpose(
                            out=pt[:, c4 * 128:(c4 + 1) * 128], in_=xi[:, kc * 128:(kc + 1) * 128], identity=ident[:]),
                            reads=[xk, "ident"], writes=[kk], inc=(c4 == 3))
                    P.op("act", lambda e, dst=dst, blk=blk, pa=pa: e.activation(
                        out=dst[:, 0:4, blk * 128:(blk + 1) * 128], in_=ps[pa][:, :].rearrange("p (c t) -> p c t", c=4), func=AF.Copy),
                        reads=[PSK(pa)], writes=[(dkey, blk)])
                    P.op("dve", lambda e, dst=dst, blk=blk, pb=pb: e.tensor_copy(
                        out=dst[:, 4:8, blk * 128:(blk + 1) * 128], in_=ps[pb][:, :].rearrange("p (c t) -> p c t", c=4)),
                        reads=[PSK(pb)], writes=[(dkey, blk, 1)])
                    n += 1
            P.dma("sp", xin[0][0:32, :], xs_d, writes=[("xin", 0)])
            for kc in range(8):
                P.op("pe", lambda e, kc=kc: e.transpose(out=ps[4][:, kc * 32:(kc + 1) * 32], in_=xin[0][0:32, kc * 128:(kc + 1) * 128], identity=ident[0:32, 0:32]),
                     reads=[("xin", 0), "ident"], writes=[PSK(4)], inc=(kc == 7))
            P.op("act", lambda e: e.activation(out=xsT[:], in_=ps[4][:, 0:256].rearrange("p (c t) -> p c t", c=8), func=AF.Copy), reads=[PSK(4)], writes=["xsT"])
            P.flush()
            stop_at("p0")

        def xkeys(dkey, b0, b1):
            r = []
            for b in range(b0, b1):
                r += [(dkey, b), (dkey, b, 1)]
            return r

        with ExitStack() as ph:
            cw = sb(ph, "cw", [128, 4, 16], F32)
            cb = sb(ph, "cb", [128, 16], F32)
            diagw = sb(ph, "diagw", [128, 4, 4, 128], BF16)
            a_bc = sb(ph, "a_bc", [128, 16], F32)
            dtb_bc = sb(ph, "dtb_bc", [128, 16], F32)
            dsk_bc = sb(ph, "dsk_bc", [128, 16], F32)
            wdt = sb(ph, "wdt", [128, 8, 16], BF16)
            dt_tok = sb(ph, "dt_tok", [128, 32, 16], F32)
            cs_tok = sb(ph, "cs_tok", [128, 32, 16], F32)
            dtte = sb(ph, "dtte", [128, 32, 16], F32)
            ecs = sb(ph, "ecs", [128, 32, 16], F32)
            dec = sb(ph, "dec", [128, 32, 16], F32)
            csT = sb(ph, "csT", [128, 4, 128], F32)
            ncs_tok = sb(ph, "ncs_tok", [128, 32, 16], F32)
            sdt = sb(ph, "sdt", [32, 16], F32)
            sncs = sb(ph, "sncs", [32, 16], F32)
            secs = sb(ph, "secs", [32, 16], F32)
            sdtte = sb(ph, "sdtte", [32, 16], F32)
            scs_bc = sb(ph, "scs_bc", [32, 16, 32], F32)
            sdec = sb(ph, "sdec", [128, 4, 16], F32)
            pro = ExitStack()
            sla = sb(pro, "sla", [32, 16], F32)
            scs = sb(pro, "scs", [32, 16], F32)
            scsl = sb(pro, "scsl", [32, 16], F32)
            stmp = sb(pro, "stmp", [32, 16], F32)
            srhs = sb(pro, "srhs", [32, 16, 32], F32)
            la_tok = sb(pro, "la_tok", [128, 32, 16], F32)
            csl_bc = sb(pro, "csl_bc", [128, 32, 16], F32)
            tmpA = sb(pro, "tmpA", [128, 32, 16], F32)

            for tap in range(4):
                P.dma("sp", cw[:, tap, :], conv_w[tap].rearrange("(c p) -> p c", p=128), writes=["cw"], allow_slow_non_contiguous=True)
            P.dma("sp", cb[:], conv_b.rearrange("(c p) -> p c", p=128), writes=["cb"], allow_slow_non_contiguous=True)
            P.dma("sp", a_bc[:], a_log.partition_broadcast(128), writes=["a_bc"])
            P.dma("sp", dtb_bc[:], dt_bias.partition_broadcast(128), writes=["dtb_bc"])
            P.dma("sp", dsk_bc[:], d_skip.partition_broadcast(128), writes=["dsk_bc"])
            P.dma("pool", wdt[:], w_in_v[:, :, COL_DT:COL_DT + 16], writes=["wdt"])
            P.op("act", lambda e: e.activation(out=a_bc[:], in_=a_bc[:], func=AF.Exp), reads=["a_bc"], writes=["a_bc"])
            P.op("dve", lambda e: e.tensor_scalar(out=a_bc[:], in0=a_bc[:], scalar1=-1.0, scalar2=None, op0=ALU.mult),
                 reads=["a_bc"], writes=["a_bc"])
            for cidx in range(32):
                src, dkey = (xT_p, "xT_p") if cidx < 16 else (xT_o, "xT_o")
                blk = cidx % 16
                for kc in range(8):
                    P.op("pe", lambda e, src=src, blk=blk, kc=kc, cidx=cidx: e.matmul(
                        ps[0][:, cidx * 16:(cidx + 1) * 16], lhsT=src[:, kc, blk * 128:(blk + 1) * 128], rhs=wdt[:, kc, :],
                        start=(kc == 0), stop=(kc == 7), skip_group_check=True),
                        reads=xkeys(dkey, blk, blk + 1) + ["wdt"], writes=[PSK(0)], inc=(kc == 7 and cidx == 31))
            ps0v = ps[0][:, :].rearrange("p (c h) -> p c h", h=16)
            P.op("dve", lambda e: e.tensor_tensor(out=tmpA[:], in0=ps0v, in1=dtb_bc[:].unsqueeze(1).broadcast_to([128, 32, 16]), op=ALU.add),
                 reads=[PSK(0), "dtb_bc"], writes=["tmpA"])
            P.op("act", lambda e: e.activation(out=tmpA[:], in_=tmpA[:], func=AF.Exp), reads=["tmpA"], writes=["tmpA"])
            P.op("act", lambda e: e.activation(out=dt_tok[:], in_=tmpA[:], func=AF.Ln, bias=1.0), reads=["tmpA"], writes=["dt_tok"])
            P.op("dve", lambda e: e.tensor_tensor(out=la_tok[:], in0=dt_tok[:], in1=a_bc[:].unsqueeze(1).broadcast_to([128, 32, 16]), op=ALU.mult),
                 reads=["dt_tok", "a_bc"], writes=["la_tok"])
            la2 = la_tok[:].rearrange("p c h -> p (c h)")
            P.op("pe", lambda e: e.matmul(ps[1][:, :], lhsT=U[:], rhs=la2, start=True, stop=True), reads=["U", "la_tok"], writes=[PSK(1)])
            P.op("pe", lambda e: e.matmul(ps[2][:, :], lhsT=onesf[:], rhs=la2, start=True, stop=True), reads=["onesf", "la_tok"], writes=[PSK(2)])
            P.op("act", lambda e: e.activation(out=cs_tok[:].rearrange("p c h -> p (c h)"), in_=ps[1][:, :], func=AF.Copy),
                 reads=[PSK(1)], writes=["cs_tok"])
            P.op("dve", lambda e: e.tensor_copy(out=csl_bc[:].rearrange("p c h -> p (c h)"), in_=ps[2][:, :]), reads=[PSK(2)], writes=["csl_bc"])
            P.op("dve", lambda e: e.tensor_scalar(out=ncs_tok[:], in0=cs_tok[:], scalar1=-1.0, scalar2=None, op0=ALU.mult), reads=["cs_tok"], writes=["ncs_tok"])
            P.op("act", lambda e: e.activation(out=ecs[:], in_=cs_tok[:], func=AF.Exp), reads=["cs_tok"], writes=["ecs"])
            P.op("act", lambda e: e.activation(out=dec[:], in_=csl_bc[:], func=AF.Exp), reads=["csl_bc"], writes=["dec"])
            P.op("dve", lambda e: e.tensor_tensor(out=tmpA[:], in0=csl_bc[:], in1=cs_tok[:], op=ALU.subtract),
                 reads=["csl_bc", "cs_tok"], writes=["tmpA"])
            P.op("act", lambda e: e.activation(out=tmpA[:], in_=tmpA[:], func=AF.Exp), reads=["tmpA"], writes=["tmpA"])
            P.op("dve", lambda e: e.tensor_tensor(out=dtte[:], in0=tmpA[:], in1=dt_tok[:], op=ALU.mult), reads=["tmpA", "dt_tok"], writes=["dtte"])
            for q in range(4):
                P.op("pe", lambda e, q=q: e.transpose(out=ps[3][:, q * 128:(q + 1) * 128],
                                                     in_=cs_tok[:].rearrange("p c h -> p (c h)")[:, q * 128:(q + 1) * 128], identity=ident[:]),
                     reads=["cs_tok", "ident"], writes=[PSK(3)], inc=(q == 3))
            P.op("dve", lambda e: e.tensor_copy(out=csT[:].rearrange("p q l -> p (q l)"), in_=ps[3][:, :]), reads=[PSK(3)], writes=["csT"])
            P.dma("sp", scr_cs.rearrange("(q p) l -> p q l", p=128), csT[:], reads=["csT"], writes=["scr_cs"])

            for kc in range(8):
                P.op("pe", lambda e, kc=kc: e.matmul(ps[4][0:32, 0:16], lhsT=xsT[:, kc, :], rhs=wdt[:, kc, :], start=(kc == 0), stop=(kc == 7)),
                     reads=["xsT", "wdt"], writes=[PSK(4)], inc=(kc == 7))
            P.op("dve", lambda e: e.tensor_tensor(out=stmp[:], in0=ps[4][0:32, 0:16], in1=dtb_bc[0:32, :], op=ALU.add), reads=[PSK(4), "dtb_bc"], writes=["stmp"])
            P.op("act", lambda e: e.activation(out=stmp[:], in_=stmp[:], func=AF.Exp), reads=["stmp"], writes=["stmp"])
            P.op("act", lambda e: e.activation(out=sdt[:], in_=stmp[:], func=AF.Ln, bias=1.0), reads=["stmp"], writes=["sdt"])
            P.op("dve", lambda e: e.tensor_tensor(out=sla[:], in0=sdt[:], in1=a_bc[0:32, :], op=ALU.mult), reads=["sdt", "a_bc"], writes=["sla"])
            P.op("pe", lambda e: e.matmul(ps[5][0:32, 0:16], lhsT=U32f, rhs=sla[:], start=True, stop=True), reads=["U32", "sla"], writes=[PSK(5)])
            P.op("pe", lambda e: e.matmul(ps[6][0:32, 0:16], lhsT=B32f, rhs=sla[:], start=True, stop=True), reads=["B32", "sla"], writes=[PSK(6)])
            P.op("act", lambda e: e.activation(out=scs[:], in_=ps[5][0:32, 0:16], func=AF.Copy), reads=[PSK(5)], writes=["scs"])
            P.op("dve", lambda e: e.tensor_copy(out=scsl[:], in_=ps[6][0:32, 0:16]), reads=[PSK(6)], writes=["scsl"])
            P.op("dve", lambda e: e.tensor_scalar(out=sncs[:], in0=scs[:], scalar1=-1.0, scalar2=None, op0=ALU.mult), reads=["scs"], writes=["sncs"])
            P.op("act", lambda e: e.activation(out=secs[:], in_=scs[:], func=AF.Exp), reads=["scs"], writes=["secs"])
            P.op("dve", lambda e: e.tensor_tensor(out=stmp[:], in0=scsl[:], in1=scs[:], op=ALU.subtract), reads=["scsl", "scs"], writes=["stmp"])
            P.op("act", lambda e: e.activation(out=stmp[:], in_=stmp[:], func=AF.Exp), reads=["stmp"], writes=["stmp"])
            P.op("dve", lambda e: e.tensor_tensor(out=sdtte[:], in0=stmp[:], in1=sdt[:], op=ALU.mult), reads=["stmp", "sdt"], writes=["sdtte"])
            P.op("dve", lambda e: e.tensor_tensor(out=srhs[:], in0=sla[:].unsqueeze(2).broadcast_to([32, 16, 32]), in1=U32f.unsqueeze(1).broadcast_to([32, 16, 32]), op=ALU.mult),
                 reads=["sla", "U32"], writes=["srhs"])
            P.op("pe", lambda e: e.matmul(ps[7][0:32, :], lhsT=onesf[0:32, 0:32], rhs=srhs[:].rearrange("p h t -> p (h t)"), start=True, stop=True),
                 reads=["onesf", "srhs"], writes=[PSK(7)])
            P.op("act", lambda e: e.activation(out=scs_bc[:].rearrange("p h t -> p (h t)"), in_=ps[7][0:32, :], func=AF.Copy), reads=[PSK(7)], writes=["scs_bc"])
            for sq_ in range(4):
                P.op("pe", lambda e, sq_=sq_: e.matmul(ps[4][:, 64 + sq_ * 16:64 + (sq_ + 1) * 16], lhsT=sel7[:, sq_, :], rhs=scsl[:], start=True, stop=True),
                     reads=["sel7", "scsl"], writes=[PSK(4)], inc=(sq_ == 3))
            P.op("act", lambda e: e.activation(out=sdec[:].rearrange("p s h -> p (s h)"), in_=ps[4][:, 64:128], func=AF.Exp), reads=[PSK(4)], writes=["sdec"])

            if "dt_tok" in dbg:
                P.dma("sp", dbg_out["dt_tok"], dt_tok[:].rearrange("p c h -> p (c h)"), reads=["dt_tok"])
            if "cs_tok" in dbg:
                P.dma("sp", dbg_out["cs_tok"], cs_tok[:].rearrange("p c h -> p (c h)"), reads=["cs_tok"])

            P.flush()
            pro.close()
            stop_at("pro")
            Wg = sb(ph, "Wg", [128, 8, 4, 128], BF16)
            Wz = sb(ph, "Wz", [128, 8, 256], BF16)
            raw = sb(ph, "raw", [128, 4, 3 + T], BF16)
            xbc = sb(ph, "xbc", [128, 4, T], BF16)
            zs = sb(ph, "zs", [128, NB, 256], BF16)
            hst = sb(ph, "hst", [128, 256], F32)
            hb = sb(ph, "hb", [128, 256], BF16)
            Btok = [sb(ph, f"Btok{i}", [128, 128], BF16) for i in range(2)]
            xdt = [sb(ph, f"xdt{i}", [128, 256], BF16) for i in range(2)]
            xdte = [sb(ph, f"xdte{i}", [128, 256], BF16) for i in range(2)]
            xD = [sb(ph, f"xD{i}", [128, 256], BF16) for i in range(2)]
            csbc = [sb(ph, f"csbc{i}", [128, 4, 128], F32) for i in range(2)]
            dd = [sb(ph, "dd0", [128, 4, 128], F32)] * 2
            ee = [sb(ph, "ee0", [128, 4, 128], F32)] * 2
            CBm = [sb(ph, f"CBm{i}", [128, 128], BF16) for i in range(2)]
            MT = [sb(ph, f"MT{i}", [128, 4, 128], BF16) for i in range(2)]
            t1 = [sb(ph, f"t1{i}", [128, 256], F32) for i in range(2)]
            yg = [sb(ph, f"yg{i}", [128, 256], F32) for i in range(2)]
            ygb = [sb(ph, f"ygb{i}", [128, 256], BF16) for i in range(2)]
            junk = sb(ph, "junk", [128, 256], F32)
            cvo = sb(ph, "cvo", [128, 512], F32)
            hT = sb(ph, "hT", [128, 2, 128], F32)
            sraw = sb(ph, "sraw", [128, 4, 4, 11], BF16)
            sxbc = sb(ph, "sxbc", [128, 4, 32], BF16)
            scv = sb(ph, "scv", [12, 4, 128], F32)
            szs = sb(ph, "szs", [32, 256], F32)
            sCTm = sb(ph, "sCTm", [128, 4, 32], BF16)
            sxm = sb(ph, "sxm", [32, 256], BF16)

            def ssd_group(g):
                chan = [2 * g, 2 * g + 1, 8 + g, 12 + g]
                for i, c in enumerate(chan):
                    P.dma("pool", Wg[:, :, i, :], w_in_v[:, :, COL_XBC + c * 128:COL_XBC + (c + 1) * 128], writes=[("Wg", i)])
                P.dma("pool", Wz[:], w_in_v[:, :, COL_Z + 256 * g:COL_Z + 256 * (g + 1)], writes=["Wz"])
                for i, c in enumerate(chan):
                    for tap in range(4):
                        P.op("dve", lambda e, i=i, c=c, tap=tap: e.tensor_scalar(
                            out=diagw[:, i, tap, :], in0=ident[:], scalar1=cw[:, tap, c:c + 1], scalar2=None, op0=ALU.mult),
                            reads=["ident", "cw"], writes=[("diagw", i)])
                P.op("pool", lambda e: e.memset(raw[:, :, 0:3], 0.0), writes=["rawpre"])
                P.op("pool", lambda e: e.memset(hst[:], 0.0), writes=["hst"])
                P.op("pool", lambda e: e.memset(hb[:], 0.0), writes=["hb"])
                pr = [0]

                def ssd_pass(pas):
                    src, dkey = (xT_p, "xT_p") if pas == 0 else (xT_o, "xT_o")
                    nch = 3 if pas == 0 else 4
                    if pas == 1:
                        P.op("pool", lambda e: e.tensor_copy(out=raw[:, :, 0:3], in_=raw[:, :, T:T + 3]),
                             reads=[("raw", i, 3) for i in range(4)], writes=["rawpre"])
                        for kc in range(8):
                            P.op("pe", lambda e, kc=kc: e.matmul(ps[4][0:3, :], lhsT=xT_o[:, kc, T - 3:T],
                                                                 rhs=Wg[:, kc, :, :].rearrange("p i n -> p (i n)"),
                                                                 start=(kc == 0), stop=(kc == 7)),
                                 reads=xkeys("xT_o", 15, 16) + [("Wg", i) for i in range(4)], writes=[PSK(4)], inc=(kc == 7))
                        P.op("act", lambda e: e.activation(out=cvo[0:3, :], in_=ps[4][0:3, :], func=AF.Copy), reads=[PSK(4)], writes=["cvo"])
                        for i, c in enumerate(chan):
                            P.dma("sp", conv_o[:, c * 128:(c + 1) * 128], cvo[0:3, i * 128:(i + 1) * 128], reads=["cvo"])
                    for i in range(4):
                        for j in range(4):
                            if pas == 0 and i == 3 and j != 3:
                                continue
                            pp = pr[0] % 2; pr[0] += 1
                            for kc in range(8):
                                P.op("pe", lambda e, pp=pp, kc=kc, i=i, j=j, src=src: e.matmul(
                                    ps[pp][:, :], lhsT=Wg[:, kc, i, :], rhs=src[:, kc, j * 512:(j + 1) * 512], start=(kc == 0), stop=(kc == 7)),
                                    reads=[("Wg", i)] + xkeys(dkey, 4 * j, 4 * j + 4), writes=[PSK(pp)], inc=(kc == 7))
                            P.op("act", lambda e, pp=pp, i=i, j=j: e.activation(out=raw[:, i, 3 + j * 512:3 + (j + 1) * 512], in_=ps[pp][:, :], func=AF.Copy),
                                 reads=[PSK(pp)], writes=[("raw", i, j)])
                    stop_at(f"g{g}p{pas}proj")
                    for i in range(nch):
                        for j in range(4):
                            pp = pr[0] % 2; pr[0] += 1
                            rk = [("raw", i, j)] + ([("raw", i, j - 1)] if j else ["rawpre"])
                            for tap in range(4):
                                P.op("pe", lambda e, pp=pp, tap=tap, i=i, j=j: e.matmul(
                                    ps[pp][:, :], lhsT=diagw[:, i, tap, :], rhs=raw[:, i, j * 512 + tap:j * 512 + tap + 512],
                                    start=(tap == 0), stop=(tap == 3)),
                                    reads=rk + [("diagw", i)], writes=[PSK(pp)], inc=(tap == 3))
                            P.op("act", lambda e, pp=pp, i=i, j=j: e.activation(out=xbc[:, i, j * 512:(j + 1) * 512], in_=ps[pp][:, :], func=AF.Silu,
                                                                               bias=cb[:, chan[i]:chan[i] + 1]),
                                 reads=[PSK(pp), "cb"], writes=[("xbc", i, j)])
                    stop_at(f"g{g}p{pas}conv")
                    if pas == 1:
                        for c in range(NB):
                            zb = 3 if c % 2 == 0 else 5
                            for kc in range(8):
                                P.op("pe", lambda e, c=c, kc=kc, zb=zb: e.matmul(ps[zb][:, 0:256], lhsT=xT_o[:, kc, c * 128:(c + 1) * 128], rhs=Wz[:, kc, :],
                                                                                start=(kc == 0), stop=(kc == 7)),
                                     reads=xkeys("xT_o", c, c + 1) + ["Wz"], writes=[PSK(zb)], inc=(kc == 7))
                            P.op("act", lambda e, c=c, zb=zb: e.activation(out=zs[:, c, :], in_=ps[zb][:, 0:256], func=AF.Silu), reads=[PSK(zb)], writes=[("zs", c)])
                        P.op("dve", lambda e: e.tensor_scalar(out=hst[:], in0=hst[:], scalar1=flag[:, 0:1], scalar2=None, op0=ALU.mult),
                             reads=["hst", "flag"], writes=["hst"])
                        P.op("act", lambda e: e.activation(out=hb[:], in_=hst[:], func=AF.Copy), reads=["hst"], writes=["hb"])
                    stop_at(f"g{g}p{pas}z")
                    def chunk_A(c):
                        cidx = pas * 16 + c
                        b = c % 2
                        j = c // 4
                        cs_ = slice(c * 128, (c + 1) * 128)
                        hs = slice(4 * g, 4 * g + 4)
                        for i in range(3):
                            P.op("pe", lambda e, i=i, cs_=cs_: e.transpose(out=psb[2][:, i * 128:(i + 1) * 128], in_=xbc[:, i, cs_], identity=identb[:]),
                                 reads=[("xbc", i, j), "identb"], writes=[PSK(2)], inc=(i == 2))
                        xtokv = psb[2][:, 0:256].rearrange("p (h d) -> p h d", h=4)
                        P.op("act", lambda e, b=b: e.activation(out=Btok[b][:], in_=psb[2][:, 256:384], func=AF.Copy), reads=[PSK(2)], writes=[("Btok", b)])
                        P.op("dve", lambda e, b=b, cidx=cidx: e.tensor_tensor(
                            out=xdte[b][:].rearrange("p (h d) -> p h d", h=4), in0=xtokv,
                            in1=dtte[:, cidx, hs].unsqueeze(2).broadcast_to([128, 4, 64]), op=ALU.mult),
                            reads=[PSK(2), "dtte"], writes=[("xdte", b)])
                        stop_at("xA0")
                        if pas == 1:
                            P.op("dve", lambda e, b=b, cidx=cidx: e.tensor_tensor(
                                out=xdt[b][:].rearrange("p (h d) -> p h d", h=4), in0=xtokv,
                                in1=dt_tok[:, cidx, hs].unsqueeze(2).broadcast_to([128, 4, 64]), op=ALU.mult),
                                reads=[PSK(2), "dt_tok"], writes=[("xdt", b)])
                            stop_at("xA1")
                            P.op("dve", lambda e, b=b: e.tensor_tensor(
                                out=xD[b][:].rearrange("p (h d) -> p h d", h=4), in0=xtokv,
                                in1=dsk_bc[:, hs].unsqueeze(2).broadcast_to([128, 4, 64]), op=ALU.mult),
                                reads=[PSK(2), "dsk_bc"], writes=[("xD", b)])
                            stop_at(f"xA")
                            P.dma("sp", csbc[b][:], scr_cs[cidx * 16 + 4 * g:cidx * 16 + 4 * g + 4, :].partition_broadcast(128),
                                  reads=["scr_cs"], writes=[("csbc", b)])
                            for hh in range(4):
                                P.op("pool", lambda e, b=b, hh=hh, cidx=cidx: e.tensor_scalar(
                                    out=dd[b][:, hh, :], in0=csbc[b][:, hh, :], scalar1=1.0, scalar2=ncs_tok[:, cidx, 4 * g + hh:4 * g + hh + 1],
                                    op0=ALU.mult, op1=ALU.add),
                                    reads=[("csbc", b), "ncs_tok"], writes=[("dd", 0, hh)])
                            P.op("pool", lambda e, b=b: e.tensor_scalar(out=dd[b][:], in0=dd[b][:], scalar1=0.0, scalar2=-1.0e30, op0=ALU.min, op1=ALU.max),
                                 reads=[("dd", 0, hh) for hh in range(4)], writes=[("dd", 0, hh) for hh in range(4)])
                            P.op("act", lambda e, b=b: e.activation(out=ee[b][:], in_=dd[b][:], func=AF.Exp),
                                 reads=[("dd", 0, hh) for hh in range(4)], writes=[("ee", 0)])
                            stop_at(f"xB")
                            P.op("pe", lambda e, cs_=cs_: e.matmul(ps[5][:, 0:128], lhsT=xbc[:, 2, cs_], rhs=xbc[:, 3, cs_], start=True, stop=True),
                                 reads=[("xbc", 2, j), ("xbc", 3, j)], writes=[PSK(5)])
                            P.op("dve", lambda e, b=b: e.tensor_tensor(out=CBm[b][:], in0=ps[5][:, 0:128], in1=U[:], op=ALU.mult),
                                 reads=[PSK(5), "U"], writes=[("CBm", b)])
                            stop_at(f"xC")
                            P.op("dve", lambda e, b=b: e.scalar_tensor_tensor(out=MT[b][:], in0=ee[b][:], scalar=1.0,
                                                                            in1=CBm[b][:].unsqueeze(1).broadcast_to([128, 4, 128]), op0=ALU.min, op1=ALU.mult),
                                 reads=[("ee", 0), ("CBm", b)], writes=[("MT", b)])
                            stop_at(f"xD")

                    def chunk_B(c):
                        cidx = pas * 16 + c
                        b = c % 2
                        j = c // 4
                        cs_ = slice(c * 128, (c + 1) * 128)
                        hs = slice(4 * g, 4 * g + 4)
                        if pas == 1:
                            P.op("pe", lambda e, b=b: e.matmul(ps[6][:, 0:256], lhsT=identb[:], rhs=xD[b][:], start=True, stop=False, skip_group_check=True),
                                 reads=["identb", ("xD", b)], writes=[PSK(6)], inc=False)
                            for hh in range(4):
                                P.op("pe", lambda e, b=b, hh=hh: e.matmul(ps[6][:, hh * 64:(hh + 1) * 64], lhsT=MT[b][:, hh, :], rhs=xdt[b][:, hh * 64:(hh + 1) * 64],
                                                                          start=False, stop=(hh == 3), skip_group_check=True),
                                     reads=[("MT", b), ("xdt", b)], writes=[PSK(6)], inc=False)
                            P.op("pe", lambda e, cs_=cs_: e.matmul(ps[6][:, 256:512], lhsT=xbc[:, 3, cs_], rhs=hb[:], start=False, stop=True, skip_group_check=True),
                                 reads=[("xbc", 3, j), "hb"], writes=[PSK(6)])
                        stop_at(f"xE")
                        P.op("pe", lambda e, b=b: e.matmul(ps[4][:, 0:256], lhsT=Btok[b][:], rhs=xdte[b][:], start=True, stop=True),
                             reads=[("Btok", b), ("xdte", b)], writes=[PSK(4)])
                        P.op("pool", lambda e, cidx=cidx: e.tensor_tensor(
                            out=hst[:].rearrange("p (h d) -> p h d", h=4), in0=hst[:].rearrange("p (h d) -> p h d", h=4),
                            in1=dec[:, cidx, hs].unsqueeze(2).broadcast_to([128, 4, 64]), op=ALU.mult),
                            reads=["hst", "dec"], writes=["hst"])
                        P.op("dve", lambda e: e.tensor_tensor(out=hst[:], in0=hst[:], in1=ps[4][:, 0:256], op=ALU.add), reads=["hst", PSK(4)], writes=["hst"])
                        P.op("act", lambda e: e.activation(out=hb[:], in_=hst[:], func=AF.Copy), reads=["hst"], writes=["hb"])
                        if pas == 1:
                            P.op("dve", lambda e, b=b, cidx=cidx: e.tensor_tensor(
                                out=t1[b][:].rearrange("p (h d) -> p h d", h=4), in0=ps[6][:, 256:512].rearrange("p (h d) -> p h d", h=4),
                                in1=ecs[:, cidx, hs].unsqueeze(2).broadcast_to([128, 4, 64]), op=ALU.mult),
                                reads=[PSK(6), "ecs"], writes=[("t1", b)])
                            P.op("dve", lambda e, b=b: e.tensor_tensor(out=t1[b][:], in0=t1[b][:], in1=ps[6][:, 0:256], op=ALU.add),
                                 reads=[("t1", b), PSK(6)], writes=[("t1", b)])
                            stop_at("xF")
                            P.op("pool", lambda e, b=b, c=c: e.tensor_tensor(out=ygb[b][:], in0=t1[b][:], in1=zs[:, c, :], op=ALU.mult),
                                 reads=[("t1", b), ("zs", c)], writes=[("ygb", b)])
                            P.op("act", lambda e, b=b, c=c: e.activation(out=junk[:], in_=ygb[b][:], func=AF.Square, accum_out=ssq_s[:, c, g:g + 1]),
                                 reads=[("ygb", b)], writes=["junk", ("ssq_s", c, g)])
                            for i in range(2):
                                P.op("pe", lambda e, b=b, i=i: e.transpose(out=psb[7][:, i * 128:(i + 1) * 128], in_=ygb[b][:, i * 128:(i + 1) * 128], identity=identb[:]),
                                     reads=[("ygb", b), "identb"], writes=[PSK(7)], inc=(i == 1))
                            P.op("act", lambda e, c=c: e.activation(out=mixT_s[:, 2 * g:2 * g + 2, c * 128:(c + 1) * 128],
                                                                    in_=psb[7][:, 0:256].rearrange("p (i t) -> p i t", i=2), func=AF.Copy),
                                 reads=[PSK(7)], writes=[("mixT_s", g, c)])
                        stop_at(f"g{g}p{pas}c{c}")

                    chunk_A(0)
                    for c in range(NB):
                        if c + 1 < NB:
                            chunk_A(c + 1)
                        chunk_B(c)

                for pas in range(2):
                    ssd_pass(pas)
                stop_at(f"g{g}loop")
                for i in range(2):
                    P.op("pe", lambda e, i=i: e.transpose(out=ps[3][:, i * 128:(i + 1) * 128], in_=hst[:, i * 128:(i + 1) * 128], identity=ident[:]),
                         reads=["hst", "ident"], writes=[PSK(3)], inc=(i == 1))
                P.op("dve", lambda e: e.tensor_copy(out=hT[:].rearrange("p i n -> p (i n)"), in_=ps[3][:, 0:256]), reads=[PSK(3)], writes=["hT"])
                P.dma("sp", ssm_o[256 * g:256 * (g + 1), :].rearrange("(i p) n -> p i n", p=128), hT[:], reads=["hT"])

                hs = slice(4 * g, 4 * g + 4)
                for i, c in enumerate(chan):
                    P.dma("sp", scv[:, i, :], sconv[:, c * 128:(c + 1) * 128], writes=[("scv", i)])
                for i in range(4):
                    P.op("pe", lambda e, i=i: e.transpose(out=ps[0][:, i * 12:(i + 1) * 12], in_=scv[0:12, i, :], identity=ident[0:12, 0:12]),
                         reads=[("scv", i), "ident"], writes=[PSK(0)], inc=(i == 3))
                P.op("act", lambda e: e.activation(out=sraw[:, :, :, 0:3], in_=ps[0][:, 0:48].rearrange("p (i s w) -> p i s w", i=4, s=4), func=AF.Copy),
                     reads=[PSK(0)], writes=["srawpre"])
                for i in range(4):
                    for kc in range(8):
                        P.op("pe", lambda e, i=i, kc=kc: e.matmul(ps[1][:, i * 32:(i + 1) * 32], lhsT=Wg[:, kc, i, :], rhs=xsT[:, kc, :], start=(kc == 0), stop=(kc == 7), skip_group_check=True),
                             reads=[("Wg", i), "xsT"], writes=[PSK(1)], inc=(kc == 7 and i == 3))
                P.op("act", lambda e: e.activation(out=sraw[:, :, :, 3:11], in_=ps[1][:, 0:128].rearrange("p (i s t) -> p i s t", i=4, s=4), func=AF.Copy),
                     reads=[PSK(1)], writes=["sraw"])
                for i in range(4):
                    for tap in range(4):
                        P.op("pe", lambda e, i=i, tap=tap: e.matmul(ps[0][:, 64 + i * 32:64 + (i + 1) * 32], lhsT=diagw[:, i, tap, :], rhs=sraw[:, i, :, tap:tap + 8],
                                                                    start=(tap == 0), stop=(tap == 3), skip_group_check=True),
                             reads=["sraw", "srawpre", ("diagw", i)], writes=[PSK(0)], inc=(tap == 3 and i == 3))
                for i in range(4):
                    P.op("act", lambda e, i=i: e.activation(out=sxbc[:, i, :], in_=ps[0][:, 64 + i * 32:64 + (i + 1) * 32], func=AF.Silu, bias=cb[:, chan[i]:chan[i] + 1]),
                         reads=[PSK(0), "cb"], writes=[("sxbc", i)])
                for kc in range(8):
                    P.op("pe", lambda e, kc=kc: e.matmul(ps[4][0:32, :], lhsT=xsT[:, kc, :], rhs=Wg[:, kc, :, :].rearrange("p i n -> p (i n)"), start=(kc == 0), stop=(kc == 7)),
                         reads=["xsT"] + [("Wg", i) for i in range(4)], writes=[PSK(4)], inc=(kc == 7))
                P.op("act", lambda e: e.activation(out=cvo[0:32, :], in_=ps[4][0:32, :], func=AF.Copy), reads=[PSK(4)], writes=["cvo"])
                for sq_ in range(4):
                    for i, c in enumerate(chan):
                        P.dma("sp", conv_s[sq_ * 3:(sq_ + 1) * 3, c * 128:(c + 1) * 128], cvo[sq_ * 8 + 5:sq_ * 8 + 8, i * 128:(i + 1) * 128], reads=["cvo"])
                for kc in range(8):
                    P.op("pe", lambda e, kc=kc: e.matmul(ps[3][0:32, 0:256], lhsT=xsT[:, kc, :], rhs=Wz[:, kc, :], start=(kc == 0), stop=(kc == 7)),
                         reads=["xsT", "Wz"], writes=[PSK(3)], inc=(kc == 7))
                P.op("act", lambda e: e.activation(out=szs[:], in_=ps[3][0:32, 0:256], func=AF.Silu), reads=[PSK(3)], writes=["szs"])
                for i in range(3):
                    P.op("pe", lambda e, i=i: e.transpose(out=psb[2][0:32, i * 128:(i + 1) * 128], in_=sxbc[:, i, :], identity=identb[:]),
                         reads=[("sxbc", i), "identb"], writes=[PSK(2)], inc=(i == 2))
                sxt = psb[2][0:32, 0:256].rearrange("p (h d) -> p h d", h=4)
                P.op("act", lambda e: e.activation(out=Btok[0][0:32, :], in_=psb[2][0:32, 256:384], func=AF.Copy), reads=[PSK(2)], writes=[("Btok", 0)])
                P.op("dve", lambda e: e.tensor_tensor(out=xdte[0][0:32, :].rearrange("p (h d) -> p h d", h=4), in0=sxt, in1=sdtte[:, hs].unsqueeze(2).broadcast_to([32, 4, 64]), op=ALU.mult),
                     reads=[PSK(2), "sdtte"], writes=[("xdte", 0)])
                P.op("dve", lambda e: e.tensor_tensor(out=xdt[0][0:32, :].rearrange("p (h d) -> p h d", h=4), in0=sxt, in1=sdt[:, hs].unsqueeze(2).broadcast_to([32, 4, 64]), op=ALU.mult),
                     reads=[PSK(2), "sdt"], writes=[("xdt", 0)])
                P.op("dve", lambda e: e.tensor_tensor(out=xD[0][0:32, :].rearrange("p (h d) -> p h d", h=4), in0=sxt, in1=dsk_bc[0:32, hs].unsqueeze(2).broadcast_to([32, 4, 64]), op=ALU.mult),
                     reads=[PSK(2), "dsk_bc"], writes=[("xD", 0)])
                for hh in range(4):
                    P.op("pool", lambda e, hh=hh: e.tensor_scalar(out=dd[0][0:32, hh, 0:32], in0=scs_bc[:, 4 * g + hh, :], scalar1=1.0, scalar2=sncs[:, 4 * g + hh:4 * g + hh + 1],
                                                                  op0=ALU.mult, op1=ALU.add),
                         reads=["scs_bc", "sncs"], writes=[("dd", 0, hh)])
                P.op("pool", lambda e: e.tensor_scalar(out=dd[0][0:32, :, 0:32], in0=dd[0][0:32, :, 0:32], scalar1=0.0, scalar2=-1.0e30, op0=ALU.min, op1=ALU.max),
                     reads=[("dd", 0, hh) for hh in range(4)], writes=[("dd", 0, hh) for hh in range(4)])
                P.op("act", lambda e: e.activation(out=ee[0][0:32, :, 0:32], in_=dd[0][0:32, :, 0:32], func=AF.Exp), reads=[("dd", 0, hh) for hh in range(4)], writes=[("ee", 0)])
                P.op("pe", lambda e: e.matmul(ps[5][0:32, 0:32], lhsT=sxbc[:, 2, :], rhs=sxbc[:, 3, :], start=True, stop=True), reads=[("sxbc", 2), ("sxbc", 3)], writes=[PSK(5)])
                P.op("dve", lambda e: e.tensor_tensor(out=CBm[0][0:32, 0:32], in0=ps[5][0:32, 0:32], in1=U32f, op=ALU.mult), reads=[PSK(5), "U32"], writes=[("CBm", 0)])
                P.op("dve", lambda e: e.scalar_tensor_tensor(out=MT[0][0:32, :, 0:32], in0=ee[0][0:32, :, 0:32], scalar=1.0,
                                                              in1=CBm[0][0:32, 0:32].unsqueeze(1).broadcast_to([32, 4, 32]), op0=ALU.min, op1=ALU.mult),
                     reads=[("ee", 0), ("CBm", 0)], writes=[("MT", 0)])
                P.op("pool", lambda e: e.memset(sCTm[:], 0.0), writes=["sCTm"])
                for sq_ in range(4):
                    P.op("pool", lambda e, sq_=sq_: e.tensor_copy(out=sCTm[:, sq_, sq_ * 8:(sq_ + 1) * 8], in_=sxbc[:, 3, sq_ * 8:(sq_ + 1) * 8]), reads=[("sxbc", 3), "sCTm"], writes=["sCTm"])
                P.op("pe", lambda e: e.matmul(ps[6][0:32, 0:256], lhsT=identb[0:32, 0:32], rhs=xD[0][0:32, :], start=True, stop=False, skip_group_check=True),
                     reads=["identb", ("xD", 0)], writes=[PSK(6)], inc=False)
                for hh in range(4):
                    P.op("pe", lambda e, hh=hh: e.matmul(ps[6][0:32, hh * 64:(hh + 1) * 64], lhsT=MT[0][0:32, hh, 0:32], rhs=xdt[0][0:32, hh * 64:(hh + 1) * 64],
                                                         start=False, stop=(hh == 3), skip_group_check=True),
                         reads=[("MT", 0), ("xdt", 0)], writes=[PSK(6)], inc=False)
                for sq_ in range(4):
                    P.dma("sp", hT[:], sssm[sq_, 256 * g:256 * (g + 1), :].rearrange("(i p) n -> p i n", p=128), writes=["hT"])
                    for i in range(2):
                        P.op("pe", lambda e, i=i: e.transpose(out=ps[3][:, i * 128:(i + 1) * 128], in_=hT[:, i, :], identity=ident[:]),
                             reads=["hT", "ident"], writes=[PSK(3)], inc=(i == 1))
                    P.op("dve", lambda e: e.tensor_copy(out=hst[:], in_=ps[3][:, 0:256]), reads=[PSK(3)], writes=["hst"])
                    P.op("act", lambda e: e.activation(out=hb[:], in_=hst[:], func=AF.Copy), reads=["hst"], writes=["hb"])
                    P.op("pe", lambda e, sq_=sq_: e.matmul(ps[6][0:32, 256:512], lhsT=sCTm[:, sq_, :], rhs=hb[:], start=False, stop=(sq_ == 3), skip_group_check=True),
                         reads=["sCTm", "hb"], writes=[PSK(6)], inc=True)
                    P.op("dve", lambda e, sq_=sq_: e.tensor_scalar(out=sxm[:], in0=xdte[0][0:32, :], scalar1=seqm[:, sq_:sq_ + 1], scalar2=None, op0=ALU.mult),
                         reads=[("xdte", 0), "seqm"], writes=["sxm"])
                    P.op("pe", lambda e: e.matmul(ps[4][:, 0:256], lhsT=Btok[0][0:32, :], rhs=sxm[:], start=True, stop=True), reads=[("Btok", 0), "sxm"], writes=[PSK(4)])
                    P.op("pool", lambda e, sq_=sq_: e.tensor_tensor(out=hst[:].rearrange("p (h d) -> p h d", h=4), in0=hst[:].rearrange("p (h d) -> p h d", h=4),
                                                                    in1=sdec[:, sq_, hs].unsqueeze(2).broadcast_to([128, 4, 64]), op=ALU.mult),
                         reads=["hst", "sdec"], writes=["hst"])
                    P.op("dve", lambda e: e.tensor_tensor(out=hst[:], in0=hst[:], in1=ps[4][:, 0:256], op=ALU.add), reads=["hst", PSK(4)], writes=["hst"])
                    for i in range(2):
                        P.op("pe", lambda e, i=i: e.transpose(out=ps[3][:, i * 128:(i + 1) * 128], in_=hst[:, i * 128:(i + 1) * 128], identity=ident[:]),
                             reads=["hst", "ident"], writes=[PSK(3)], inc=(i == 1))
                    P.op("dve", lambda e: e.tensor_copy(out=hT[:].rearrange("p i n -> p (i n)"), in_=ps[3][:, 0:256]), reads=[PSK(3)], writes=["hT"])
                    P.dma("sp", ssm_s[sq_, 256 * g:256 * (g + 1), :].rearrange("(i p) n -> p i n", p=128), hT[:], reads=["hT"])
                P.op("dve", lambda e: e.tensor_tensor(out=t1[0][0:32, :].rearrange("p (h d) -> p h d", h=4), in0=ps[6][0:32, 256:512].rearrange("p (h d) -> p h d", h=4),
                                                      in1=secs[:, hs].unsqueeze(2).broadcast_to([32, 4, 64]), op=ALU.mult),
                     reads=[PSK(6), "secs"], writes=[("t1", 0)])
                P.op("dve", lambda e: e.tensor_tensor(out=t1[0][0:32, :], in0=t1[0][0:32, :], in1=ps[6][0:32, 0:256], op=ALU.add), reads=[("t1", 0), PSK(6)], writes=[("t1", 0)])
                P.op("pool", lambda e: e.tensor_tensor(out=yg[0][0:32, :], in0=t1[0][0:32, :], in1=szs[:], op=ALU.mult), reads=[("t1", 0), "szs"], writes=[("yg", 0)])
                P.op("act", lambda e: e.activation(out=junk[0:32, :], in_=yg[0][0:32, :], func=AF.Square, accum_out=sssq_s[:, g:g + 1]), reads=[("yg", 0)], writes=["junk", ("sssq_s", g)])
                P.op("pool", lambda e: e.tensor_copy(out=ygb[0][0:32, :], in_=yg[0][0:32, :]), reads=[("yg", 0)], writes=[("ygb", 0)])
                for i in range(2):
                    P.op("pe", lambda e, i=i: e.transpose(out=psb[7][:, i * 32:(i + 1) * 32], in_=ygb[0][0:32, i * 128:(i + 1) * 128], identity=identb[0:32, 0:32]),
                         reads=[("ygb", 0), "identb"], writes=[PSK(7)], inc=(i == 1))
                P.op("act", lambda e: e.activation(out=smixT_s[:, 2 * g:2 * g + 2, :], in_=psb[7][:, 0:64].rearrange("p (i t) -> p i t", i=2), func=AF.Copy),
                     reads=[PSK(7)], writes=[("smixT_s", g)])
                stop_at(f"s{g}")

            for g in range(4):
                ssd_group(g)
            if "mixT_s" in dbg:
                P.dma("pool", dbg_out["mixT_s"].rearrange("p (c t) -> p c t", c=8), mixT_s[:], reads=[("mixT_s", g, c) for g in range(4) for c in range(NB)])
            if "smixT_s" in dbg:
                P.dma("pool", dbg_out["smixT_s"].rearrange("p (c t) -> p c t", c=8), smixT_s[:], reads=[("smixT_s", g) for g in range(4)])
            if "ssq_s" in dbg:
                P.dma("sp", dbg_out["ssq_s"], ssq_s[:].rearrange("p c g -> p (c g)"), reads=[("ssq_s", c, g) for g in range(4) for c in range(NB)])
            P.flush()

        mixT_a = sb(msc, "mixT_a", [128, 8, T], BF16, side="right")
        ssq_a = sb(msc, "ssq_a", [128, NB], F32, side="right")
        with ExitStack() as ph:
            Wq = sb(ph, "Wq", [128, 8, 128], BF16)
            Wk = sb(ph, "Wk", [128, 8, 128], BF16)
            Wv = sb(ph, "Wv", [128, 8, 128], BF16)
            QT = sb(ph, "QT", [128, T], BF16)
            KT = sb(ph, "KT", [128, 2 * T], BF16)
            VT = sb(ph, "VT", [128, 2 * T], BF16)
            kv32 = [sb(ph, f"kv32{i}", [128, 512], F32) for i in range(2)]
            kvo = [sb(ph, f"kvo{i}", [128, 4, 128], F32) for i in range(2)]
            Vaug = sb(ph, "Vaug", [128, 32, 192], BF16)
            Dm = sb(ph, "Dm", [128, 128], F32)
            Dmx = sb(ph, "Dmx", [128, 128], F32)
            Dp = sb(ph, "Dp", [128, 128], F32)
            vd = sb(ph, "vd", [128, 128], F32)
            vp = sb(ph, "vp", [128, 128], F32)
            mtmp = sb(ph, "mtmp", [128, 256], F32)
            masks = [sb(ph, f"masks{i}", [128, 3, 256], BF16) for i in range(2)]
            Eb = [sb(ph, f"Eb{i}", [128, 512], BF16) for i in range(2)]
            Pm = [sb(ph, f"Pm{i}", [128, 512], BF16) for i in range(2)]
            acc = sb(ph, "acc", [128, T], F32)
            rden = sb(ph, "rden", [128, 512], F32)
            sq = sb(ph, "sq", [128, 512], BF16)
            sQT = sb(ph, "sQT", [128, 32], BF16)
            sKT = sb(ph, "sKT", [128, 32], BF16)
            sknv = sb(ph, "sknv", [32, 2, 128], F32)
            sVnew = sb(ph, "sVnew", [8, 4, 192], BF16)
            ktile2 = [VT[:, o_:o_ + 13 * 128].rearrange("p (k d) -> p k d", d=128) for o_ in (0, 2048)]
            KTc2 = [KT[:, o_:o_ + 13 * 128].rearrange("p (k d) -> p k d", d=128) for o_ in (0, 2048)]
            sVaug2 = [Vaug[:, 0:13, :], Vaug[:, 16:29, :]]
            Ds = sb(ph, "Ds", [128, 24], F32)
            Vs = sb(ph, "Vs", [128, 24], F32)
            stmpm = sb(ph, "stmpm", [128, 24], F32)
            smask = sb(ph, "smask", [128, 2, 24], BF16)
            Dn = sb(ph, "Dn", [8, 8], F32)
            Dnx = sb(ph, "Dnx", [8, 8], F32)
            Wn = sb(ph, "Wn", [8, 8], F32)
            Z2 = sb(ph, "Z2", [8, 8], F32)
            tnn = sb(ph, "tnn", [8, 8], F32)
            smaskn = sb(ph, "smaskn", [8, 2, 8], BF16)
            sE2 = [sb(ph, f"sE{i}", [128, 48], BF16) for i in range(2)]
            sPm2 = [sb(ph, f"sPm{i}", [128, 48], BF16) for i in range(2)]
            sEn2 = [sb(ph, f"sEn{i}", [8, 16], BF16) for i in range(2)]
            sPn2 = [sb(ph, f"sPn{i}", [8, 16], BF16) for i in range(2)]
            srd = sb(ph, "srd", [128, 8], F32)
            sqs = sb(ph, "sqs", [128, 32], BF16)
            onesb = sb(ph, "onesb", [128, 1], BF16)
            sstmp = sb(ph, "sstmp", [128, NB], F32)

            P.op("pool", lambda e: e.memset(onesb[:], 1.0), writes=["onesb"])
            P.op("pool", lambda e: e.memset(ssq_a[:], 0.0), writes=["ssq_a"])
            P.op("pool", lambda e: e.memset(Vaug[:, 0:16, 64:128], 1.0), writes=["Vones"])
            P.op("pool", lambda e: e.memset(Vaug[:, 16:32, 64:128], 1.0), writes=["Vones"])
            P.op("dve", lambda e: e.tensor_scalar(out=Vaug[:, 16:32, 64:128], in0=Vaug[:, 16:32, 64:128], scalar1=flag[:, 0:1], scalar2=None, op0=ALU.mult),
                 reads=["flag"], writes=["Vones"])
            P.op("pool", lambda e: e.iota(Dm[:], pattern=[[1, 128]], base=0, channel_multiplier=-1, allow_small_or_imprecise_dtypes=True), writes=["Dm"])
            P.op("dve", lambda e: e.tensor_scalar(out=Dmx[:], in0=Dm[:], scalar1=0.0, scalar2=None, op0=ALU.max), reads=["Dm"], writes=["Dmx"])
            P.op("dve", lambda e: e.tensor_scalar(out=Dp[:], in0=Dm[:], scalar1=128.0, scalar2=None, op0=ALU.add), reads=["Dm"], writes=["Dp"])
            P.op("dve", lambda e: e.tensor_scalar(out=vd[:], in0=Dm[:], scalar1=0.0, scalar2=None, op0=ALU.is_ge), reads=["Dm"], writes=["vd"])
            P.op("dve", lambda e: e.tensor_scalar(out=vp[:], in0=Dm[:], scalar1=0.0, scalar2=None, op0=ALU.is_le), reads=["Dm"], writes=["vp"])

            P.op("pool", lambda e: e.memset(sssq_a[:], 0.0), writes=["sssq_a"])
            P.op("pool", lambda e: e.memset(sVnew[:, :, 64:128], 1.0), writes=["sVnones"])
            P.op("pool", lambda e: e.iota(Ds[:, 0:8], pattern=[[1, 8]], base=128, channel_multiplier=-1, allow_small_or_imprecise_dtypes=True), writes=["Ds"])
            P.op("pool", lambda e: e.iota(Ds[:, 8:16].rearrange("p (c q) -> p c q", c=4), pattern=[[0, 4], [4, 2]], base=512, channel_multiplier=-4,
                                          allow_small_or_imprecise_dtypes=True), writes=["Ds"])
            P.op("pool", lambda e: e.iota(Ds[:, 16:24], pattern=[[0, 8]], base=2048, channel_multiplier=-16, allow_small_or_imprecise_dtypes=True), writes=["Ds"])
            P.op("pool", lambda e: e.memset(Vs[:], 1.0), writes=["Vs"])
            P.op("dve", lambda e: e.tensor_scalar(out=Vs[:, 0:8], in0=Ds[:, 0:8], scalar1=128.0, scalar2=None, op0=ALU.is_le), reads=["Ds", "Vs"], writes=["Vs"])
            P.op("dve", lambda e: e.tensor_scalar(out=Vs[:, 8:16], in0=Ds[:, 8:16], scalar1=512.0, scalar2=None, op0=ALU.is_le), reads=["Ds", "Vs"], writes=["Vs"])
            P.op("pool", lambda e: e.iota(Dn[:], pattern=[[1, 8]], base=0, channel_multiplier=-1, allow_small_or_imprecise_dtypes=True), writes=["Dn"])
            P.op("dve", lambda e: e.tensor_scalar(out=Dnx[:], in0=Dn[:], scalar1=0.0, scalar2=None, op0=ALU.max), reads=["Dn"], writes=["Dnx"])
            P.op("dve", lambda e: e.tensor_scalar(out=Wn[:], in0=Dn[:], scalar1=0.0, scalar2=None, op0=ALU.is_ge), reads=["Dn"], writes=["Wn"])
            P.op("dve", lambda e: e.tensor_scalar(out=tnn[:], in0=Dn[:], scalar1=4.0, scalar2=None, op0=ALU.is_equal), reads=["Dn"], writes=["tnn"])
            P.op("dve", lambda e: e.tensor_tensor(out=Wn[:], in0=Wn[:], in1=tnn[:], op=ALU.add), reads=["Wn", "tnn"], writes=["Wn"])
            P.op("dve", lambda e: e.tensor_scalar(out=Z2[:], in0=Dn[:], scalar1=0.0, scalar2=2.0, op0=ALU.is_equal, op1=ALU.mult), reads=["Dn"], writes=["Z2"])

            DILS = (1, 4, 16)
            pj = [0]

            def attn_pair(hp):
                def make_masks(hh):
                    slope = 2.0 ** (-8.0 * (2 * hp + hh + 1) / 16.0)
                    for bi, dil in enumerate(DILS):
                        sc = -slope * dil
                        P.op("act", lambda e, sc=sc: e.activation(out=mtmp[:, 0:128], in_=Dmx[:], func=AF.Exp, scale=sc), reads=["Dmx"], writes=["mtmp0"])
                        P.op("act", lambda e, sc=sc: e.activation(out=mtmp[:, 128:256], in_=Dp[:], func=AF.Exp, scale=sc), reads=["Dp"], writes=["mtmp1"])
                        P.op("dve", lambda e, bi=bi: e.tensor_tensor(out=masks[hh][:, bi, 0:128], in0=mtmp[:, 0:128], in1=vd[:], op=ALU.mult),
                             reads=["mtmp0", "vd"], writes=[("masks", hh)])
                        P.op("dve", lambda e, bi=bi: e.tensor_tensor(out=masks[hh][:, bi, 128:256], in0=mtmp[:, 128:256], in1=vp[:], op=ALU.mult),
                             reads=["mtmp1", "vp"], writes=[("masks", hh)])

                for hh in range(2):
                    make_masks(hh)

                def smasks(hh):
                    slope = 2.0 ** (-8.0 * (2 * hp + hh + 1) / 16.0)
                    P.op("act", lambda e: e.activation(out=stmpm[:], in_=Ds[:], func=AF.Exp, scale=-slope), reads=["Ds"], writes=["stmpm"])
                    P.op("dve", lambda e: e.tensor_tensor(out=smask[:, hh, :], in0=stmpm[:], in1=Vs[:], op=ALU.mult), reads=["stmpm", "Vs"], writes=[("smask", hh)])
                    P.op("act", lambda e: e.activation(out=tnn[:], in_=Dnx[:], func=AF.Exp, scale=-slope), reads=["Dnx"], writes=["tnn"])
                    P.op("dve", lambda e: e.tensor_tensor(out=tnn[:], in0=tnn[:], in1=Wn[:], op=ALU.mult), reads=["tnn", "Wn"], writes=["tnn"])
                    P.op("dve", lambda e: e.tensor_tensor(out=smaskn[:, hh, :], in0=tnn[:], in1=Z2[:], op=ALU.add), reads=["tnn", "Z2"], writes=[("smaskn", hh)])

                stop_at(f"assq{hp}")
                for hh in range(2):
                    smasks(hh)

                for Wt, col, nm in ((Wq, COL_Q, "Wq"), (Wk, COL_K, "Wk"), (Wv, COL_V, "Wv")):
                    P.dma("pool", Wt[:], w_in_v[:, :, col + hp * 128:col + (hp + 1) * 128], writes=[nm])
                kvn = [0]

                def proj(Wt, nm, dst, dnm, u0, src, dkey, j, outd):
                    pp = 4 + pj[0] % 2; pj[0] += 1
                    for kc in range(8):
                        P.op("pe", lambda e, kc=kc: e.matmul(ps[pp][:, :], lhsT=Wt[:, kc, :], rhs=src[:, kc, j * 512:(j + 1) * 512], start=(kc == 0), stop=(kc == 7)),
                             reads=[nm] + xkeys(dkey, 4 * j, 4 * j + 4), writes=[PSK(pp)], inc=(kc == 7))
                    P.op("act", lambda e: e.activation(out=dst[:, u0 + j * 512:u0 + (j + 1) * 512], in_=ps[pp][:, :], func=AF.Copy),
                         reads=[PSK(pp)], writes=[(dnm, (u0 + j * 512) // 512)])
                    if outd is None:
                        return None
                    b2 = kvn[0] % 2; kvn[0] += 1
                    P.op("dve", lambda e: e.tensor_copy(out=kv32[b2][:], in_=ps[pp][:, :]), reads=[PSK(pp)], writes=[("kv32", b2)])

                    def back():
                        for q4 in range(4):
                            P.op("pe", lambda e, q4=q4: e.transpose(out=ps[6][:, q4 * 128:(q4 + 1) * 128], in_=kv32[b2][:, q4 * 128:(q4 + 1) * 128], identity=ident[:]),
                                 reads=[("kv32", b2), "ident"], writes=[PSK(6)], inc=(q4 == 3))
                        P.op("dve", lambda e: e.tensor_copy(out=kvo[b2][:].rearrange("p b d -> p (b d)"), in_=ps[6][:, :]), reads=[PSK(6)], writes=[("kvo", b2)])
                        P.dma("sp", outd[j * 512:(j + 1) * 512, hp * 128:(hp + 1) * 128].rearrange("(b p) d -> p b d", p=128), kvo[b2][:], reads=[("kvo", b2)])
                    return back

                plist = [(Wq, "Wq", QT, "QT", 0, xT_o, "xT_o", j, None) for j in range(4)]
                plist += [(Wk, "Wk", KT, "KT", 0, xT_p, "xT_p", j, None) for j in range(4)]
                plist += [(Wk, "Wk", KT, "KT", T, xT_o, "xT_o", j, k_o) for j in range(4)]
                plist += [(Wv, "Wv", VT, "VT", 0, xT_p, "xT_p", j, None) for j in range(4)]
                plist += [(Wv, "Wv", VT, "VT", T, xT_o, "xT_o", j, v_o) for j in range(4)]
                pend_b = None
                for args in plist:
                    bk_ = proj(*args)
                    if pend_b is not None:
                        pend_b()
                    pend_b = bk_
                if pend_b is not None:
                    pend_b()
                stop_at(f"aproj{hp}")
                allK = [("KT", i) for i in range(8)]
                allV = [("VT", i) for i in range(8)]
                allQ = [("QT", i) for i in range(4)]

                sn = [0]
                qn = [0]
                ACCK = [("acc", i) for i in (0, 1, 2, 3, 9)]

                def branch(bi, dil, hh):
                    nmb = 2 * T // (128 * dil)
                    own0 = nmb // 2
                    slot = {}
                    so, spast = 0, 16
                    tiles = []
                    for r in range(dil):
                        for mb in range(own0 - 1, nmb):
                            if mb >= own0:
                                slot[(r, mb)] = so; so += 1
                            else:
                                slot[(r, mb)] = spast; spast += 1
                            tiles.append((r, mb))
                    order = sorted(tiles, key=lambda t: slot[t])

                    def build_group(grp):
                        for n, (r, mb) in enumerate(grp):
                            u0 = r + dil * 128 * mb
                            P.op("pe", lambda e, n=n, u0=u0: e.transpose(out=psb[6][:, n * 128:(n + 1) * 128], in_=VT[:, u0:u0 + 127 * dil + 1:dil], identity=identb[:]),
                                 reads=allV + ["identb"], writes=[PSK(6)], inc=(n == len(grp) - 1))
                        s0 = slot[grp[0]]
                        ng = len(grp)
                        src_v = psb[6][:, 0:ng * 128].rearrange("p (s a d) -> p s a d", s=ng, a=2)
                        for a_ in range(2):
                            dst_v = Vaug[:, s0:s0 + ng, a_ * 128:a_ * 128 + 64]
                            if s0 >= 16:
                                P.op("dve", lambda e, a_=a_, dst_v=dst_v: e.tensor_scalar(out=dst_v, in0=src_v[:, :, a_, :], scalar1=flag[:, 0:1], scalar2=None, op0=ALU.mult),
                                     reads=[PSK(6), "flag"], writes=[("Vaug", a_)])
                            else:
                                P.op("dve", lambda e, a_=a_, dst_v=dst_v: e.tensor_copy(out=dst_v, in_=src_v[:, :, a_, :]), reads=[PSK(6)], writes=[("Vaug", a_)])

                    g0 = 0
                    while g0 < len(order):
                        g1 = g0
                        while g1 < len(order) and g1 - g0 < 8 and (slot[order[g1]] >= 16) == (slot[order[g0]] >= 16):
                            g1 += 1
                        build_group(order[g0:g1])
                        g0 = g1

                    hsl = slice(64 * hh, 64 * hh + 64)
                    vsl = slice(64 * hh, 64 * hh + 128)
                    if dil == 1:
                        quads = [[(0, own0 + 4 * q + k) for k in range(4)] for q in range(4)]
                        accv = [acc[:, q * 512:(q + 1) * 512] for q in range(4)]
                    elif dil == 4:
                        quads = [[(r, own0 + k) for k in range(4)] for r in range(4)]
                        accv = [acc[:, r:T:4] for r in range(4)]
                    else:
                        quads = [[(4 * q + k, 1) for k in range(4)] for q in range(4)]
                        accv = [acc[:].rearrange("p (i s) -> p s i", s=16)[:, 4 * q:4 * q + 4, :] for q in range(4)]

                    def s_part(q, half):
                        sbk = sn[0] % 2; sn[0] += 1
                        pair = quads[q][2 * half:2 * half + 2]
                        for n, (r, mb) in enumerate(pair):
                            t0 = r + dil * 128 * (mb - own0)
                            for w, kmb in enumerate((mb, mb - 1)):
                                u0 = r + dil * 128 * kmb
                                P.op("pe", lambda e, n=n, w=w, t0=t0, u0=u0: e.matmul(
                                    ps[sbk][:, (2 * n + w) * 128:(2 * n + w + 1) * 128], lhsT=KT[hsl, u0:u0 + 127 * dil + 1:dil], rhs=QT[hsl, t0:t0 + 127 * dil + 1:dil],
                                    start=True, stop=True),
                                    reads=allK + allQ, writes=[PSK(sbk)], inc=(n == 1 and w == 1))
                        P.op("act", lambda e: e.activation(out=Eb[sbk][:], in_=ps[sbk][:, :], func=AF.Exp, scale=0.125), reads=[PSK(sbk)], writes=[("Eb", sbk)])
                        P.op("dve", lambda e: e.tensor_tensor(out=Pm[sbk][:].rearrange("p (n c) -> p n c", n=2), in0=Eb[sbk][:].rearrange("p (n c) -> p n c", n=2),
                                                              in1=masks[hh][:, bi, :].unsqueeze(1).broadcast_to([128, 2, 256]), op=ALU.mult),
                             reads=[("Eb", sbk), ("masks", hh)], writes=[("Pm", sbk)])
                        return sbk

                    def pv_part(q, half, ob, sbk):
                        pair = quads[q][2 * half:2 * half + 2]
                        for n, (r, mb) in enumerate(pair):
                            k4 = 2 * half + n
                            for w, kmb in enumerate((mb, mb - 1)):
                                sl_ = slot[(r, kmb)]
                                P.op("pe", lambda e, n=n, w=w, k4=k4, sl_=sl_: e.matmul(
                                    ps[ob][:, k4 * 128:(k4 + 1) * 128], lhsT=Vaug[:, sl_, vsl], rhs=Pm[sbk][:, (2 * n + w) * 128:(2 * n + w + 1) * 128],
                                    start=(w == 0), stop=(w == 1), skip_group_check=True),
                                    reads=[("Vaug", 0), ("Vaug", 1), "Vones", ("Pm", sbk)], writes=[PSK(ob)], inc=(w == 1 and n == 1))

                    def evac_quad(q, ob):
                        if dil == 16:
                            src_q = ps[ob][:, :].rearrange("p (k i) -> p k i", k=4)
                        else:
                            src_q = ps[ob][:, :]
                        if bi == 0:
                            P.op("act", lambda e: e.activation(out=accv[q], in_=src_q, func=AF.Copy), reads=[PSK(ob)], writes=[("acc", q)])
                        else:
                            P.op("dve", lambda e: e.tensor_tensor(out=accv[q], in0=src_q, in1=accv[q], op=ALU.add), reads=[PSK(ob)] + ACCK, writes=[("acc", 9)])

                    items = [(q, half) for q in range(4) for half in range(2)]
                    obs = {}
                    for q in range(4):
                        obs[q] = 2 + qn[0] % 2; qn[0] += 1
                    sb_of = {}
                    LOOK = 2
                    for idx in range(len(items) + LOOK):
                        j_ = idx - LOOK
                        if j_ >= 0:
                            q, half = items[j_]
                            pv_part(q, half, obs[q], sb_of[j_])
                            if half == 1:
                                evac_quad(q, obs[q])
                        if idx < len(items):
                            sb_of[idx] = s_part(*items[idx])

                def finish_head(hh):
                    numsl = slice(64 * hh, 64 * hh + 64)
                    densl = slice(64 - 64 * hh, 128 - 64 * hh)
                    for j in range(4):
                        P.op("act", lambda e, j=j: e.activation(out=rden[numsl, :], in_=acc[densl, j * 512:(j + 1) * 512], func=AF.Ln), reads=ACCK, writes=["rden"])
                        P.op("act", lambda e: e.activation(out=rden[numsl, :], in_=rden[numsl, :], func=AF.Exp, scale=-1.0), reads=["rden"], writes=["rden"])
                        P.op("dve", lambda e, j=j: e.tensor_tensor(out=mixT_a[numsl, hp, j * 512:(j + 1) * 512], in0=acc[numsl, j * 512:(j + 1) * 512], in1=rden[numsl, :], op=ALU.mult),
                             reads=["rden"] + ACCK, writes=[("mixT_a", hp, hh)])

                for hh in range(2):
                    for bi, dil in enumerate(DILS):
                        branch(bi, dil, hh)
                        stop_at(f"abr{hp}_{hh}_{bi}")
                    finish_head(hh)
                    stop_at(f"afin{hp}_{hh}")
                for j4 in range(4):
                    P.op("act", lambda e, j4=j4: e.activation(out=sq[:], in_=mixT_a[:, hp, j4 * 512:(j4 + 1) * 512], func=AF.Square),
                         reads=[("mixT_a", hp, 0), ("mixT_a", hp, 1)], writes=["sq"])
                    for b4 in range(4):
                        blk = 4 * j4 + b4
                        P.op("pe", lambda e, blk=blk, b4=b4: e.matmul(ps[7][:, blk:blk + 1], lhsT=sq[:, b4 * 128:(b4 + 1) * 128], rhs=onesb[:], start=True, stop=True, skip_group_check=True),
                             reads=["sq", "onesb"], writes=[PSK(7)], inc=(b4 == 3))
                P.op("dve", lambda e: e.tensor_tensor(out=ssq_a[:], in0=ps[7][:, 0:NB], in1=ssq_a[:], op=ALU.add), reads=[PSK(7), "ssq_a"], writes=["ssq_a"])

                cols = slice(hp * 128, (hp + 1) * 128)
                for bi_, (Wt, nm, dst, dk) in enumerate(((Wq, "Wq", sQT, "sQT"), (Wk, "Wk", sKT, "sKT"))):
                    pb_ = 4 + bi_
                    for kc in range(8):
                        P.op("pe", lambda e, kc=kc, Wt=Wt, pb_=pb_: e.matmul(ps[pb_][:, 0:32], lhsT=Wt[:, kc, :], rhs=xsT[:, kc, :], start=(kc == 0), stop=(kc == 7)),
                             reads=[nm, "xsT"], writes=[PSK(pb_)], inc=(kc == 7))
                    P.op("act", lambda e, dst=dst, pb_=pb_: e.activation(out=dst[:], in_=ps[pb_][:, 0:32], func=AF.Copy), reads=[PSK(pb_)], writes=[dk])
                for a_, (Wt, nm, outd) in enumerate(((Wk, "Wk", kn_s), (Wv, "Wv", vn_s))):
                    pb_ = 4 + a_
                    for kc in range(8):
                        P.op("pe", lambda e, kc=kc, Wt=Wt, pb_=pb_: e.matmul(ps[pb_][0:32, 0:128], lhsT=xsT[:, kc, :], rhs=Wt[:, kc, :], start=(kc == 0), stop=(kc == 7)),
                             reads=[nm, "xsT"], writes=[PSK(pb_)], inc=(kc == 7))
                    P.op("dve", lambda e, a_=a_, pb_=pb_: e.tensor_copy(out=sknv[:, a_, :], in_=ps[pb_][0:32, 0:128]), reads=[PSK(pb_)], writes=[("sknv", a_)])
                    P.dma("sp", outd[:, cols], sknv[:, a_, :], reads=[("sknv", a_)])
                for sq_ in range(4):
                    pb_ = 4 + sq_ % 2
                    for kc in range(8):
                        P.op("pe", lambda e, kc=kc, sq_=sq_, pb_=pb_: e.matmul(ps[pb_][0:8, 128:256],
                                                                               lhsT=xsT[:, kc, sq_ * 8:(sq_ + 1) * 8], rhs=Wv[:, kc, :], start=(kc == 0), stop=(kc == 7)),
                             reads=["Wv", "xsT"], writes=[PSK(pb_)], inc=(kc == 7))
                    P.op("dve", lambda e, sq_=sq_, pb_=pb_: e.tensor_copy(out=sVnew[:, sq_, 0:64], in_=ps[pb_][0:8, 128:192]), reads=[PSK(pb_)], writes=[("sVnew", sq_)])
                    P.op("act", lambda e, sq_=sq_, pb_=pb_: e.activation(out=sVnew[:, sq_, 128:192], in_=ps[pb_][0:8, 192:256], func=AF.Copy), reads=[PSK(pb_)], writes=[("sVnew", sq_)])

                ktf = VT[:, 0:2 * 13 * 128].bitcast(F32).rearrange("p (k d) -> p k d", d=128)
                svf = acc[:, 0:13 * 128].rearrange("p (k d) -> p k d", d=128)
                ktfk = [("VT", i_) for i_ in range(7)]

                def sample_load(sq_):
                    for dst_, src_, wk in ((ktf, ck, ktfk), (svf, cv, ACCK)):
                        P.dma("sp", dst_[:, 0, :], src_[sq_, 1920:2048, cols], writes=wk)
                        P.dma("sp", dst_[:, 1:5, :], src_[sq_, 1536:2048, cols].rearrange("(i c) d -> i c d", c=4), writes=wk)
                        P.dma("sp", dst_[:, 5:13, :], src_[sq_, :, cols].rearrange("(i s) d -> i s d", s=16)[:, 0:8, :], writes=wk)

                def sample_A1(sq_):
                    bf = sq_ % 2
                    sVaug = sVaug2[bf]; KTc = KTc2[bf]
                    ktk = [("KT", 4 * bf + i_) for i_ in range(4)]
                    svk = [("sV", bf, 0), ("sV", bf, 1)] + ([("Vaug", 0), ("Vaug", 1)] if sq_ < 2 else [])
                    for k0 in range(0, 13, 4):
                        k1 = min(13, k0 + 4)
                        for k in range(k0, k1):
                            P.op("pe", lambda e, k=k, k0=k0: e.transpose(out=ps[6][:, (k - k0) * 128:(k - k0 + 1) * 128], in_=ktf[:, k, :], identity=ident[:]),
                                 reads=ktfk + ["ident"], writes=[PSK(6)], inc=(k == k1 - 1))
                        P.op("dve" if (k0 // 4) % 2 == 0 else "act", (lambda e, k0=k0, k1=k1: e.tensor_copy(out=KTc[:, k0:k1, :], in_=ps[6][:, 0:(k1 - k0) * 128].rearrange("p (k d) -> p k d", d=128)))
                             if (k0 // 4) % 2 == 0 else
                             (lambda e, k0=k0, k1=k1: e.activation(out=KTc[:, k0:k1, :], in_=ps[6][:, 0:(k1 - k0) * 128].rearrange("p (k d) -> p k d", d=128), func=AF.Copy)),
                             reads=[PSK(6)], writes=ktk)
                    P.op("pool", lambda e: e.tensor_copy(out=sVaug[:, :, 0:64], in_=svf[:, :, 0:64]), reads=ACCK, writes=svk)
                    P.op("pool", lambda e: e.tensor_copy(out=sVaug[:, :, 128:192], in_=svf[:, :, 64:128]), reads=ACCK, writes=svk)

                def sample_A(sq_):
                    bf = sq_ % 2
                    sVaug = sVaug2[bf]; KTc = KTc2[bf]
                    ktk = [("KT", 4 * bf + i_) for i_ in range(4)]
                    sE = sE2[bf]; sPm = sPm2[bf]; sEn = sEn2[bf]; sPn = sPn2[bf]
                    sbank = ps[bf]
                    qs = sq_ * 8
                    for hh in range(2):
                        hsl = slice(64 * hh, 64 * hh + 64)
                        c0 = hh * 24
                        mm = []
                        mm.append((sbank[:, c0:c0 + 8], KTc[hsl, 0, :], sQT[hsl, qs:qs + 8]))
                        for c in range(4):
                            mm.append((sbank[:, c0 + 8 + 2 * c:c0 + 10 + 2 * c], KTc[hsl, 1 + c, :], sQT[hsl, qs + c:qs + c + 5:4]))
                        for t in range(8):
                            mm.append((sbank[:, c0 + 16 + t:c0 + 17 + t], KTc[hsl, 5 + t, :], sQT[hsl, qs + t:qs + t + 1]))
                        mm.append((sbank[0:8, 256 + hh * 8:264 + hh * 8], sKT[hsl, qs:qs + 8], sQT[hsl, qs:qs + 8]))
                        for n, (o_, l_, r_) in enumerate(mm):
                            P.op("pe", lambda e, o_=o_, l_=l_, r_=r_: e.matmul(o_, lhsT=l_, rhs=r_, start=True, stop=True, skip_group_check=True),
                                 reads=ktk + ["sQT", "sKT"], writes=[PSK(bf)], inc=(n == len(mm) - 1))
                    P.op("act", lambda e: e.activation(out=sE[:], in_=sbank[:, 0:48], func=AF.Exp, scale=0.125), reads=[PSK(bf)], writes=[("sE", bf)])
                    P.op("act", lambda e: e.activation(out=sEn[:], in_=sbank[0:8, 256:272], func=AF.Exp, scale=0.125), reads=[PSK(bf)], writes=[("sEn", bf)])
                    P.op("dve", lambda e: e.tensor_tensor(out=sPm[:], in0=sE[:], in1=smask[:].rearrange("p h c -> p (h c)"), op=ALU.mult), reads=[("sE", bf), ("smask", 0), ("smask", 1)], writes=[("sPm", bf)])
                    P.op("dve", lambda e: e.tensor_tensor(out=sPn[:], in0=sEn[:], in1=smaskn[:].rearrange("p h c -> p (h c)"), op=ALU.mult), reads=[("sEn", bf), ("smaskn", 0), ("smaskn", 1)], writes=[("sPn", bf)])

                def sample_B(sq_):
                    bf = sq_ % 2
                    sVaug = sVaug2[bf]
                    sPm = sPm2[bf]; sPn = sPn2[bf]
                    qs = sq_ * 8
                    obank = ps[2 + bf]
                    P.op("dve", lambda e: e.memset(obank[:, 0:16], 0.0), writes=[PSK(2 + bf)])
                    for hh in range(2):
                        vsl = slice(64 * hh, 64 * hh + 128)
                        c0 = hh * 24
                        o0 = hh * 8
                        mm = []
                        mm.append((obank[:, o0:o0 + 8], sVaug[:, 0, vsl], sPm[:, c0:c0 + 8]))
                        for c in range(4):
                            mm.append((obank[:, o0 + c:o0 + c + 5:4], sVaug[:, 1 + c, vsl], sPm[:, c0 + 8 + 2 * c:c0 + 10 + 2 * c]))
                        for t in range(8):
                            mm.append((obank[:, o0 + t:o0 + t + 1], sVaug[:, 5 + t, vsl], sPm[:, c0 + 16 + t:c0 + 17 + t]))
                        mm.append((obank[:, o0:o0 + 8], sVnew[0:8, sq_, vsl], sPn[0:8, hh * 8:hh * 8 + 8]))
                        for n, (o_, l_, r_) in enumerate(mm):
                            P.op("pe", lambda e, o_=o_, l_=l_, r_=r_: e.matmul(o_, lhsT=l_, rhs=r_, start=False, stop=(n == len(mm) - 1), skip_group_check=True),
                                 reads=[("sV", bf, 0), ("sV", bf, 1), ("Vaug", 0), ("Vaug", 1), "Vones", "sVnones", ("sVnew", sq_), ("sPm", bf), ("sPn", bf)], writes=[PSK(2 + bf)], inc=(n == len(mm) - 1))
                    for hh in range(2):
                        numsl = slice(64 * hh, 64 * hh + 64)
                        densl = slice(64 - 64 * hh, 128 - 64 * hh)
                        o0 = hh * 8
                        P.op("dve", lambda e, numsl=numsl, densl=densl, o0=o0: e.reciprocal(out=srd[numsl, :], in_=obank[densl, o0:o0 + 8]), reads=[PSK(2 + bf)], writes=["srd"])
                        P.op("dve", lambda e, numsl=numsl, o0=o0: e.tensor_tensor(out=smixT_a[numsl, hp, qs:qs + 8], in0=obank[numsl, o0:o0 + 8], in1=srd[numsl, :], op=ALU.mult),
                             reads=[PSK(2 + bf), "srd"], writes=[("smixT_a", hp)])

                P.op("pool", lambda e: e.memset(Vaug[:, 16:29, 64:128], 1.0), writes=["Vones"])
                sample_load(0)
                sample_A1(0)
                sample_load(1)
                sample_A(0)
                sample_A1(1)
                sample_load(2)
                sample_B(0)
                sample_A(1)
                sample_A1(2)
                sample_load(3)
                sample_B(1)
                sample_A(2)
                sample_A1(3)
                sample_B(2)
                sample_A(3)
                sample_B(3)
                P.op("dve", lambda e: e.tensor_copy(out=Vaug[:, 16:32, 64:128], in_=flag[:, 0:1].unsqueeze(1).broadcast_to([128, 16, 64])), reads=["flag"], writes=["Vones"])
                P.op("act", lambda e: e.activation(out=sqs[:], in_=smixT_a[:, hp, :], func=AF.Square), reads=[("smixT_a", hp)], writes=["sqs"])
                P.op("pe", lambda e: e.matmul(ps[7][0:32, 32:33], lhsT=sqs[:], rhs=onesb[:], start=True, stop=True), reads=["sqs", "onesb"], writes=[PSK(7)])
                P.op("dve", lambda e: e.tensor_tensor(out=sssq_a[:], in0=ps[7][0:32, 32:33], in1=sssq_a[:], op=ALU.add), reads=[PSK(7), "sssq_a"], writes=["sssq_a"])
                stop_at(f"a{hp}")

            for hp in range(NPAIR):
                attn_pair(hp)
            if "mixT_a" in dbg:
                P.dma("pool", dbg_out["mixT_a"].rearrange("p (c t) -> p c t", c=8), mixT_a[:], reads=[("mixT_a", hp, hh) for hp in range(8) for hh in range(2)])
            if "ssq_a" in dbg:
                P.dma("sp", dbg_out["ssq_a"], ssq_a[:], reads=["ssq_a"])
            if "smixT_a" in dbg:
                P.dma("pool", dbg_out["smixT_a"].rearrange("p (c t) -> p c t", c=8), smixT_a[:], reads=[("smixT_a", hp) for hp in range(8)])
            P.flush()
        xsc.close()

        AXX = mybir.AxisListType.X
        hdn_scr = nc.dram_tensor("hdn_scr", [T + 32, D], F32, kind="Internal").ap()
        hdnT_all = sb(es, "hdnT_all", [128, 8, T], BF16)
        shdnT = sb(es, "shdnT", [128, 8, 32], BF16)

        def rstd_from_ssq(dst, src, nparts, ncols, tmpc, key_src, key_dst):
            P.op("pool", lambda e: e.tensor_scalar(out=dst[0:nparts, 0:ncols], in0=src, scalar1=1.0 / 1024.0, scalar2=RMS_EPS, op0=ALU.mult, op1=ALU.add),
                 reads=[key_src], writes=[key_dst])
            P.op("pool", lambda e: e.tensor_tensor(out=dst[0:nparts, 0:ncols], in0=dst[0:nparts, 0:ncols], in1=tmpc[0:nparts, 0:ncols], op=ALU.pow),
                 reads=[key_dst, "mhalf"], writes=[key_dst])

        bst = sb(es, "bst", [128, 2, 6], F32)
        mv = sb(es, "mv", [128, 2], F32)
        lnr = sb(es, "lnr", [128, 1], F32)
        lnb = sb(es, "lnb", [128, 1], F32)
        lnh = sb(es, "lnh", [128, 1], F32)
        P.op("pool", lambda e: e.memset(lnh[:], -0.5), writes=["lnh"])

        def layer_norm(src_t, dst_t, nt, gbc, bbc, skey, dkey, gk, bk):
            skeys = skey if isinstance(skey, list) else [skey]
            for h2 in range(2):
                P.op("dve", lambda e, h2=h2: e.bn_stats(out=bst[0:nt, h2, :], in_=src_t[0:nt, h2 * 512:(h2 + 1) * 512]), reads=skeys, writes=[("bst", h2)])
            P.op("dve", lambda e: e.bn_aggr(out=mv[0:nt, :], in_=bst[0:nt, :, :].rearrange("p a s -> p (a s)")), reads=[("bst", 0), ("bst", 1)], writes=["mv"])
            P.op("pool", lambda e: e.tensor_scalar(out=lnr[0:nt, :], in0=mv[0:nt, 1:2], scalar1=LN_EPS, scalar2=None, op0=ALU.add), reads=["mv"], writes=["lnr"])
            P.op("pool", lambda e: e.tensor_tensor(out=lnr[0:nt, :], in0=lnr[0:nt, :], in1=lnh[0:nt, :], op=ALU.pow), reads=["lnr", "lnh"], writes=["lnr"])
            P.op("dve", lambda e: e.scalar_tensor_tensor(out=lnb[0:nt, :], in0=mv[0:nt, 0:1], scalar=-1.0, in1=lnr[0:nt, :], op0=ALU.mult, op1=ALU.mult),
                 reads=["mv", "lnr"], writes=["lnb"])
            P.op("act", lambda e: e.activation(out=dst_t[0:nt, :], in_=src_t[0:nt, :], func=AF.Identity, scale=lnr[0:nt, 0:1], bias=lnb[0:nt, 0:1]),
                 reads=skeys + ["lnr", "lnb"], writes=[dkey])
            P.op("pool", lambda e: e.tensor_tensor(out=dst_t[0:nt, :], in0=dst_t[0:nt, :], in1=gbc[0:nt, :], op=ALU.mult), reads=[dkey, gk], writes=[dkey])
            P.op("pool", lambda e: e.tensor_tensor(out=dst_t[0:nt, :], in0=dst_t[0:nt, :], in1=bbc[0:nt, :], op=ALU.add), reads=[dkey, bk], writes=[dkey])


        with ExitStack() as ph:
            Wo = sb(ph, "Wo", [128, 16, D], BF16)
            gcol = sb(ph, "gcol", [128, 16], F32)
            g1_bc = sb(ph, "g1_bc", [128, D], F32)
            b1_bc = sb(ph, "b1_bc", [128, D], F32)
            mhalf = sb(ph, "mhalf", [128, NB], F32)
            rs_a = sb(ph, "rs_a", [128, NB], F32)
            rs_s = sb(ph, "rs_s", [128, NB], F32)
            sst = sb(ph, "sst", [128, NB], F32)
            xt = [sb(ph, f"xt{i}", [128, D], F32) for i in range(3)]
            pre = [sb(ph, f"pre{i}", [128, D], F32) for i in range(3)]
            hdn = [sb(ph, f"hdn{i}", [128, D], F32) for i in range(3)]

            for c in range(16):
                P.dma("pool", Wo[:, c, :], w_out[c * 128:(c + 1) * 128, :], writes=[("Wo", c)])
            P.dma("sp", gcol[:, 0:8], attn_g.rearrange("(c p) -> p c", p=128), writes=["gcol"], allow_slow_non_contiguous=True)
            P.dma("sp", gcol[:, 8:16], ssm_g.rearrange("(c p) -> p c", p=128), writes=["gcol"], allow_slow_non_contiguous=True)
            P.dma("sp", g1_bc[:], ln1_g.partition_broadcast(128), writes=["g1_bc"])
            P.dma("sp", b1_bc[:], ln1_b.partition_broadcast(128), writes=["b1_bc"])
            for c in range(16):
                P.op("dve", lambda e, c=c: e.tensor_scalar(out=Wo[:, c, :], in0=Wo[:, c, :], scalar1=gcol[:, c:c + 1], scalar2=None, op0=ALU.mult),
                     reads=[("Wo", c), "gcol"], writes=[("Wo", c)])
            P.op("pool", lambda e: e.memset(mhalf[:], -0.5), writes=["mhalf"])
            P.op("dve", lambda e: e.tensor_reduce(out=sst[:], in_=ssq_s[:], axis=AXX, op=ALU.add),
                 reads=[("ssq_s", c, g) for g in range(4) for c in range(NB)], writes=["sst"])
            rstd_from_ssq(rs_a, ssq_a[:], 128, NB, mhalf, "ssq_a", "rs_a")
            rstd_from_ssq(rs_s, sst[:], 128, NB, mhalf, "sst", "rs_s")
            allWo = [("Wo", c) for c in range(16)]
            mixa_keys = [("mixT_a", hp, hh) for hp in range(8) for hh in range(2)]

            def outproj_block(tb, nt, mA, mS, tcol, akeys, skeys, rsa_col, rss_col, xrows, hT_dst, hT_col, scr_rows, hkey, rak="rs_a", rsk="rs_s"):
                b = tb % 3
                P.dma("sp", xt[b][0:nt, :], xrows, writes=[("xt", b)])
                for half in range(2):
                    for c in range(8):
                        P.op("pe", lambda e, c=c, half=half: e.matmul(ps[half][0:nt, :], lhsT=mA[:, c, tcol:tcol + nt], rhs=Wo[:, c, half * 512:(half + 1) * 512],
                                                                      start=(c == 0), stop=(c == 7)),
                             reads=akeys + [("Wo", c)], writes=[PSK(half)], inc=(c == 7))
                    for c in range(8):
                        P.op("pe", lambda e, c=c, half=half: e.matmul(ps[2 + half][0:nt, :], lhsT=mS[:, c, tcol:tcol + nt], rhs=Wo[:, 8 + c, half * 512:(half + 1) * 512],
                                                                      start=(c == 0), stop=(c == 7)),
                             reads=skeys + [("Wo", 8 + c)], writes=[PSK(2 + half)], inc=(c == 7))
                for half in range(2):
                    hs_ = slice(half * 512, (half + 1) * 512)
                    P.op("act", lambda e, half=half, hs_=hs_: e.activation(out=pre[b][0:nt, hs_], in_=ps[half][0:nt, :], func=AF.Identity, scale=rsa_col),
                         reads=[PSK(half), rak], writes=[("pre", b, half)])
                    P.op("dve", lambda e, hs_=hs_: e.scalar_tensor_tensor(out=pre[b][0:nt, hs_], in0=xt[b][0:nt, hs_], scalar=ALPHA, in1=pre[b][0:nt, hs_], op0=ALU.mult, op1=ALU.add),
                         reads=[("xt", b), ("pre", b, half)], writes=[("pre", b, half)])
                    P.op("dve", lambda e, half=half, hs_=hs_: e.scalar_tensor_tensor(out=pre[b][0:nt, hs_], in0=ps[2 + half][0:nt, :], scalar=rss_col, in1=pre[b][0:nt, hs_],
                                                                                     op0=ALU.mult, op1=ALU.add),
                         reads=[PSK(2 + half), rsk, ("pre", b, half)], writes=[("pre", b, half)])
                layer_norm(pre[b], hdn[b], nt, g1_bc, b1_bc, [("pre", b, 0), ("pre", b, 1)], ("hdn", b), "g1_bc", "b1_bc")
                P.dma("sp", scr_rows, hdn[b][0:nt, :], reads=[("hdn", b)], writes=["hdn_scr"])

                def back():
                    back_part(tb, nt, b, hT_dst, hT_col, hkey)
                return back

            def back_part(tb, nt, b, hT_dst, hT_col, hkey):
                for kc in range(8):
                    pt = 4 + kc // 4
                    P.op("pe", lambda e, kc=kc, pt=pt: e.transpose(out=ps[pt][:, (kc % 4) * 128:(kc % 4) * 128 + nt], in_=hdn[b][0:nt, kc * 128:(kc + 1) * 128], identity=ident[0:nt, 0:nt]),
                         reads=[("hdn", b), "ident"], writes=[PSK(pt)], inc=(kc % 4 == 3))
                P.op("act", lambda e: e.activation(out=hT_dst[:, 0:4, hT_col:hT_col + nt], in_=ps[4][:, :].rearrange("p (c t) -> p c t", c=4)[:, :, 0:nt], func=AF.Copy),
                     reads=[PSK(4)], writes=[(hkey, tb, 0)])
                P.op("dve", lambda e: e.tensor_copy(out=hT_dst[:, 4:8, hT_col:hT_col + nt], in_=ps[5][:, :].rearrange("p (c t) -> p c t", c=4)[:, :, 0:nt]),
                     reads=[PSK(5)], writes=[(hkey, tb, 1)])

            pendq = []
            for tb in range(NB):
                bk = outproj_block(tb, 128, mixT_a, mixT_s, tb * 128, mixa_keys, [("mixT_s", g, tb) for g in range(4)],
                                   rs_a[:, tb:tb + 1], rs_s[:, tb:tb + 1], xo[tb * 128:(tb + 1) * 128, :], hdnT_all, tb * 128,
                                   hdn_scr[tb * 128:(tb + 1) * 128, :], "hdnT")
                pendq.append(bk)
                if len(pendq) > 2:
                    pendq.pop(0)()
                stop_at(f"o{tb}")
            srs_a = sb(ph, "srs_a", [32, 1], F32)
            srs_s = sb(ph, "srs_s", [32, 1], F32)
            ssst = sb(ph, "ssst", [32, 1], F32)
            P.op("dve", lambda e: e.tensor_reduce(out=ssst[:], in_=sssq_s[:], axis=AXX, op=ALU.add), reads=[("sssq_s", g) for g in range(4)], writes=["ssst"])
            rstd_from_ssq(srs_a, sssq_a[:], 32, 1, mhalf, "sssq_a", "srs_a")
            rstd_from_ssq(srs_s, ssst[:], 32, 1, mhalf, "ssst", "srs_s")
            bk = outproj_block(16, 32, smixT_a, smixT_s, 0, [("smixT_a", hp) for hp in range(8)], [("smixT_s", g) for g in range(4)],
                               srs_a[:, 0:1], srs_s[:, 0:1], xs_d, shdnT, 0, hdn_scr[T:T + 32, :], "shdnT", rak="srs_a", rsk="srs_s")
            for f_ in pendq:
                f_()
            bk()
            if "shdn" in dbg:
                P.flush()
                P.dma("sp", dbg_out["shdn"], hdn_scr[T:T + 32, :], reads=["hdn_scr"])
            if "hdn" in dbg:
                P.flush()
                P.dma("sp", dbg_out["hdn"], hdn_scr[0:T, :], reads=["hdn_scr"])
            P.flush()

        msc.close()
        w_gate_v = w_gate.rearrange("(kc p) n -> p kc n", p=128)
        w_up_v = w_up.rearrange("(kc p) n -> p kc n", p=128)
        with ExitStack() as ph:
            Wd = sb(ph, "Wd", [128, NF, D], BF16)
            g2_bc = sb(ph, "g2_bc", [128, D], F32)
            b2_bc = sb(ph, "b2_bc", [128, D], F32)
            Wgu = [sb(ph, f"Wgu{i}", [128, 2, 8, 128], BF16) for i in range(3)]
            hTf = sb(ph, "hTf", [128, NF, 1056], BF16)
            sg = [sb(ph, f"sg{i}", [128, 512], BF16) for i in range(4)]
            hres = [sb(ph, f"hres{i}", [128, D], F32) for i in range(2)]
            ypre = [sb(ph, f"ypre{i}", [128, D], F32) for i in range(2)]
            yout = [sb(ph, f"yout{i}", [128, D], F32) for i in range(2)]
            wd_done = [False]
            P.dma("sp", g2_bc[:], ln2_g.partition_broadcast(128), writes=["g2_bc"])
            P.dma("sp", b2_bc[:], ln2_b.partition_broadcast(128), writes=["b2_bc"])
            wn = [0]
            bn = [0]

            def ffn_tile(segs, hkeys, blocks, tag):
                offs = []
                o_ = 0
                for (_, _, n_) in segs:
                    offs.append(o_); o_ += n_
                for f in range(NF):
                    wi = wn[0] % 3; wn[0] += 1
                    par = f % 2
                    P.dma("pool", Wgu[wi][:, 0, :, :], w_gate_v[:, :, f * 128:(f + 1) * 128], writes=[("Wgu", wi, 0)])
                    P.dma("pool", Wgu[wi][:, 1, :, :], w_up_v[:, :, f * 128:(f + 1) * 128], writes=[("Wgu", wi, 1)])
                    if not wd_done[0]:
                        P.dma("pool", Wd[:, f, :], w_down[f * 128:(f + 1) * 128, :], writes=[("Wd", f)])
                    for si, (hsrc, hcol, n_) in enumerate(segs):
                        if len(segs) <= 2:
                            pa, pb, ca, cb_ = 4 * par + si, 4 * par + 2 + si, 0, 0
                            sgi = 2 * par + si
                        elif si < 2:
                            pa, pb, ca, cb_ = 2 * si, 2 * si + 1, 0, 0
                            sgi = si
                        else:
                            pa, pb, ca, cb_ = 4, 4, 0, 32
                            sgi = 2
                        for gu, pp, cc in ((0, pa, ca), (1, pb, cb_)):
                            for kc in range(8):
                                P.op("pe", lambda e, kc=kc, gu=gu, pp=pp, cc=cc, wi=wi, hsrc=hsrc, hcol=hcol, n_=n_: e.matmul(
                                    ps[pp][:, cc:cc + n_], lhsT=Wgu[wi][:, gu, kc, :], rhs=hsrc[:, kc, hcol:hcol + n_], start=(kc == 0), stop=(kc == 7), skip_group_check=True),
                                    reads=[("Wgu", wi, gu)] + hkeys, writes=[PSK(pp)], inc=(kc == 7))
                        P.op("act", lambda e, pa=pa, ca=ca, sgi=sgi, n_=n_: e.activation(out=sg[sgi][:, 0:n_], in_=ps[pa][:, ca:ca + n_], func=AF.Silu), reads=[PSK(pa)], writes=[("sg", sgi)])
                        o2 = offs[si]
                        P.op("dve", lambda e, pb=pb, cb_=cb_, sgi=sgi, f=f, n_=n_, o2=o2: e.tensor_tensor(out=hTf[:, f, o2:o2 + n_], in0=ps[pb][:, cb_:cb_ + n_], in1=sg[sgi][:, 0:n_], op=ALU.mult),
                             reads=[PSK(pb), ("sg", sgi)], writes=[("hTf", f)])
                wd_done[0] = True
                for (col, nt, scr_rows, y_rows) in blocks:
                    b = bn[0] % 2; bn[0] += 1
                    P.dma("sp", hres[b][0:nt, :], scr_rows, reads=["hdn_scr"], writes=[("hres", b)])
                    for half in range(2):
                        pp = 4 + 2 * b + half
                        for f in range(NF):
                            P.op("pe", lambda e, f=f, half=half, pp=pp, col=col, nt=nt: e.matmul(ps[pp][0:nt, :], lhsT=hTf[:, f, col:col + nt], rhs=Wd[:, f, half * 512:(half + 1) * 512],
                                                                                                start=(f == 0), stop=(f == NF - 1)),
                                 reads=[("hTf", f), ("Wd", f)], writes=[PSK(pp)], inc=(f == NF - 1))
                        P.op("dve", lambda e, half=half, pp=pp, b=b, nt=nt: e.scalar_tensor_tensor(out=ypre[b][0:nt, half * 512:(half + 1) * 512], in0=hres[b][0:nt, half * 512:(half + 1) * 512],
                                                                                                 scalar=ALPHA, in1=ps[pp][0:nt, :], op0=ALU.mult, op1=ALU.add),
                             reads=[("hres", b), PSK(pp)], writes=[("ypre", b)])
                    layer_norm(ypre[b], yout[b], nt, g2_bc, b2_bc, ("ypre", b), ("yout", b), "g2_bc", "b2_bc")
                    P.dma("sp", y_rows, yout[b][0:nt, :], reads=[("yout", b)])
                stop_at(f"f{tag}")

            for j in range(T // 1024):
                segs_ = [(hdnT_all, j * 1024, 512), (hdnT_all, j * 1024 + 512, 512)]
                keys_ = [("hdnT", tb, k) for tb in range(8 * j, 8 * j + 8) for k in range(2)]
                blocks_ = [(i * 128, 128, hdn_scr[j * 1024 + i * 128:j * 1024 + (i + 1) * 128, :], y_o[j * 1024 + i * 128:j * 1024 + (i + 1) * 128, :]) for i in range(8)]
                if j == T // 1024 - 1:
                    segs_.append((shdnT, 0, 32))
                    keys_ += [("shdnT", 16, 0), ("shdnT", 16, 1)]
                    blocks_.append((1024, 32, hdn_scr[T:T + 32, :], y_s))
                ffn_tile(segs_, keys_, blocks_, j)
            P.flush()
    return nc


def make_in_maps(inputs):
    f = lambda a: np.ascontiguousarray(np.asarray(a, dtype=np.float32))
    xpr = f(inputs["x_prompt"]); xsm = f(inputs["x_sample"])
    ckw = f(inputs["cache_k_win"])[0]; cvw = f(inputs["cache_v_win"])[0]
    sss = f(inputs["state_ssm"])[0]; scv = f(inputs["state_conv"])[0]
    common = {
        "w_in": f(inputs["w_in"])[0], "conv_w": f(inputs["conv_w"])[0], "conv_b": f(inputs["conv_b"])[0],
        "dt_bias": f(inputs["dt_bias"])[0], "a_log": f(inputs["a_log"])[0], "d_skip": f(inputs["d_skip"])[0],
        "attn_g": f(inputs["attn_norm_g"])[0], "ssm_g": f(inputs["ssm_norm_g"])[0], "w_out": f(inputs["w_out"])[0],
        "ln1_g": f(inputs["ln1_g"])[0], "ln1_b": f(inputs["ln1_b"])[0], "w_gate": f(inputs["w_gate"])[0],
        "w_up": f(inputs["w_up"])[0], "w_down": f(inputs["w_down"])[0], "ln2_g": f(inputs["ln2_g"])[0], "ln2_b": f(inputs["ln2_b"])[0],
    }
    maps = []
    for c in range(8):
        b, hf = c // 2, c % 2
        m = dict(common)
        m["xo"] = np.ascontiguousarray(xpr[b, hf * T:(hf + 1) * T])
        m["xp"] = np.ascontiguousarray(xpr[b, 0:T]) if hf == 1 else np.zeros((T, D), np.float32)
        m["flag"] = np.full((128, 1), float(hf), np.float32)
        m["xs"] = np.ascontiguousarray(xsm[4 * c:4 * c + 4].reshape(32, D))
        m["ck"] = np.ascontiguousarray(ckw[4 * c:4 * c + 4].reshape(4, 2048, D))
        m["cv"] = np.ascontiguousarray(cvw[4 * c:4 * c + 4].reshape(4, 2048, D))
        m["sssm"] = np.ascontiguousarray(sss[4 * c:4 * c + 4].reshape(4, 1024, 128))
        m["sconv"] = np.ascontiguousarray(scv[4 * c:4 * c + 4].reshape(12, 2048))
        maps.append(m)
    return maps


def kernel(**inputs):
    nc = build()
    maps = make_in_maps(inputs)
    res = run_bass_kernel_spmd(nc, maps, core_ids=list(range(8)))
    R = res.results
    y_p = np.zeros((4, 4096, D), np.float32)
    kp = np.zeros((1, 4, 2048, 16, 64), np.float32); vp = np.zeros_like(kp)
    sp_ = np.zeros((1, 4, 16, 64, 128), np.float32); cp = np.zeros((1, 4, 3, 2048), np.float32)
    y_s = np.zeros((32, 8, D), np.float32)
    kn = np.zeros((1, 32, 8, 16, 64), np.float32); vn = np.zeros_like(kn)
    ss = np.zeros((1, 32, 16, 64, 128), np.float32); cs = np.zeros((1, 32, 3, 2048), np.float32)
    for c in range(8):
        b, hf = c // 2, c % 2
        r = R[c]
        y_p[b, hf * T:(hf + 1) * T] = r["y_o"]
        if hf == 1:
            kp[0, b] = r["k_o"].reshape(2048, 16, 64); vp[0, b] = r["v_o"].reshape(2048, 16, 64)
            sp_[0, b] = r["ssm_o"].reshape(16, 64, 128); cp[0, b] = r["conv_o"]
        y_s[4 * c:4 * c + 4] = r["y_s"].reshape(4, 8, D)
        kn[0, 4 * c:4 * c + 4] = r["kn_s"].reshape(4, 8, 16, 64); vn[0, 4 * c:4 * c + 4] = r["vn_s"].reshape(4, 8, 16, 64)
        ss[0, 4 * c:4 * c + 4] = r["ssm_s"].reshape(4, 16, 64, 128); cs[0, 4 * c:4 * c + 4] = r["conv_s"].reshape(4, 3, 2048)
    return (y_p, y_s, kp, vp, sp_, cp, kn, vn, ss, cs)
```

```python
import numpy as np
from contextlib import ExitStack
import concourse.bass as bass
import concourse.mybir as mybir
from concourse.bass_utils import run_bass_kernel_spmd

F32 = mybir.dt.float32
BF16 = mybir.dt.bfloat16
AF = mybir.ActivationFunctionType
ALU = mybir.AluOpType

D = 1024
T = 2048
NB = T // 128
DIN = 6160
DFF = 2816
NF = DFF // 128
ALPHA = 2.0 ** 0.25
LN_EPS = 1e-5
RMS_EPS = 1e-5
COL_Q, COL_K, COL_V, COL_Z, COL_XBC, COL_DT = 0, 1024, 2048, 3072, 4096, 6144
NPAIR = 8
SKIP_SSD = False


class Prog:
    NDMA = 24

    def __init__(self, nc, es):
        self.nc = nc
        self.eng = {"pe": nc.tensor, "act": nc.scalar, "dve": nc.vector, "pool": nc.gpsimd, "sp": nc.sync}
        self.ops = {e: [] for e in self.eng}
        self.sem = {e: es.enter_context(nc.semaphore("s_" + e)) for e in ("pe", "act", "dve", "pool")}
        self.cnt = {e: 0 for e in self.sem}
        self.seen = {e: {} for e in self.eng}
        self.lastw = {}
        self.readers = {}
        self.dsem = {q: [es.enter_context(nc.semaphore(f"d_{q}{i}")) for i in range(self.NDMA)] for q in ("sp", "pool")}
        self.dcnt = {q: [0] * self.NDMA for q in ("sp", "pool")}
        self.drr = {q: 0 for q in ("sp", "pool")}
        self.semobj = {}
        for e, s in self.sem.items():
            self.semobj[("c", e)] = s
        for q in self.dsem:
            for i, s in enumerate(self.dsem[q]):
                self.semobj[("d", q, i)] = s
        self.nops = 0

    def _deps(self, e, reads, writes):
        deps = {}

        def add(d):
            if d is None:
                return
            k, v = d
            if k == ("c", "pe") and e == "pe":
                return
            if deps.get(k, 0) < v:
                deps[k] = v

        for r in reads:
            add(self.lastw.get(r))
        for w in writes:
            add(self.lastw.get(w))
            for k, v in self.readers.get(w, {}).items():
                add((k, v))
        out = []
        for k, v in deps.items():
            if self.seen[e].get(k, 0) < v:
                self.seen[e][k] = v
                out.append((k, v))
        return out

    def _commit(self, dep, reads, writes):
        for w in writes:
            self.lastw[w] = dep
            self.readers[w] = {}
        for r in reads:
            if r in writes:
                continue
            d = self.readers.setdefault(r, {})
            if d.get(dep[0], 0) < dep[1]:
                d[dep[0]] = dep[1]

    def op(self, e, fn, reads=(), writes=(), inc=True):
        psr = [r for r in reads if isinstance(r, tuple) and r[0] == "ps"]
        if psr:
            reads = [r for r in reads if r not in psr]
            writes = list(writes) + psr
        waits = self._deps(e, reads, writes)
        if inc:
            self.cnt[e] += 1
            dep = (("c", e), self.cnt[e])
        else:
            assert e == "pe"
            dep = (("c", e), self.cnt[e] + 1)
        self.ops[e].append((waits, fn, (("c", e), 1) if inc else None))
        self._commit(dep, reads, writes)
        self.nops += 1

    def dma(self, q, out, in_, reads=(), writes=(), **kw):
        i = self.drr[q]
        self.drr[q] = (i + 1) % self.NDMA
        key = ("d", q, i)
        waits = self._deps(q, reads, writes)
        prev = self.dcnt[q][i]
        if prev and self.seen[q].get(key, 0) < prev:
            self.seen[q][key] = prev
            waits.append((key, prev))
        self.dcnt[q][i] += 16
        dep = (key, self.dcnt[q][i])

        def fn(eng, out=out, in_=in_, kw=kw):
            return eng.dma_start(out=out, in_=in_, **kw)

        self.ops[q].append((waits, fn, (key, 16)))
        self._commit(dep, reads, writes)
        self.nops += 1

    def flush(self):
        fin = []
        for q in self.dsem:
            for i in range(self.NDMA):
                if self.dcnt[q][i]:
                    fin.append((("d", q, i), self.dcnt[q][i]))
        for e in self.sem:
            if self.cnt[e]:
                fin.append((("c", e), self.cnt[e]))
        for e in self.eng:
            w = [(k, v) for k, v in fin if self.seen[e].get(k, 0) < v]
            for k, v in w:
                self.seen[e][k] = v
            self.ops[e].append((w, None, None))
        ops = self.ops
        self.ops = {e: [] for e in self.eng}
        semobj = self.semobj
        with self.nc.Block() as block:
            def mk(e):
                def body(eng):
                    for waits, fn, inc in ops[e]:
                        for k, v in waits:
                            eng.wait_ge(semobj[k], v)
                        if fn is None:
                            continue
                        ins = fn(eng)
                        if inc is not None:
                            ins.then_inc(semobj[inc[0]], inc[1])
                return body
            block.sync(mk("sp"))
            block.tensor(mk("pe"))
            block.scalar(mk("act"))
            block.vector(mk("dve"))
            block.gpsimd(mk("pool"))


class _Stop(Exception):
    pass


STOP = None


def build(dbg=None):
    try:
        return _build(dbg)
    except _Stop as ex:
        return ex.args[0]


def _build(dbg=None):
    dbg = dbg or {}
    nc = bass.Bass("TRN2", target_bir_lowering=False)

    def din(name, shape):
        return nc.dram_tensor(name, list(shape), F32, kind="ExternalInput").ap()

    def dout(name, shape):
        return nc.dram_tensor(name, list(shape), F32, kind="ExternalOutput").ap()

    xo = din("xo", [T, D]); xp = din("xp", [T, D]); flag_d = din("flag", [128, 1])
    xs_d = din("xs", [32, D])
    ck = din("ck", [4, 2048, D]); cv = din("cv", [4, 2048, D])
    sssm = din("sssm", [4, 1024, 128]); sconv = din("sconv", [12, 2048])
    w_in = din("w_in", [D, DIN]); conv_w = din("conv_w", [4, 2048]); conv_b = din("conv_b", [2048])
    dt_bias = din("dt_bias", [16]); a_log = din("a_log", [16]); d_skip = din("d_skip", [16])
    attn_g = din("attn_g", [1024]); ssm_g = din("ssm_g", [1024])
    w_out = din("w_out", [2048, D]); ln1_g = din("ln1_g", [D]); ln1_b = din("ln1_b", [D])
    w_gate = din("w_gate", [D, DFF]); w_up = din("w_up", [D, DFF]); w_down = din("w_down", [DFF, D])
    ln2_g = din("ln2_g", [D]); ln2_b = din("ln2_b", [D])

    y_o = dout("y_o", [T, D]); y_s = dout("y_s", [32, D])
    k_o = dout("k_o", [T, D]); v_o = dout("v_o", [T, D])
    ssm_o = dout("ssm_o", [1024, 128]); conv_o = dout("conv_o", [3, 2048])
    kn_s = dout("kn_s", [32, D]); vn_s = dout("vn_s", [32, D])
    ssm_s = dout("ssm_s", [4, 1024, 128]); conv_s = dout("conv_s", [12, 2048])
    scr_cs = nc.dram_tensor("scr_cs", [512, 128], F32, kind="Internal").ap()
    dbg_out = {k: dout(k, shp) for k, shp in dbg.items()}

    w_in_v = w_in.rearrange("(kc p) n -> p kc n", p=128)

    with ExitStack() as es:
        P = Prog(nc, es)

        def stop_at(name):
            if STOP == name:
                P.flush()
                raise _Stop(nc)

        def sb(st, name, shape, dt, side=None):
            if side is None:
                return st.enter_context(nc.sbuf_tensor("t_" + name, list(shape), dt))
            return st.enter_context(nc.sbuf_tensor("t_" + name, list(shape), dt, side=side))

        ps = [es.enter_context(nc.psum_tensor(f"ps{i}", [128, 512], F32)) for i in range(8)]
        psb = [p[:, :].bitcast(BF16) for p in ps]

        def PSK(i):
            return ("ps", i)

        ident = sb(es, "ident", [128, 128], F32)
        identb = sb(es, "identb", [128, 128], BF16)
        U = sb(es, "U", [128, 128], F32)
        Ub = sb(es, "Ub", [128, 128], BF16)
        onesf = sb(es, "onesf", [128, 128], F32)
        flag = sb(es, "flag", [128, 1], F32)
        P.op("pool", lambda e: e.memset(ident[:], 0.0), writes=["ident"])
        P.op("pool", lambda e: e.affine_select(out=ident[:], in_=ident[:], pattern=[[-1, 128]], compare_op=ALU.not_equal,
                                               fill=1.0, base=0, channel_multiplier=1), reads=["ident"], writes=["ident"])
        P.op("dve", lambda e: e.tensor_copy(out=identb[:], in_=ident[:]), reads=["ident"], writes=["identb"])
        P.op("pool", lambda e: e.memset(onesf[:], 1.0), writes=["onesf"])
        P.op("pool", lambda e: e.memset(U[:], 1.0), writes=["U"])
        P.op("pool", lambda e: e.affine_select(out=U[:], in_=U[:], pattern=[[1, 128]], compare_op=ALU.is_ge,
                                               fill=0.0, base=0, channel_multiplier=-1), reads=["U"], writes=["U"])
        P.op("dve", lambda e: e.tensor_copy(out=Ub[:], in_=U[:]), reads=["U"], writes=["Ub"])
        P.dma("sp", flag[:], flag_d, writes=["flag"])

        xsT = sb(es, "xsT", [128, 8, 32], BF16)
        smixT_a = sb(es, "smixT_a", [128, 8, 32], BF16)
        smixT_s = sb(es, "smixT_s", [128, 8, 32], BF16)
        sssq_a = sb(es, "sssq_a", [32, 1], F32)
        sssq_s = sb(es, "sssq_s", [32, 4], F32)
        U32 = sb(es, "U32", [32, 4, 8], F32)
        B32 = sb(es, "B32", [32, 4, 8], F32)
        seqm = sb(es, "seqm", [32, 4], F32)
        sel7 = sb(es, "sel7", [32, 4, 128], F32)
        P.op("pool", lambda e: e.memset(B32[:], 1.0), writes=["B32"])
        P.op("pool", lambda e: e.affine_select(out=B32[:], in_=B32[:], pattern=[[8, 4], [0, 8]], compare_op=ALU.is_ge, fill=0.0, base=7, channel_multiplier=-1),
             reads=["B32"], writes=["B32"])
        P.op("pool", lambda e: e.affine_select(out=B32[:], in_=B32[:], pattern=[[-8, 4], [0, 8]], compare_op=ALU.is_ge, fill=0.0, base=0, channel_multiplier=1),
             reads=["B32"], writes=["B32"])
        P.op("pool", lambda e: e.affine_select(out=U32[:], in_=B32[:], pattern=[[8, 4], [1, 8]], compare_op=ALU.is_ge, fill=0.0, base=0, channel_multiplier=-1),
             reads=["B32"], writes=["U32"])
        P.op("pool", lambda e: e.memset(seqm[:], 1.0), writes=["seqm"])
        P.op("pool", lambda e: e.affine_select(out=seqm[:], in_=seqm[:], pattern=[[8, 4]], compare_op=ALU.is_ge, fill=0.0, base=7, channel_multiplier=-1),
             reads=["seqm"], writes=["seqm"])
        P.op("pool", lambda e: e.affine_select(out=seqm[:], in_=seqm[:], pattern=[[-8, 4]], compare_op=ALU.is_ge, fill=0.0, base=0, channel_multiplier=1),
             reads=["seqm"], writes=["seqm"])
        P.op("pool", lambda e: e.memset(sel7[:], 1.0), writes=["sel7"])
        P.op("pool", lambda e: e.affine_select(out=sel7[:], in_=sel7[:], pattern=[[-8, 4], [0, 128]], compare_op=ALU.is_equal, fill=0.0, base=-7, channel_multiplier=1),
             reads=["sel7"], writes=["sel7"])
        U32f = U32[:].rearrange("p s t -> p (s t)")
        B32f = B32[:].rearrange("p s t -> p (s t)")

        xsc = ExitStack()
        xT_o = sb(xsc, "xT_o", [128, 8, T], BF16)
        xT_p = sb(xsc, "xT_p", [128, 8, T], BF16)
        msc = ExitStack()
        mixT_s = sb(msc, "mixT_s", [128, 8, T], BF16, side="right")
        ssq_s = sb(msc, "ssq_s", [128, NB, 4], F32, side="right")


        with ExitStack() as ph:
            xin = [sb(ph, f"xin{i}", [128, D], F32) for i in range(2)]
            n = 0
            for src, dst, dkey in ((xp, xT_p, "xT_p"), (xo, xT_o, "xT_o")):
                for blk in range(NB):
                    xi = xin[n % 2]; xk = ("xin", n % 2)
                    pa, pb = (0, 1) if n % 2 == 0 else (2, 3)
                    P.dma("sp", xi[:], src[blk * 128:(blk + 1) * 128, :], writes=[xk])
                    for kc in range(8):
                        pt = ps[pa] if kc < 4 else ps[pb]
                        kk = PSK(pa) if kc < 4 else PSK(pb)
                        c4 = kc % 4
                        P.op("pe", lambda e, pt=pt, c4=c4, xi=xi, kc=kc: e.transpose(
                            out=pt[:, c4 * 128:(c4 + 1) * 128], in_=xi[:, kc * 128:(kc + 1) * 128], identity=ident[:]),
                            reads=[xk, "ident"], writes=[kk], inc=(c4 == 3))
                    P.op("act", lambda e, dst=dst, blk=blk, pa=pa: e.activation(
                        out=dst[:, 0:4, blk * 128:(blk + 1) * 128], in_=ps[pa][:, :].rearrange("p (c t) -> p c t", c=4), func=AF.Copy),
                        reads=[PSK(pa)], writes=[(dkey, blk)])
                    P.op("dve", lambda e, dst=dst, blk=blk, pb=pb: e.tensor_copy(
                        out=dst[:, 4:8, blk * 128:(blk + 1) * 128], in_=ps[pb][:, :].rearrange("p (c t) -> p c t", c=4)),
                        reads=[PSK(pb)], writes=[(dkey, blk, 1)])
                    n += 1
            P.dma("sp", xin[0][0:32, :], xs_d, writes=[("xin", 0)])
            for kc in range(8):
                P.op("pe", lambda e, kc=kc: e.transpose(out=ps[4][:, kc * 32:(kc + 1) * 32], in_=xin[0][0:32, kc * 128:(kc + 1) * 128], identity=ident[0:32, 0:32]),
                     reads=[("xin", 0), "ident"], writes=[PSK(4)], inc=(kc == 7))
            P.op("act", lambda e: e.activation(out=xsT[:], in_=ps[4][:, 0:256].rearrange("p (c t) -> p c t", c=8), func=AF.Copy), reads=[PSK(4)], writes=["xsT"])
            P.flush()
            stop_at("p0")

        def xkeys(dkey, b0, b1):
            r = []
            for b in range(b0, b1):
                r += [(dkey, b), (dkey, b, 1)]
            return r

        with ExitStack() as ph:
            cw = sb(ph, "cw", [128, 4, 16], F32)
            cb = sb(ph, "cb", [128, 16], F32)
            diagw = sb(ph, "diagw", [128, 4, 4, 128], BF16)
            a_bc = sb(ph, "a_bc", [128, 16], F32)
            dtb_bc = sb(ph, "dtb_bc", [128, 16], F32)
            dsk_bc = sb(ph, "dsk_bc", [128, 16], F32)
            wdt = sb(ph, "wdt", [128, 8, 16], BF16)
            dt_tok = sb(ph, "dt_tok", [128, 32, 16], F32)
            cs_tok = sb(ph, "cs_tok", [128, 32, 16], F32)
            dtte = sb(ph, "dtte", [128, 32, 16], F32)
            ecs = sb(ph, "ecs", [128, 32, 16], F32)
            dec = sb(ph, "dec", [128, 32, 16], F32)
            csT = sb(ph, "csT", [128, 4, 128], F32)
            ncs_tok = sb(ph, "ncs_tok", [128, 32, 16], F32)
            sdt = sb(ph, "sdt", [32, 16], F32)
            sncs = sb(ph, "sncs", [32, 16], F32)
            secs = sb(ph, "secs", [32, 16], F32)
            sdtte = sb(ph, "sdtte", [32, 16], F32)
            scs_bc = sb(ph, "scs_bc", [32, 16, 32], F32)
            sdec = sb(ph, "sdec", [128, 4, 16], F32)
            pro = ExitStack()
            sla = sb(pro, "sla", [32, 16], F32)
            scs = sb(pro, "scs", [32, 16], F32)
            scsl = sb(pro, "scsl", [32, 16], F32)
            stmp = sb(pro, "stmp", [32, 16], F32)
            srhs = sb(pro, "srhs", [32, 16, 32], F32)
            la_tok = sb(pro, "la_tok", [128, 32, 16], F32)
            csl_bc = sb(pro, "csl_bc", [128, 32, 16], F32)
            tmpA = sb(pro, "tmpA", [128, 32, 16], F32)

            for tap in range(4):
                P.dma("sp", cw[:, tap, :], conv_w[tap].rearrange("(c p) -> p c", p=128), writes=["cw"], allow_slow_non_contiguous=True)
            P.dma("sp", cb[:], conv_b.rearrange("(c p) -> p c", p=128), writes=["cb"], allow_slow_non_contiguous=True)
            P.dma("sp", a_bc[:], a_log.partition_broadcast(128), writes=["a_bc"])
            P.dma("sp", dtb_bc[:], dt_bias.partition_broadcast(128), writes=["dtb_bc"])
            P.dma("sp", dsk_bc[:], d_skip.partition_broadcast(128), writes=["dsk_bc"])
            P.dma("pool", wdt[:], w_in_v[:, :, COL_DT:COL_DT + 16], writes=["wdt"])
            P.op("act", lambda e: e.activation(out=a_bc[:], in_=a_bc[:], func=AF.Exp), reads=["a_bc"], writes=["a_bc"])
            P.op("dve", lambda e: e.tensor_scalar(out=a_bc[:], in0=a_bc[:], scalar1=-1.0, scalar2=None, op0=ALU.mult),
                 reads=["a_bc"], writes=["a_bc"])
            for cidx in range(32):
                src, dkey = (xT_p, "xT_p") if cidx < 16 else (xT_o, "xT_o")
                blk = cidx % 16
                for kc in range(8):
                    P.op("pe", lambda e, src=src, blk=blk, kc=kc, cidx=cidx: e.matmul(
                        ps[0][:, cidx * 16:(cidx + 1) * 16], lhsT=src[:, kc, blk * 128:(blk + 1) * 128], rhs=wdt[:, kc, :],
                        start=(kc == 0), stop=(kc == 7), skip_group_check=True),
                        reads=xkeys(dkey, blk, blk + 1) + ["wdt"], writes=[PSK(0)], inc=(kc == 7 and cidx == 31))
            ps0v = ps[0][:, :].rearrange("p (c h) -> p c h", h=16)
            P.op("dve", lambda e: e.tensor_tensor(out=tmpA[:], in0=ps0v, in1=dtb_bc[:].unsqueeze(1).broadcast_to([128, 32, 16]), op=ALU.add),
                 reads=[PSK(0), "dtb_bc"], writes=["tmpA"])
            P.op("act", lambda e: e.activation(out=tmpA[:], in_=tmpA[:], func=AF.Exp), reads=["tmpA"], writes=["tmpA"])
            P.op("act", lambda e: e.activation(out=dt_tok[:], in_=tmpA[:], func=AF.Ln, bias=1.0), reads=["tmpA"], writes=["dt_tok"])
            P.op("dve", lambda e: e.tensor_tensor(out=la_tok[:], in0=dt_tok[:], in1=a_bc[:].unsqueeze(1).broadcast_to([128, 32, 16]), op=ALU.mult),
                 reads=["dt_tok", "a_bc"], writes=["la_tok"])
            la2 = la_tok[:].rearrange("p c h -> p (c h)")
            P.op("pe", lambda e: e.matmul(ps[1][:, :], lhsT=U[:], rhs=la2, start=True, stop=True), reads=["U", "la_tok"], writes=[PSK(1)])
            P.op("pe", lambda e: e.matmul(ps[2][:, :], lhsT=onesf[:], rhs=la2, start=True, stop=True), reads=["onesf", "la_tok"], writes=[PSK(2)])
            P.op("act", lambda e: e.activation(out=cs_tok[:].rearrange("p c h -> p (c h)"), in_=ps[1][:, :], func=AF.Copy),
                 reads=[PSK(1)], writes=["cs_tok"])
            P.op("dve", lambda e: e.tensor_copy(out=csl_bc[:].rearrange("p c h -> p (c h)"), in_=ps[2][:, :]), reads=[PSK(2)], writes=["csl_bc"])
            P.op("dve", lambda e: e.tensor_scalar(out=ncs_tok[:], in0=cs_tok[:], scalar1=-1.0, scalar2=None, op0=ALU.mult), reads=["cs_tok"], writes=["ncs_tok"])
            P.op("act", lambda e: e.activation(out=ecs[:], in_=cs_tok[:], func=AF.Exp), reads=["cs_tok"], writes=["ecs"])
            P.op("act", lambda e: e.activation(out=dec[:], in_=csl_bc[:], func=AF.Exp), reads=["csl_bc"], writes=["dec"])
            P.op("dve", lambda e: e.tensor_tensor(out=tmpA[:], in0=csl_bc[:], in1=cs_tok[:], op=ALU.subtract),
                 reads=["csl_bc", "cs_tok"], writes=["tmpA"])
            P.op("act", lambda e: e.activation(out=tmpA[:], in_=tmpA[:], func=AF.Exp), reads=["tmpA"], writes=["tmpA"])
            P.op("dve", lambda e: e.tensor_tensor(out=dtte[:], in0=tmpA[:], in1=dt_tok[:], op=ALU.mult), reads=["tmpA", "dt_tok"], writes=["dtte"])
            for q in range(4):
                P.op("pe", lambda e, q=q: e.transpose(out=ps[3][:, q * 128:(q + 1) * 128],
                                                     in_=cs_tok[:].rearrange("p c h -> p (c h)")[:, q * 128:(q + 1) * 128], identity=ident[:]),
                     reads=["cs_tok", "ident"], writes=[PSK(3)], inc=(q == 3))
            P.op("dve", lambda e: e.tensor_copy(out=csT[:].rearrange("p q l -> p (q l)"), in_=ps[3][:, :]), reads=[PSK(3)], writes=["csT"])
            P.dma("sp", scr_cs.rearrange("(q p) l -> p q l", p=128), csT[:], reads=["csT"], writes=["scr_cs"])

            for kc in range(8):
                P.op("pe", lambda e, kc=kc: e.matmul(ps[4][0:32, 0:16], lhsT=xsT[:, kc, :], rhs=wdt[:, kc, :], start=(kc == 0), stop=(kc == 7)),
                     reads=["xsT", "wdt"], writes=[PSK(4)], inc=(kc == 7))
            P.op("dve", lambda e: e.tensor_tensor(out=stmp[:], in0=ps[4][0:32, 0:16], in1=dtb_bc[0:32, :], op=ALU.add), reads=[PSK(4), "dtb_bc"], writes=["stmp"])
            P.op("act", lambda e: e.activation(out=stmp[:], in_=stmp[:], func=AF.Exp), reads=["stmp"], writes=["stmp"])
            P.op("act", lambda e: e.activation(out=sdt[:], in_=stmp[:], func=AF.Ln, bias=1.0), reads=["stmp"], writes=["sdt"])
            P.op("dve", lambda e: e.tensor_tensor(out=sla[:], in0=sdt[:], in1=a_bc[0:32, :], op=ALU.mult), reads=["sdt", "a_bc"], writes=["sla"])
            P.op("pe", lambda e: e.matmul(ps[5][0:32, 0:16], lhsT=U32f, rhs=sla[:], start=True, stop=True), reads=["U32", "sla"], writes=[PSK(5)])
            P.op("pe", lambda e: e.matmul(ps[6][0:32, 0:16], lhsT=B32f, rhs=sla[:], start=True, stop=True), reads=["B32", "sla"], writes=[PSK(6)])
            P.op("act", lambda e: e.activation(out=scs[:], in_=ps[5][0:32, 0:16], func=AF.Copy), reads=[PSK(5)], writes=["scs"])
            P.op("dve", lambda e: e.tensor_copy(out=scsl[:], in_=ps[6][0:32, 0:16]), reads=[PSK(6)], writes=["scsl"])
            P.op("dve", lambda e: e.tensor_scalar(out=sncs[:], in0=scs[:], scalar1=-1.0, scalar2=None, op0=ALU.mult), reads=["scs"], writes=["sncs"])
            P.op("act", lambda e: e.activation(out=secs[:], in_=scs[:], func=AF.Exp), reads=["scs"], writes=["secs"])
            P.op("dve", lambda e: e.tensor_tensor(out=stmp[:], in0=scsl[:], in1=scs[:], op=ALU.subtract), reads=["scsl", "scs"], writes=["stmp"])
            P.op("act", lambda e: e.activation(out=stmp[:], in_=stmp[:], func=AF.Exp), reads=["stmp"], writes=["stmp"])
            P.op("dve", lambda e: e.tensor_tensor(out=sdtte[:], in0=stmp[:], in1=sdt[:], op=ALU.mult), reads=["stmp", "sdt"], writes=["sdtte"])
            P.op("dve", lambda e: e.tensor_tensor(out=srhs[:], in0=sla[:].unsqueeze(2).broadcast_to([32, 16, 32]), in1=U32f.unsqueeze(1).broadcast_to([32, 16, 32]), op=ALU.mult),
                 reads=["sla", "U32"], writes=["srhs"])
            P.op("pe", lambda e: e.matmul(ps[7][0:32, :], lhsT=onesf[0:32, 0:32], rhs=srhs[:].rearrange("p h t -> p (h t)"), start=True, stop=True),
                 reads=["onesf", "srhs"], writes=[PSK(7)])
            P.op("act", lambda e: e.activation(out=scs_bc[:].rearrange("p h t -> p (h t)"), in_=ps[7][0:32, :], func=AF.Copy), reads=[PSK(7)], writes=["scs_bc"])
            for sq_ in range(4):
                P.op("pe", lambda e, sq_=sq_: e.matmul(ps[4][:, 64 + sq_ * 16:64 + (sq_ + 1) * 16], lhsT=sel7[:, sq_, :], rhs=scsl[:], start=True, stop=True),
                     reads=["sel7", "scsl"], writes=[PSK(4)], inc=(sq_ == 3))
            P.op("act", lambda e: e.activation(out=sdec[:].rearrange("p s h -> p (s h)"), in_=ps[4][:, 64:128], func=AF.Exp), reads=[PSK(4)], writes=["sdec"])

            if "dt_tok" in dbg:
                P.dma("sp", dbg_out["dt_tok"], dt_tok[:].rearrange("p c h -> p (c h)"), reads=["dt_tok"])
            if "cs_tok" in dbg:
                P.dma("sp", dbg_out["cs_tok"], cs_tok[:].rearrange("p c h -> p (c h)"), reads=["cs_tok"])

            P.flush()
            pro.close()
            stop_at("pro")
            Wg = sb(ph, "Wg", [128, 8, 4, 128], BF16)
            Wz = sb(ph, "Wz", [128, 8, 256], BF16)
            raw = sb(ph, "raw", [128, 4, 3 + T], BF16)
            xbc = sb(ph, "xbc", [128, 4, T], BF16)
            zs = sb(ph, "zs", [128, NB, 256], BF16)
            hst = sb(ph, "hst", [128, 256], F32)
            hb = sb(ph, "hb", [128, 256], BF16)
            Btok = [sb(ph, f"Btok{i}", [128, 128], BF16) for i in range(2)]
            xdt = [sb(ph, f"xdt{i}", [128, 256], BF16) for i in range(2)]
            xdte = [sb(ph, f"xdte{i}", [128, 256], BF16) for i in range(2)]
            xD = [sb(ph, f"xD{i}", [128, 256], BF16) for i in range(2)]
            csbc = [sb(ph, f"csbc{i}", [128, 4, 128], F32) for i in range(2)]
            dd = [sb(ph, "dd0", [128, 4, 128], F32)] * 2
            ee = [sb(ph, "ee0", [128, 4, 128], F32)] * 2
            CBm = [sb(ph, f"CBm{i}", [128, 128], BF16) for i in range(2)]
            MT = [sb(ph, f"MT{i}", [128, 4, 128], BF16) for i in range(2)]
            t1 = [sb(ph, f"t1{i}", [128, 256], F32) for i in range(2)]
            yg = [sb(ph, f"yg{i}", [128, 256], F32) for i in range(2)]
            ygb = [sb(ph, f"ygb{i}", [128, 256], BF16) for i in range(2)]
            junk = sb(ph, "junk", [128, 256], F32)
            cvo = sb(ph, "cvo", [128, 512], F32)
            hT = sb(ph, "hT", [128, 2, 128], F32)
            sraw = sb(ph, "sraw", [128, 4, 4, 11], BF16)
            sxbc = sb(ph, "sxbc", [128, 4, 32], BF16)
            scv = sb(ph, "scv", [12, 4, 128], F32)
            szs = sb(ph, "szs", [32, 256], F32)
            sCTm = sb(ph, "sCTm", [128, 4, 32], BF16)
            sxm = sb(ph, "sxm", [32, 256], BF16)

            def ssd_group(g):
                chan = [2 * g, 2 * g + 1, 8 + g, 12 + g]
                for i, c in enumerate(chan):
                    P.dma("pool", Wg[:, :, i, :], w_in_v[:, :, COL_XBC + c * 128:COL_XBC + (c + 1) * 128], writes=[("Wg", i)])
                P.dma("pool", Wz[:], w_in_v[:, :, COL_Z + 256 * g:COL_Z + 256 * (g + 1)], writes=["Wz"])
                for i, c in enumerate(chan):
                    for tap in range(4):
                        P.op("dve", lambda e, i=i, c=c, tap=tap: e.tensor_scalar(
                            out=diagw[:, i, tap, :], in0=ident[:], scalar1=cw[:, tap, c:c + 1], scalar2=None, op0=ALU.mult),
                            reads=["ident", "cw"], writes=[("diagw", i)])
                P.op("pool", lambda e: e.memset(raw[:, :, 0:3], 0.0), writes=["rawpre"])
                P.op("pool", lambda e: e.memset(hst[:], 0.0), writes=["hst"])
                P.op("pool", lambda e: e.memset(hb[:], 0.0), writes=["hb"])
                pr = [0]

                def ssd_pass(pas):
                    src, dkey = (xT_p, "xT_p") if pas == 0 else (xT_o, "xT_o")
                    nch = 3 if pas == 0 else 4
                    if pas == 1:
                        P.op("pool", lambda e: e.tensor_copy(out=raw[:, :, 0:3], in_=raw[:, :, T:T + 3]),
                             reads=[("raw", i, 3) for i in range(4)], writes=["rawpre"])
                        for kc in range(8):
                            P.op("pe", lambda e, kc=kc: e.matmul(ps[4][0:3, :], lhsT=xT_o[:, kc, T - 3:T],
                                                                 rhs=Wg[:, kc, :, :].rearrange("p i n -> p (i n)"),
                                                                 start=(kc == 0), stop=(kc == 7)),
                                 reads=xkeys("xT_o", 15, 16) + [("Wg", i) for i in range(4)], writes=[PSK(4)], inc=(kc == 7))
                        P.op("act", lambda e: e.activation(out=cvo[0:3, :], in_=ps[4][0:3, :], func=AF.Copy), reads=[PSK(4)], writes=["cvo"])
                        for i, c in enumerate(chan):
                            P.dma("sp", conv_o[:, c * 128:(c + 1) * 128], cvo[0:3, i * 128:(i + 1) * 128], reads=["cvo"])
                    for i in range(4):
                        for j in range(4):
                            if pas == 0 and i == 3 and j != 3:
                                continue
                            pp = pr[0] % 2; pr[0] += 1
                            for kc in range(8):
                                P.op("pe", lambda e, pp=pp, kc=kc, i=i, j=j, src=src: e.matmul(
                                    ps[pp][:, :], lhsT=Wg[:, kc, i, :], rhs=src[:, kc, j * 512:(j + 1) * 512], start=(kc == 0), stop=(kc == 7)),
                                    reads=[("Wg", i)] + xkeys(dkey, 4 * j, 4 * j + 4), writes=[PSK(pp)], inc=(kc == 7))
                            P.op("act", lambda e, pp=pp, i=i, j=j: e.activation(out=raw[:, i, 3 + j * 512:3 + (j + 1) * 512], in_=ps[pp][:, :], func=AF.Copy),
                                 reads=[PSK(pp)], writes=[("raw", i, j)])
                    stop_at(f"g{g}p{pas}proj")
                    for i in range(nch):
                        for j in range(4):
                            pp = pr[0] % 2; pr[0] += 1
                            rk = [("raw", i, j)] + ([("raw", i, j - 1)] if j else ["rawpre"])
                            for tap in range(4):
                                P.op("pe", lambda e, pp=pp, tap=tap, i=i, j=j: e.matmul(
                                    ps[pp][:, :], lhsT=diagw[:, i, tap, :], rhs=raw[:, i, j * 512 + tap:j * 512 + tap + 512],
                                    start=(tap == 0), stop=(tap == 3)),
                                    reads=rk + [("diagw", i)], writes=[PSK(pp)], inc=(tap == 3))
                            P.op("act", lambda e, pp=pp, i=i, j=j: e.activation(out=xbc[:, i, j * 512:(j + 1) * 512], in_=ps[pp][:, :], func=AF.Silu,
                                                                               bias=cb[:, chan[i]:chan[i] + 1]),
                                 reads=[PSK(pp), "cb"], writes=[("xbc", i, j)])
                    stop_at(f"g{g}p{pas}conv")
                    if pas == 1:
                        for c in range(NB):
                            zb = 3 if c % 2 == 0 else 5
                            for kc in range(8):
                                P.op("pe", lambda e, c=c, kc=kc, zb=zb: e.matmul(ps[zb][:, 0:256], lhsT=xT_o[:, kc, c * 128:(c + 1) * 128], rhs=Wz[:, kc, :],
                                                                                start=(kc == 0), stop=(kc == 7)),
                                     reads=xkeys("xT_o", c, c + 1) + ["Wz"], writes=[PSK(zb)], inc=(kc == 7))
                            P.op("act", lambda e, c=c, zb=zb: e.activation(out=zs[:, c, :], in_=ps[zb][:, 0:256], func=AF.Silu), reads=[PSK(zb)], writes=[("zs", c)])
                        P.op("dve", lambda e: e.tensor_scalar(out=hst[:], in0=hst[:], scalar1=flag[:, 0:1], scalar2=None, op0=ALU.mult),
                             reads=["hst", "flag"], writes=["hst"])
                        P.op("act", lambda e: e.activation(out=hb[:], in_=hst[:], func=AF.Copy), reads=["hst"], writes=["hb"])
                    stop_at(f"g{g}p{pas}z")
                    def chunk_A(c):
                        cidx = pas * 16 + c
                        b = c % 2
                        j = c // 4
                        cs_ = slice(c * 128, (c + 1) * 128)
                        hs = slice(4 * g, 4 * g + 4)
                        for i in range(3):
                            P.op("pe", lambda e, i=i, cs_=cs_: e.transpose(out=psb[2][:, i * 128:(i + 1) * 128], in_=xbc[:, i, cs_], identity=identb[:]),
                                 reads=[("xbc", i, j), "identb"], writes=[PSK(2)], inc=(i == 2))
                        xtokv = psb[2][:, 0:256].rearrange("p (h d) -> p h d", h=4)
                        P.op("act", lambda e, b=b: e.activation(out=Btok[b][:], in_=psb[2][:, 256:384], func=AF.Copy), reads=[PSK(2)], writes=[("Btok", b)])
                        P.op("dve", lambda e, b=b, cidx=cidx: e.tensor_tensor(
                            out=xdte[b][:].rearrange("p (h d) -> p h d", h=4), in0=xtokv,
                            in1=dtte[:, cidx, hs].unsqueeze(2).broadcast_to([128, 4, 64]), op=ALU.mult),
                            reads=[PSK(2), "dtte"], writes=[("xdte", b)])
                        stop_at("xA0")
                        if pas == 1:
                            P.op("dve", lambda e, b=b, cidx=cidx: e.tensor_tensor(
                                out=xdt[b][:].rearrange("p (h d) -> p h d", h=4), in0=xtokv,
                                in1=dt_tok[:, cidx, hs].unsqueeze(2).broadcast_to([128, 4, 64]), op=ALU.mult),
                                reads=[PSK(2), "dt_tok"], writes=[("xdt", b)])
                            stop_at("xA1")
                            P.op("dve", lambda e, b=b: e.tensor_tensor(
                                out=xD[b][:].rearrange("p (h d) -> p h d", h=4), in0=xtokv,
                                in1=dsk_bc[:, hs].unsqueeze(2).broadcast_to([128, 4, 64]), op=ALU.mult),
                                reads=[PSK(2), "dsk_bc"], writes=[("xD", b)])
                            stop_at(f"xA")
                            P.dma("sp", csbc[b][:], scr_cs[cidx * 16 + 4 * g:cidx * 16 + 4 * g + 4, :].partition_broadcast(128),
                                  reads=["scr_cs"], writes=[("csbc", b)])
                            for hh in range(4):
                                P.op("pool", lambda e, b=b, hh=hh, cidx=cidx: e.tensor_scalar(
                                    out=dd[b][:, hh, :], in0=csbc[b][:, hh, :], scalar1=1.0, scalar2=ncs_tok[:, cidx, 4 * g + hh:4 * g + hh + 1],
                                    op0=ALU.mult, op1=ALU.add),
                                    reads=[("csbc", b), "ncs_tok"], writes=[("dd", 0, hh)])
                            P.op("pool", lambda e, b=b: e.tensor_scalar(out=dd[b][:], in0=dd[b][:], scalar1=0.0, scalar2=-1.0e30, op0=ALU.min, op1=ALU.max),
                                 reads=[("dd", 0, hh) for hh in range(4)], writes=[("dd", 0, hh) for hh in range(4)])
                            P.op("act", lambda e, b=b: e.activation(out=ee[b][:], in_=dd[b][:], func=AF.Exp),
                                 reads=[("dd", 0, hh) for hh in range(4)], writes=[("ee", 0)])
                            stop_at(f"xB")
                            P.op("pe", lambda e, cs_=cs_: e.matmul(ps[5][:, 0:128], lhsT=xbc[:, 2, cs_], rhs=xbc[:, 3, cs_], start=True, stop=True),
                                 reads=[("xbc", 2, j), ("xbc", 3, j)], writes=[PSK(5)])
                            P.op("dve", lambda e, b=b: e.tensor_tensor(out=CBm[b][:], in0=ps[5][:, 0:128], in1=U[:], op=ALU.mult),
                                 reads=[PSK(5), "U"], writes=[("CBm", b)])
                            stop_at(f"xC")
                            P.op("dve", lambda e, b=b: e.scalar_tensor_tensor(out=MT[b][:], in0=ee[b][:], scalar=1.0,
                                                                            in1=CBm[b][:].unsqueeze(1).broadcast_to([128, 4, 128]), op0=ALU.min, op1=ALU.mult),
                                 reads=[("ee", 0), ("CBm", b)], writes=[("MT", b)])
                            stop_at(f"xD")

                    def chunk_B(c):
                        cidx = pas * 16 + c
                        b = c % 2
                        j = c // 4
                        cs_ = slice(c * 128, (c + 1) * 128)
                        hs = slice(4 * g, 4 * g + 4)
                        if pas == 1:
                            P.op("pe", lambda e, b=b: e.matmul(ps[6][:, 0:256], lhsT=identb[:], rhs=xD[b][:], start=True, stop=False, skip_group_check=True),
                                 reads=["identb", ("xD", b)], writes=[PSK(6)], inc=False)
                            for hh in range(4):
                                P.op("pe", lambda e, b=b, hh=hh: e.matmul(ps[6][:, hh * 64:(hh + 1) * 64], lhsT=MT[b][:, hh, :], rhs=xdt[b][:, hh * 64:(hh + 1) * 64],
                                                                          start=False, stop=(hh == 3), skip_group_check=True),
                                     reads=[("MT", b), ("xdt", b)], writes=[PSK(6)], inc=False)
                            P.op("pe", lambda e, cs_=cs_: e.matmul(ps[6][:, 256:512], lhsT=xbc[:, 3, cs_], rhs=hb[:], start=False, stop=True, skip_group_check=True),
                                 reads=[("xbc", 3, j), "hb"], writes=[PSK(6)])
                        stop_at(f"xE")
                        P.op("pe", lambda e, b=b: e.matmul(ps[4][:, 0:256], lhsT=Btok[b][:], rhs=xdte[b][:], start=True, stop=True),
                             reads=[("Btok", b), ("xdte", b)], writes=[PSK(4)])
                        P.op("pool", lambda e, cidx=cidx: e.tensor_tensor(
                            out=hst[:].rearrange("p (h d) -> p h d", h=4), in0=hst[:].rearrange("p (h d) -> p h d", h=4),
                            in1=dec[:, cidx, hs].unsqueeze(2).broadcast_to([128, 4, 64]), op=ALU.mult),
                            reads=["hst", "dec"], writes=["hst"])
                        P.op("dve", lambda e: e.tensor_tensor(out=hst[:], in0=hst[:], in1=ps[4][:, 0:256], op=ALU.add), reads=["hst", PSK(4)], writes=["hst"])
                        P.op("act", lambda e: e.activation(out=hb[:], in_=hst[:], func=AF.Copy), reads=["hst"], writes=["hb"])
                        if pas == 1:
                            P.op("dve", lambda e, b=b, cidx=cidx: e.tensor_tensor(
                                out=t1[b][:].rearrange("p (h d) -> p h d", h=4), in0=ps[6][:, 256:512].rearrange("p (h d) -> p h d", h=4),
                                in1=ecs[:, cidx, hs].unsqueeze(2).broadcast_to([128, 4, 64]), op=ALU.mult),
                                reads=[PSK(6), "ecs"], writes=[("t1", b)])
                            P.op("dve", lambda e, b=b: e.tensor_tensor(out=t1[b][:], in0=t1[b][:], in1=ps[6][:, 0:256], op=ALU.add),
                                 reads=[("t1", b), PSK(6)], writes=[("t1", b)])
                            stop_at("xF")
                            P.op("pool", lambda e, b=b, c=c: e.tensor_tensor(out=ygb[b][:], in0=t1[b][:], in1=zs[:, c, :], op=ALU.mult),
                                 reads=[("t1", b), ("zs", c)], writes=[("ygb", b)])
                            P.op("act", lambda e, b=b, c=c: e.activation(out=junk[:], in_=ygb[b][:], func=AF.Square, accum_out=ssq_s[:, c, g:g + 1]),
                                 reads=[("ygb", b)], writes=["junk", ("ssq_s", c, g)])
                            for i in range(2):
                                P.op("pe", lambda e, b=b, i=i: e.transpose(out=psb[7][:, i * 128:(i + 1) * 128], in_=ygb[b][:, i * 128:(i + 1) * 128], identity=identb[:]),
                                     reads=[("ygb", b), "identb"], writes=[PSK(7)], inc=(i == 1))
                            P.op("act", lambda e, c=c: e.activation(out=mixT_s[:, 2 * g:2 * g + 2, c * 128:(c + 1) * 128],
                                                                    in_=psb[7][:, 0:256].rearrange("p (i t) -> p i t", i=2), func=AF.Copy),
                                 reads=[PSK(7)], writes=[("mixT_s", g, c)])
                        stop_at(f"g{g}p{pas}c{c}")

                    chunk_A(0)
                    for c in range(NB):
                        if c + 1 < NB:
                            chunk_A(c + 1)
                        chunk_B(c)

                for pas in range(2):
                    ssd_pass(pas)
                stop_at(f"g{g}loop")
                for i in range(2):
                    P.op("pe", lambda e, i=i: e.transpose(out=ps[3][:, i * 128:(i + 1) * 128], in_=hst[:, i * 128:(i + 1) * 128], identity=ident[:]),
                         reads=["hst", "ident"], writes=[PSK(3)], inc=(i == 1))
                P.op("dve", lambda e: e.tensor_copy(out=hT[:].rearrange("p i n -> p (i n)"), in_=ps[3][:, 0:256]), reads=[PSK(3)], writes=["hT"])
                P.dma("sp", ssm_o[256 * g:256 * (g + 1), :].rearrange("(i p) n -> p i n", p=128), hT[:], reads=["hT"])

                hs = slice(4 * g, 4 * g + 4)
                for i, c in enumerate(chan):
                    P.dma("sp", scv[:, i, :], sconv[:, c * 128:(c + 1) * 128], writes=[("scv", i)])
                for i in range(4):
                    P.op("pe", lambda e, i=i: e.transpose(out=ps[0][:, i * 12:(i + 1) * 12], in_=scv[0:12, i, :], identity=ident[0:12, 0:12]),
                         reads=[("scv", i), "ident"], writes=[PSK(0)], inc=(i == 3))
                P.op("act", lambda e: e.activation(out=sraw[:, :, :, 0:3], in_=ps[0][:, 0:48].rearrange("p (i s w) -> p i s w", i=4, s=4), func=AF.Copy),
                     reads=[PSK(0)], writes=["srawpre"])
                for i in range(4):
                    for kc in range(8):
                        P.op("pe", lambda e, i=i, kc=kc: e.matmul(ps[1][:, i * 32:(i + 1) * 32], lhsT=Wg[:, kc, i, :], rhs=xsT[:, kc, :], start=(kc == 0), stop=(kc == 7), skip_group_check=True),
                             reads=[("Wg", i), "xsT"], writes=[PSK(1)], inc=(kc == 7 and i == 3))
                P.op("act", lambda e: e.activation(out=sraw[:, :, :, 3:11], in_=ps[1][:, 0:128].rearrange("p (i s t) -> p i s t", i=4, s=4), func=AF.Copy),
                     reads=[PSK(1)], writes=["sraw"])
                for i in range(4):
                    for tap in range(4):
                        P.op("pe", lambda e, i=i, tap=tap: e.matmul(ps[0][:, 64 + i * 32:64 + (i + 1) * 32], lhsT=diagw[:, i, tap, :], rhs=sraw[:, i, :, tap:tap + 8],
                                                                    start=(tap == 0), stop=(tap == 3), skip_group_check=True),
                             reads=["sraw", "srawpre", ("diagw", i)], writes=[PSK(0)], inc=(tap == 3 and i == 3))
                for i in range(4):
                    P.op("act", lambda e, i=i: e.activation(out=sxbc[:, i, :], in_=ps[0][:, 64 + i * 32:64 + (i + 1) * 32], func=AF.Silu, bias=cb[:, chan[i]:chan[i] + 1]),
                         reads=[PSK(0), "cb"], writes=[("sxbc", i)])
                for kc in range(8):
                    P.op("pe", lambda e, kc=kc: e.matmul(ps[4][0:32, :], lhsT=xsT[:, kc, :], rhs=Wg[:, kc, :, :].rearrange("p i n -> p (i n)"), start=(kc == 0), stop=(kc == 7)),
                         reads=["xsT"] + [("Wg", i) for i in range(4)], writes=[PSK(4)], inc=(kc == 7))
                P.op("act", lambda e: e.activation(out=cvo[0:32, :], in_=ps[4][0:32, :], func=AF.Copy), reads=[PSK(4)], writes=["cvo"])
                for sq_ in range(4):
                    for i, c in enumerate(chan):
                        P.dma("sp", conv_s[sq_ * 3:(sq_ + 1) * 3, c * 128:(c + 1) * 128], cvo[sq_ * 8 + 5:sq_ * 8 + 8, i * 128:(i + 1) * 128], reads=["cvo"])
                for kc in range(8):
                    P.op("pe", lambda e, kc=kc: e.matmul(ps[3][0:32, 0:256], lhsT=xsT[:, kc, :], rhs=Wz[:, kc, :], start=(kc == 0), stop=(kc == 7)),
                         reads=["xsT", "Wz"], writes=[PSK(3)], inc=(kc == 7))
                P.op("act", lambda e: e.activation(out=szs[:], in_=ps[3][0:32, 0:256], func=AF.Silu), reads=[PSK(3)], writes=["szs"])
                for i in range(3):
                    P.op("pe", lambda e, i=i: e.transpose(out=psb[2][0:32, i * 128:(i + 1) * 128], in_=sxbc[:, i, :], identity=identb[:]),
                         reads=[("sxbc", i), "identb"], writes=[PSK(2)], inc=(i == 2))
                sxt = psb[2][0:32, 0:256].rearrange("p (h d) -> p h d", h=4)
                P.op("act", lambda e: e.activation(out=Btok[0][0:32, :], in_=psb[2][0:32, 256:384], func=AF.Copy), reads=[PSK(2)], writes=[("Btok", 0)])
                P.op("dve", lambda e: e.tensor_tensor(out=xdte[0][0:32, :].rearrange("p (h d) -> p h d", h=4), in0=sxt, in1=sdtte[:, hs].unsqueeze(2).broadcast_to([32, 4, 64]), op=ALU.mult),
                     reads=[PSK(2), "sdtte"], writes=[("xdte", 0)])
                P.op("dve", lambda e: e.tensor_tensor(out=xdt[0][0:32, :].rearrange("p (h d) -> p h d", h=4), in0=sxt, in1=sdt[:, hs].unsqueeze(2).broadcast_to([32, 4, 64]), op=ALU.mult),
                     reads=[PSK(2), "sdt"], writes=[("xdt", 0)])
                P.op("dve", lambda e: e.tensor_tensor(out=xD[0][0:32, :].rearrange("p (h d) -> p h d", h=4), in0=sxt, in1=dsk_bc[0:32, hs].unsqueeze(2).broadcast_to([32, 4, 64]), op=ALU.mult),
                     reads=[PSK(2), "dsk_bc"], writes=[("xD", 0)])
                for hh in range(4):
                    P.op("pool", lambda e, hh=hh: e.tensor_scalar(out=dd[0][0:32, hh, 0:32], in0=scs_bc[:, 4 * g + hh, :], scalar1=1.0, scalar2=sncs[:, 4 * g + hh:4 * g + hh + 1],
                                                                  op0=ALU.mult, op1=ALU.add),
                         reads=["scs_bc", "sncs"], writes=[("dd", 0, hh)])
                P.op("pool", lambda e: e.tensor_scalar(out=dd[0][0:32, :, 0:32], in0=dd[0][0:32, :, 0:32], scalar1=0.0, scalar2=-1.0e30, op0=ALU.min, op1=ALU.max),
                     reads=[("dd", 0, hh) for hh in range(4)], writes=[("dd", 0, hh) for hh in range(4)])
                P.op("act", lambda e: e.activation(out=ee[0][0:32, :, 0:32], in_=dd[0][0:32, :, 0:32], func=AF.Exp), reads=[("dd", 0, hh) for hh in range(4)], writes=[("ee", 0)])
                P.op("pe", lambda e: e.matmul(ps[5][0:32, 0:32], lhsT=sxbc[:, 2, :], rhs=sxbc[:, 3, :], start=True, stop=True), reads=[("sxbc", 2), ("sxbc", 3)], writes=[PSK(5)])
                P.op("dve", lambda e: e.tensor_tensor(out=CBm[0][0:32, 0:32], in0=ps[5][0:32, 0:32], in1=U32f, op=ALU.mult), reads=[PSK(5), "U32"], writes=[("CBm", 0)])
                P.op("dve", lambda e: e.scalar_tensor_tensor(out=MT[0][0:32, :, 0:32], in0=ee[0][0:32, :, 0:32], scalar=1.0,
                                                              in1=CBm[0][0:32, 0:32].unsqueeze(1).broadcast_to([32, 4, 32]), op0=ALU.min, op1=ALU.mult),
                     reads=[("ee", 0), ("CBm", 0)], writes=[("MT", 0)])
                P.op("pool", lambda e: e.memset(sCTm[:], 0.0), writes=["sCTm"])
                for sq_ in range(4):
                    P.op("pool", lambda e, sq_=sq_: e.tensor_copy(out=sCTm[:, sq_, sq_ * 8:(sq_ + 1) * 8], in_=sxbc[:, 3, sq_ * 8:(sq_ + 1) * 8]), reads=[("sxbc", 3), "sCTm"], writes=["sCTm"])
                P.op("pe", lambda e: e.matmul(ps[6][0:32, 0:256], lhsT=identb[0:32, 0:32], rhs=xD[0][0:32, :], start=True, stop=False, skip_group_check=True),
                     reads=["identb", ("xD", 0)], writes=[PSK(6)], inc=False)
                for hh in range(4):
                    P.op("pe", lambda e, hh=hh: e.matmul(ps[6][0:32, hh * 64:(hh + 1) * 64], lhsT=MT[0][0:32, hh, 0:32], rhs=xdt[0][0:32, hh * 64:(hh + 1) * 64],
                                                         start=False, stop=(hh == 3), skip_group_check=True),
                         reads=[("MT", 0), ("xdt", 0)], writes=[PSK(6)], inc=False)
                for sq_ in range(4):
                    P.dma("sp", hT[:], sssm[sq_, 256 * g:256 * (g + 1), :].rearrange("(i p) n -> p i n", p=128), writes=["hT"])
                    for i in range(2):
                        P.op("pe", lambda e, i=i: e.transpose(out=ps[3][:, i * 128:(i + 1) * 128], in_=hT[:, i, :], identity=ident[:]),
                             reads=["hT", "ident"], writes=[PSK(3)], inc=(i == 1))
                    P.op("dve", lambda e: e.tensor_copy(out=hst[:], in_=ps[3][:, 0:256]), reads=[PSK(3)], writes=["hst"])
                    P.op("act", lambda e: e.activation(out=hb[:], in_=hst[:], func=AF.Copy), reads=["hst"], writes=["hb"])
                    P.op("pe", lambda e, sq_=sq_: e.matmul(ps[6][0:32, 256:512], lhsT=sCTm[:, sq_, :], rhs=hb[:], start=False, stop=(sq_ == 3), skip_group_check=True),
                         reads=["sCTm", "hb"], writes=[PSK(6)], inc=True)
                    P.op("dve", lambda e, sq_=sq_: e.tensor_scalar(out=sxm[:], in0=xdte[0][0:32, :], scalar1=seqm[:, sq_:sq_ + 1], scalar2=None, op0=ALU.mult),
                         reads=[("xdte", 0), "seqm"], writes=["sxm"])
                    P.op("pe", lambda e: e.matmul(ps[4][:, 0:256], lhsT=Btok[0][0:32, :], rhs=sxm[:], start=True, stop=True), reads=[("Btok", 0), "sxm"], writes=[PSK(4)])
                    P.op("pool", lambda e, sq_=sq_: e.tensor_tensor(out=hst[:].rearrange("p (h d) -> p h d", h=4), in0=hst[:].rearrange("p (h d) -> p h d", h=4),
                                                                    in1=sdec[:, sq_, hs].unsqueeze(2).broadcast_to([128, 4, 64]), op=ALU.mult),
                         reads=["hst", "sdec"], writes=["hst"])
                    P.op("dve", lambda e: e.tensor_tensor(out=hst[:], in0=hst[:], in1=ps[4][:, 0:256], op=ALU.add), reads=["hst", PSK(4)], writes=["hst"])
                    for i in range(2):
                        P.op("pe", lambda e, i=i: e.transpose(out=ps[3][:, i * 128:(i + 1) * 128], in_=hst[:, i * 128:(i + 1) * 128], identity=ident[:]),
                             reads=["hst", "ident"], writes=[PSK(3)], inc=(i == 1))
                    P.op("dve", lambda e: e.tensor_copy(out=hT[:].rearrange("p i n -> p (i n)"), in_=ps[3][:, 0:256]), reads=[PSK(3)], writes=["hT"])
                    P.dma("sp", ssm_s[sq_, 256 * g:256 * (g + 1), :].rearrange("(i p) n -> p i n", p=128), hT[:], reads=["hT"])
                P.op("dve", lambda e: e.tensor_tensor(out=t1[0][0:32, :].rearrange("p (h d) -> p h d", h=4), in0=ps[6][0:32, 256:512].rearrange("p (h d) -> p h d", h=4),
                                                      in1=secs[:, hs].unsqueeze(2).broadcast_to([32, 4, 64]), op=ALU.mult),
                     reads=[PSK(6), "secs"], writes=[("t1", 0)])
                P.op("dve", lambda e: e.tensor_tensor(out=t1[0][0:32, :], in0=t1[0][0:32, :], in1=ps[6][0:32, 0:256], op=ALU.add), reads=[("t1", 0), PSK(6)], writes=[("t1", 0)])
                P.op("pool", lambda e: e.tensor_tensor(out=yg[0][0:32, :], in0=t1[0][0:32, :], in1=szs[:], op=ALU.mult), reads=[("t1", 0), "szs"], writes=[("yg", 0)])
                P.op("act", lambda e: e.activation(out=junk[0:32, :], in_=yg[0][0:32, :], func=AF.Square, accum_out=sssq_s[:, g:g + 1]), reads=[("yg", 0)], writes=["junk", ("sssq_s", g)])
                P.op("pool", lambda e: e.tensor_copy(out=ygb[0][0:32, :], in_=yg[0][0:32, :]), reads=[("yg", 0)], writes=[("ygb", 0)])
                for i in range(2):
                    P.op("pe", lambda e, i=i: e.transpose(out=psb[7][:, i * 32:(i + 1) * 32], in_=ygb[0][0:32, i * 128:(i + 1) * 128], identity=identb[0:32, 0:32]),
                         reads=[("ygb", 0), "identb"], writes=[PSK(7)], inc=(i == 1))
                P.op("act", lambda e: e.activation(out=smixT_s[:, 2 * g:2 * g + 2, :], in_=psb[7][:, 0:64].rearrange("p (i t) -> p i t", i=2), func=AF.Copy),
                     reads=[PSK(7)], writes=[("smixT_s", g)])
                stop_at(f"s{g}")

            for g in range(4):
                ssd_group(g)
            if "mixT_s" in dbg:
                P.dma("pool", dbg_out["mixT_s"].rearrange("p (c t) -> p c t", c=8), mixT_s[:], reads=[("mixT_s", g, c) for g in range(4) for c in range(NB)])
            if "smixT_s" in dbg:
                P.dma("pool", dbg_out["smixT_s"].rearrange("p (c t) -> p c t", c=8), smixT_s[:], reads=[("smixT_s", g) for g in range(4)])
            if "ssq_s" in dbg:
                P.dma("sp", dbg_out["ssq_s"], ssq_s[:].rearrange("p c g -> p (c g)"), reads=[("ssq_s", c, g) for g in range(4) for c in range(NB)])
            P.flush()

        mixT_a = sb(msc, "mixT_a", [128, 8, T], BF16, side="right")
        ssq_a = sb(msc, "ssq_a", [128, NB], F32, side="right")
        with ExitStack() as ph:
            Wq = sb(ph, "Wq", [128, 8, 128], BF16)
            Wk = sb(ph, "Wk", [128, 8, 128], BF16)
            Wv = sb(ph, "Wv", [128, 8, 128], BF16)
            QT = sb(ph, "QT", [128, T], BF16)
            KT = sb(ph, "KT", [128, 2 * T], BF16)
            VT = sb(ph, "VT", [128, 2 * T], BF16)
            kv32 = [sb(ph, f"kv32{i}", [128, 512], F32) for i in range(2)]
            kvo = [sb(ph, f"kvo{i}", [128, 4, 128], F32) for i in range(2)]
            Vaug = sb(ph, "Vaug", [128, 32, 192], BF16)
            Dm = sb(ph, "Dm", [128, 128], F32)
            Dmx = sb(ph, "Dmx", [128, 128], F32)
            Dp = sb(ph, "Dp", [128, 128], F32)
            vd = sb(ph, "vd", [128, 128], F32)
            vp = sb(ph, "vp", [128, 128], F32)
            mtmp = sb(ph, "mtmp", [128, 256], F32)
            masks = [sb(ph, f"masks{i}", [128, 3, 256], BF16) for i in range(2)]
            Eb = [sb(ph, f"Eb{i}", [128, 512], BF16) for i in range(2)]
            Pm = [sb(ph, f"Pm{i}", [128, 512], BF16) for i in range(2)]
            acc = sb(ph, "acc", [128, T], F32)
            rden = sb(ph, "rden", [128, 512], F32)
            sq = sb(ph, "sq", [128, 512], BF16)
            sQT = sb(ph, "sQT", [128, 32], BF16)
            sKT = sb(ph, "sKT", [128, 32], BF16)
            sknv = sb(ph, "sknv", [32, 2, 128], F32)
            sVnew = sb(ph, "sVnew", [8, 4, 192], BF16)
            ktile2 = [VT[:, o_:o_ + 13 * 128].rearrange("p (k d) -> p k d", d=128) for o_ in (0, 2048)]
            KTc2 = [KT[:, o_:o_ + 13 * 128].rearrange("p (k d) -> p k d", d=128) for o_ in (0, 2048)]
            sVaug2 = [Vaug[:, 0:13, :], Vaug[:, 16:29, :]]
            Ds = sb(ph, "Ds", [128, 24], F32)
            Vs = sb(ph, "Vs", [128, 24], F32)
            stmpm = sb(ph, "stmpm", [128, 24], F32)
            smask = sb(ph, "smask", [128, 2, 24], BF16)
            Dn = sb(ph, "Dn", [8, 8], F32)
            Dnx = sb(ph, "Dnx", [8, 8], F32)
            Wn = sb(ph, "Wn", [8, 8], F32)
            Z2 = sb(ph, "Z2", [8, 8], F32)
            tnn = sb(ph, "tnn", [8, 8], F32)
            smaskn = sb(ph, "smaskn", [8, 2, 8], BF16)
            sE2 = [sb(ph, f"sE{i}", [128, 48], BF16) for i in range(2)]
            sPm2 = [sb(ph, f"sPm{i}", [128, 48], BF16) for i in range(2)]
            sEn2 = [sb(ph, f"sEn{i}", [8, 16], BF16) for i in range(2)]
            sPn2 = [sb(ph, f"sPn{i}", [8, 16], BF16) for i in range(2)]
            srd = sb(ph, "srd", [128, 8], F32)
            sqs = sb(ph, "sqs", [128, 32], BF16)
            onesb = sb(ph, "onesb", [128, 1], BF16)
            sstmp = sb(ph, "sstmp", [128, NB], F32)

            P.op("pool", lambda e: e.memset(onesb[:], 1.0), writes=["onesb"])
            P.op("pool", lambda e: e.memset(ssq_a[:], 0.0), writes=["ssq_a"])
            P.op("pool", lambda e: e.memset(Vaug[:, 0:16, 64:128], 1.0), writes=["Vones"])
            P.op("pool", lambda e: e.memset(Vaug[:, 16:32, 64:128], 1.0), writes=["Vones"])
            P.op("dve", lambda e: e.tensor_scalar(out=Vaug[:, 16:32, 64:128], in0=Vaug[:, 16:32, 64:128], scalar1=flag[:, 0:1], scalar2=None, op0=ALU.mult),
                 reads=["flag"], writes=["Vones"])
            P.op("pool", lambda e: e.iota(Dm[:], pattern=[[1, 128]], base=0, channel_multiplier=-1, allow_small_or_imprecise_dtypes=True), writes=["Dm"])
            P.op("dve", lambda e: e.tensor_scalar(out=Dmx[:], in0=Dm[:], scalar1=0.0, scalar2=None, op0=ALU.max), reads=["Dm"], writes=["Dmx"])
            P.op("dve", lambda e: e.tensor_scalar(out=Dp[:], in0=Dm[:], scalar1=128.0, scalar2=None, op0=ALU.add), reads=["Dm"], writes=["Dp"])
            P.op("dve", lambda e: e.tensor_scalar(out=vd[:], in0=Dm[:], scalar1=0.0, scalar2=None, op0=ALU.is_ge), reads=["Dm"], writes=["vd"])
            P.op("dve", lambda e: e.tensor_scalar(out=vp[:], in0=Dm[:], scalar1=0.0, scalar2=None, op0=ALU.is_le), reads=["Dm"], writes=["vp"])

            P.op("pool", lambda e: e.memset(sssq_a[:], 0.0), writes=["sssq_a"])
            P.op("pool", lambda e: e.memset(sVnew[:, :, 64:128], 1.0), writes=["sVnones"])
            P.op("pool", lambda e: e.iota(Ds[:, 0:8], pattern=[[1, 8]], base=128, channel_multiplier=-1, allow_small_or_imprecise_dtypes=True), writes=["Ds"])
            P.op("pool", lambda e: e.iota(Ds[:, 8:16].rearrange("p (c q) -> p c q", c=4), pattern=[[0, 4], [4, 2]], base=512, channel_multiplier=-4,
                                          allow_small_or_imprecise_dtypes=True), writes=["Ds"])
            P.op("pool", lambda e: e.iota(Ds[:, 16:24], pattern=[[0, 8]], base=2048, channel_multiplier=-16, allow_small_or_imprecise_dtypes=True), writes=["Ds"])
            P.op("pool", lambda e: e.memset(Vs[:], 1.0), writes=["Vs"])
            P.op("dve", lambda e: e.tensor_scalar(out=Vs[:, 0:8], in0=Ds[:, 0:8], scalar1=128.0, scalar2=None, op0=ALU.is_le), reads=["Ds", "Vs"], writes=["Vs"])
            P.op("dve", lambda e: e.tensor_scalar(out=Vs[:, 8:16], in0=Ds[:, 8:16], scalar1=512.0, scalar2=None, op0=ALU.is_le), reads=["Ds", "Vs"], writes=["Vs"])
            P.op("pool", lambda e: e.iota(Dn[:], pattern=[[1, 8]], base=0, channel_multiplier=-1, allow_small_or_imprecise_dtypes=True), writes=["Dn"])
            P.op("dve", lambda e: e.tensor_scalar(out=Dnx[:], in0=Dn[:], scalar1=0.0, scalar2=None, op0=ALU.max), reads=["Dn"], writes=["Dnx"])
            P.op("dve", lambda e: e.tensor_scalar(out=Wn[:], in0=Dn[:], scalar1=0.0, scalar2=None, op0=ALU.is_ge), reads=["Dn"], writes=["Wn"])
            P.op("dve", lambda e: e.tensor_scalar(out=tnn[:], in0=Dn[:], scalar1=4.0, scalar2=None, op0=ALU.is_equal), reads=["Dn"], writes=["tnn"])
            P.op("dve", lambda e: e.tensor_tensor(out=Wn[:], in0=Wn[:], in1=tnn[:], op=ALU.add), reads=["Wn", "tnn"], writes=["Wn"])
            P.op("dve", lambda e: e.tensor_scalar(out=Z2[:], in0=Dn[:], scalar1=0.0, scalar2=2.0, op0=ALU.is_equal, op1=ALU.mult), reads=["Dn"], writes=["Z2"])

            DILS = (1, 4, 16)
            pj = [0]

            def attn_pair(hp):
                def make_masks(hh):
                    slope = 2.0 ** (-8.0 * (2 * hp + hh + 1) / 16.0)
                    for bi, dil in enumerate(DILS):
                        sc = -slope * dil
                        P.op("act", lambda e, sc=sc: e.activation(out=mtmp[:, 0:128], in_=Dmx[:], func=AF.Exp, scale=sc), reads=["Dmx"], writes=["mtmp0"])
                        P.op("act", lambda e, sc=sc: e.activation(out=mtmp[:, 128:256], in_=Dp[:], func=AF.Exp, scale=sc), reads=["Dp"], writes=["mtmp1"])
                        P.op("dve", lambda e, bi=bi: e.tensor_tensor(out=masks[hh][:, bi, 0:128], in0=mtmp[:, 0:128], in1=vd[:], op=ALU.mult),
                             reads=["mtmp0", "vd"], writes=[("masks", hh)])
                        P.op("dve", lambda e, bi=bi: e.tensor_tensor(out=masks[hh][:, bi, 128:256], in0=mtmp[:, 128:256], in1=vp[:], op=ALU.mult),
                             reads=["mtmp1", "vp"], writes=[("masks", hh)])

                for hh in range(2):
                    make_masks(hh)

                def smasks(hh):
                    slope = 2.0 ** (-8.0 * (2 * hp + hh + 1) / 16.0)
                    P.op("act", lambda e: e.activation(out=stmpm[:], in_=Ds[:], func=AF.Exp, scale=-slope), reads=["Ds"], writes=["stmpm"])
                    P.op("dve", lambda e: e.tensor_tensor(out=smask[:, hh, :], in0=stmpm[:], in1=Vs[:], op=ALU.mult), reads=["stmpm", "Vs"], writes=[("smask", hh)])
                    P.op("act", lambda e: e.activation(out=tnn[:], in_=Dnx[:], func=AF.Exp, scale=-slope), reads=["Dnx"], writes=["tnn"])
                    P.op("dve", lambda e: e.tensor_tensor(out=tnn[:], in0=tnn[:], in1=Wn[:], op=ALU.mult), reads=["tnn", "Wn"], writes=["tnn"])
                    P.op("dve", lambda e: e.tensor_tensor(out=smaskn[:, hh, :], in0=tnn[:], in1=Z2[:], op=ALU.add), reads=["tnn", "Z2"], writes=[("smaskn", hh)])

                stop_at(f"assq{hp}")
                for hh in range(2):
                    smasks(hh)

                for Wt, col, nm in ((Wq, COL_Q, "Wq"), (Wk, COL_K, "Wk"), (Wv, COL_V, "Wv")):
                    P.dma("pool", Wt[:], w_in_v[:, :, col + hp * 128:col + (hp + 1) * 128], writes=[nm])
                kvn = [0]

                def proj(Wt, nm, dst, dnm, u0, src, dkey, j, outd):
                    pp = 4 + pj[0] % 2; pj[0] += 1
                    for kc in range(8):
                        P.op("pe", lambda e, kc=kc: e.matmul(ps[pp][:, :], lhsT=Wt[:, kc, :], rhs=src[:, kc, j * 512:(j + 1) * 512], start=(kc == 0), stop=(kc == 7)),
                             reads=[nm] + xkeys(dkey, 4 * j, 4 * j + 4), writes=[PSK(pp)], inc=(kc == 7))
                    P.op("act", lambda e: e.activation(out=dst[:, u0 + j * 512:u0 + (j + 1) * 512], in_=ps[pp][:, :], func=AF.Copy),
                         reads=[PSK(pp)], writes=[(dnm, (u0 + j * 512) // 512)])
                    if outd is None:
                        return None
                    b2 = kvn[0] % 2; kvn[0] += 1
                    P.op("dve", lambda e: e.tensor_copy(out=kv32[b2][:], in_=ps[pp][:, :]), reads=[PSK(pp)], writes=[("kv32", b2)])

                    def back():
                        for q4 in range(4):
                            P.op("pe", lambda e, q4=q4: e.transpose(out=ps[6][:, q4 * 128:(q4 + 1) * 128], in_=kv32[b2][:, q4 * 128:(q4 + 1) * 128], identity=ident[:]),
                                 reads=[("kv32", b2), "ident"], writes=[PSK(6)], inc=(q4 == 3))
                        P.op("dve", lambda e: e.tensor_copy(out=kvo[b2][:].rearrange("p b d -> p (b d)"), in_=ps[6][:, :]), reads=[PSK(6)], writes=[("kvo", b2)])
                        P.dma("sp", outd[j * 512:(j + 1) * 512, hp * 128:(hp + 1) * 128].rearrange("(b p) d -> p b d", p=128), kvo[b2][:], reads=[("kvo", b2)])
                    return back

                plist = [(Wq, "Wq", QT, "QT", 0, xT_o, "xT_o", j, None) for j in range(4)]
                plist += [(Wk, "Wk", KT, "KT", 0, xT_p, "xT_p", j, None) for j in range(4)]
                plist += [(Wk, "Wk", KT, "KT", T, xT_o, "xT_o", j, k_o) for j in range(4)]
                plist += [(Wv, "Wv", VT, "VT", 0, xT_p, "xT_p", j, None) for j in range(4)]
                plist += [(Wv, "Wv", VT, "VT", T, xT_o, "xT_o", j, v_o) for j in range(4)]
                pend_b = None
                for args in plist:
                    bk_ = proj(*args)
                    if pend_b is not None:
                        pend_b()
                    pend_b = bk_
                if pend_b is not None:
                    pend_b()
                stop_at(f"aproj{hp}")
                allK = [("KT", i) for i in range(8)]
                allV = [("VT", i) for i in range(8)]
                allQ = [("QT", i) for i in range(4)]

                sn = [0]
                qn = [0]
                ACCK = [("acc", i) for i in (0, 1, 2, 3, 9)]

                def branch(bi, dil, hh):
                    nmb = 2 * T // (128 * dil)
                    own0 = nmb // 2
                    slot = {}
                    so, spast = 0, 16
                    tiles = []
                    for r in range(dil):
                        for mb in range(own0 - 1, nmb):
                            if mb >= own0:
                                slot[(r, mb)] = so; so += 1
                            else:
                                slot[(r, mb)] = spast; spast += 1
                            tiles.append((r, mb))
                    order = sorted(tiles, key=lambda t: slot[t])

                    def build_group(grp):
                        for n, (r, mb) in enumerate(grp):
                            u0 = r + dil * 128 * mb
                            P.op("pe", lambda e, n=n, u0=u0: e.transpose(out=psb[6][:, n * 128:(n + 1) * 128], in_=VT[:, u0:u0 + 127 * dil + 1:dil], identity=identb[:]),
                                 reads=allV + ["identb"], writes=[PSK(6)], inc=(n == len(grp) - 1))
                        s0 = slot[grp[0]]
                        ng = len(grp)
                        src_v = psb[6][:, 0:ng * 128].rearrange("p (s a d) -> p s a d", s=ng, a=2)
                        for a_ in range(2):
                            dst_v = Vaug[:, s0:s0 + ng, a_ * 128:a_ * 128 + 64]
                            if s0 >= 16:
                                P.op("dve", lambda e, a_=a_, dst_v=dst_v: e.tensor_scalar(out=dst_v, in0=src_v[:, :, a_, :], scalar1=flag[:, 0:1], scalar2=None, op0=ALU.mult),
                                     reads=[PSK(6), "flag"], writes=[("Vaug", a_)])
                            else:
                                P.op("dve", lambda e, a_=a_, dst_v=dst_v: e.tensor_copy(out=dst_v, in_=src_v[:, :, a_, :]), reads=[PSK(6)], writes=[("Vaug", a_)])

                    g0 = 0
                    while g0 < len(order):
                        g1 = g0
                        while g1 < len(order) and g1 - g0 < 8 and (slot[order[g1]] >= 16) == (slot[order[g0]] >= 16):
                            g1 += 1
                        build_group(order[g0:g1])
                        g0 = g1

                    hsl = slice(64 * hh, 64 * hh + 64)
                    vsl = slice(64 * hh, 64 * hh + 128)
                    if dil == 1:
                        quads = [[(0, own0 + 4 * q + k) for k in range(4)] for q in range(4)]
                        accv = [acc[:, q * 512:(q + 1) * 512] for q in range(4)]
                    elif dil == 4:
                        quads = [[(r, own0 + k) for k in range(4)] for r in range(4)]
                        accv = [acc[:, r:T:4] for r in range(4)]
                    else:
                        quads = [[(4 * q + k, 1) for k in range(4)] for q in range(4)]
                        accv = [acc[:].rearrange("p (i s) -> p s i", s=16)[:, 4 * q:4 * q + 4, :] for q in range(4)]

                    def s_part(q, half):
                        sbk = sn[0] % 2; sn[0] += 1
                        pair = quads[q][2 * half:2 * half + 2]
                        for n, (r, mb) in enumerate(pair):
                            t0 = r + dil * 128 * (mb - own0)
                            for w, kmb in enumerate((mb, mb - 1)):
                                u0 = r + dil * 128 * kmb
                                P.op("pe", lambda e, n=n, w=w, t0=t0, u0=u0: e.matmul(
                                    ps[sbk][:, (2 * n + w) * 128:(2 * n + w + 1) * 128], lhsT=KT[hsl, u0:u0 + 127 * dil + 1:dil], rhs=QT[hsl, t0:t0 + 127 * dil + 1:dil],
                                    start=True, stop=True),
                                    reads=allK + allQ, writes=[PSK(sbk)], inc=(n == 1 and w == 1))
                        P.op("act", lambda e: e.activation(out=Eb[sbk][:], in_=ps[sbk][:, :], func=AF.Exp, scale=0.125), reads=[PSK(sbk)], writes=[("Eb", sbk)])
                        P.op("dve", lambda e: e.tensor_tensor(out=Pm[sbk][:].rearrange("p (n c) -> p n c", n=2), in0=Eb[sbk][:].rearrange("p (n c) -> p n c", n=2),
                                                              in1=masks[hh][:, bi, :].unsqueeze(1).broadcast_to([128, 2, 256]), op=ALU.mult),
                             reads=[("Eb", sbk), ("masks", hh)], writes=[("Pm", sbk)])
                        return sbk

                    def pv_part(q, half, ob, sbk):
                        pair = quads[q][2 * half:2 * half + 2]
                        for n, (r, mb) in enumerate(pair):
                            k4 = 2 * half + n
                            for w, kmb in enumerate((mb, mb - 1)):
                                sl_ = slot[(r, kmb)]
                                P.op("pe", lambda e, n=n, w=w, k4=k4, sl_=sl_: e.matmul(
                                    ps[ob][:, k4 * 128:(k4 + 1) * 128], lhsT=Vaug[:, sl_, vsl], rhs=Pm[sbk][:, (2 * n + w) * 128:(2 * n + w + 1) * 128],
                                    start=(w == 0), stop=(w == 1), skip_group_check=True),
                                    reads=[("Vaug", 0), ("Vaug", 1), "Vones", ("Pm", sbk)], writes=[PSK(ob)], inc=(w == 1 and n == 1))

                    def evac_quad(q, ob):
                        if dil == 16:
                            src_q = ps[ob][:, :].rearrange("p (k i) -> p k i", k=4)
                        else:
                            src_q = ps[ob][:, :]
                        if bi == 0:
                            P.op("act", lambda e: e.activation(out=accv[q], in_=src_q, func=AF.Copy), reads=[PSK(ob)], writes=[("acc", q)])
                        else:
                            P.op("dve", lambda e: e.tensor_tensor(out=accv[q], in0=src_q, in1=accv[q], op=ALU.add), reads=[PSK(ob)] + ACCK, writes=[("acc", 9)])

                    items = [(q, half) for q in range(4) for half in range(2)]
                    obs = {}
                    for q in range(4):
                        obs[q] = 2 + qn[0] % 2; qn[0] += 1
                    sb_of = {}
                    LOOK = 2
                    for idx in range(len(items) + LOOK):
                        j_ = idx - LOOK
                        if j_ >= 0:
                            q, half = items[j_]
                            pv_part(q, half, obs[q], sb_of[j_])
                            if half == 1:
                                evac_quad(q, obs[q])
                        if idx < len(items):
                            sb_of[idx] = s_part(*items[idx])

                def finish_head(hh):
                    numsl = slice(64 * hh, 64 * hh + 64)
                    densl = slice(64 - 64 * hh, 128 - 64 * hh)
                    for j in range(4):
                        rd = kv32[j % 2]; rk = ("kv32", j % 2)
                        P.op("act", lambda e, j=j, rd=rd: e.activation(out=rd[numsl, :], in_=acc[densl, j * 512:(j + 1) * 512], func=AF.Ln), reads=ACCK, writes=[rk])
                        P.op("act", lambda e, rd=rd: e.activation(out=rd[numsl, :], in_=rd[numsl, :], func=AF.Exp, scale=-1.0), reads=[rk], writes=[rk])
                        P.op("dve", lambda e, j=j, rd=rd: e.tensor_tensor(out=mixT_a[numsl, hp, j * 512:(j + 1) * 512], in0=acc[numsl, j * 512:(j + 1) * 512], in1=rd[numsl, :], op=ALU.mult),
                             reads=[rk] + ACCK, writes=[("mixT_a", hp, hh)])

                for hh in range(2):
                    for bi, dil in enumerate(DILS):
                        branch(bi, dil, hh)
                        stop_at(f"abr{hp}_{hh}_{bi}")
                    finish_head(hh)
                    stop_at(f"afin{hp}_{hh}")
                for j4 in range(4):
                    sqb = Eb[j4 % 2]; sqk = ("Eb", j4 % 2)
                    P.op("act", lambda e, j4=j4, sqb=sqb: e.activation(out=sqb[:], in_=mixT_a[:, hp, j4 * 512:(j4 + 1) * 512], func=AF.Square),
                         reads=[("mixT_a", hp, 0), ("mixT_a", hp, 1)], writes=[sqk])
                    for b4 in range(4):
                        blk = 4 * j4 + b4
                        P.op("pe", lambda e, blk=blk, b4=b4, sqb=sqb: e.matmul(ps[7][:, blk:blk + 1], lhsT=sqb[:, b4 * 128:(b4 + 1) * 128], rhs=onesb[:], start=True, stop=True, skip_group_check=True),
                             reads=[sqk, "onesb"], writes=[PSK(7)], inc=(b4 == 3))
                P.op("dve", lambda e: e.tensor_tensor(out=ssq_a[:], in0=ps[7][:, 0:NB], in1=ssq_a[:], op=ALU.add), reads=[PSK(7), "ssq_a"], writes=["ssq_a"])

                cols = slice(hp * 128, (hp + 1) * 128)
                for bi_, (Wt, nm, dst, dk) in enumerate(((Wq, "Wq", sQT, "sQT"), (Wk, "Wk", sKT, "sKT"))):
                    pb_ = 4 + bi_
                    for kc in range(8):
                        P.op("pe", lambda e, kc=kc, Wt=Wt, pb_=pb_: e.matmul(ps[pb_][:, 0:32], lhsT=Wt[:, kc, :], rhs=xsT[:, kc, :], start=(kc == 0), stop=(kc == 7)),
                             reads=[nm, "xsT"], writes=[PSK(pb_)], inc=(kc == 7))
                    P.op("act", lambda e, dst=dst, pb_=pb_: e.activation(out=dst[:], in_=ps[pb_][:, 0:32], func=AF.Copy), reads=[PSK(pb_)], writes=[dk])
                for a_, (Wt, nm, outd) in enumerate(((Wk, "Wk", kn_s), (Wv, "Wv", vn_s))):
                    pb_ = 4 + a_
                    for kc in range(8):
                        P.op("pe", lambda e, kc=kc, Wt=Wt, pb_=pb_: e.matmul(ps[pb_][0:32, 0:128], lhsT=xsT[:, kc, :], rhs=Wt[:, kc, :], start=(kc == 0), stop=(kc == 7)),
                             reads=[nm, "xsT"], writes=[PSK(pb_)], inc=(kc == 7))
                    P.op("dve", lambda e, a_=a_, pb_=pb_: e.tensor_copy(out=sknv[:, a_, :], in_=ps[pb_][0:32, 0:128]), reads=[PSK(pb_)], writes=[("sknv", a_)])
                    P.dma("sp", outd[:, cols], sknv[:, a_, :], reads=[("sknv", a_)])
                for sq_ in range(4):
                    pb_ = 4 + sq_ % 2
                    for kc in range(8):
                        P.op("pe", lambda e, kc=kc, sq_=sq_, pb_=pb_: e.matmul(ps[pb_][0:8, 128:256],
                                                                               lhsT=xsT[:, kc, sq_ * 8:(sq_ + 1) * 8], rhs=Wv[:, kc, :], start=(kc == 0), stop=(kc == 7)),
                             reads=["Wv", "xsT"], writes=[PSK(pb_)], inc=(kc == 7))
                    P.op("dve", lambda e, sq_=sq_, pb_=pb_: e.tensor_copy(out=sVnew[:, sq_, 0:64], in_=ps[pb_][0:8, 128:192]), reads=[PSK(pb_)], writes=[("sVnew", sq_)])
                    P.op("act", lambda e, sq_=sq_, pb_=pb_: e.activation(out=sVnew[:, sq_, 128:192], in_=ps[pb_][0:8, 192:256], func=AF.Copy), reads=[PSK(pb_)], writes=[("sVnew", sq_)])

                ktf = VT[:, 0:2 * 13 * 128].bitcast(F32).rearrange("p (k d) -> p k d", d=128)
                svf = acc[:, 0:13 * 128].rearrange("p (k d) -> p k d", d=128)
                ktfk = [("VT", i_) for i_ in range(7)]

                def sample_load(sq_):
                    for dst_, src_, wk in ((ktf, ck, ktfk), (svf, cv, ACCK)):
                        P.dma("sp", dst_[:, 0, :], src_[sq_, 1920:2048, cols], writes=wk)
                        P.dma("sp", dst_[:, 1:5, :], src_[sq_, 1536:2048, cols].rearrange("(i c) d -> i c d", c=4), writes=wk)
                        P.dma("sp", dst_[:, 5:13, :], src_[sq_, :, cols].rearrange("(i s) d -> i s d", s=16)[:, 0:8, :], writes=wk)

                def sample_A1(sq_):
                    bf = sq_ % 2
                    sVaug = sVaug2[bf]; KTc = KTc2[bf]
                    ktk = [("KT", 4 * bf + i_) for i_ in range(4)]
                    svk = [("sV", bf, 0), ("sV", bf, 1)] + ([("Vaug", 0), ("Vaug", 1)] if sq_ < 2 else [])
                    for k0 in range(0, 13, 4):
                        k1 = min(13, k0 + 4)
                        for k in range(k0, k1):
                            P.op("pe", lambda e, k=k, k0=k0: e.transpose(out=ps[6][:, (k - k0) * 128:(k - k0 + 1) * 128], in_=ktf[:, k, :], identity=ident[:]),
                                 reads=ktfk + ["ident"], writes=[PSK(6)], inc=(k == k1 - 1))
                        P.op("dve" if (k0 // 4) % 2 == 0 else "act", (lambda e, k0=k0, k1=k1: e.tensor_copy(out=KTc[:, k0:k1, :], in_=ps[6][:, 0:(k1 - k0) * 128].rearrange("p (k d) -> p k d", d=128)))
                             if (k0 // 4) % 2 == 0 else
                             (lambda e, k0=k0, k1=k1: e.activation(out=KTc[:, k0:k1, :], in_=ps[6][:, 0:(k1 - k0) * 128].rearrange("p (k d) -> p k d", d=128), func=AF.Copy)),
                             reads=[PSK(6)], writes=ktk)
                    P.op("pool", lambda e: e.tensor_copy(out=sVaug[:, :, 0:64], in_=svf[:, :, 0:64]), reads=ACCK, writes=svk)
                    P.op("pool", lambda e: e.tensor_copy(out=sVaug[:, :, 128:192], in_=svf[:, :, 64:128]), reads=ACCK, writes=svk)

                def sample_A(sq_):
                    bf = sq_ % 2
                    sVaug = sVaug2[bf]; KTc = KTc2[bf]
                    ktk = [("KT", 4 * bf + i_) for i_ in range(4)]
                    sE = sE2[bf]; sPm = sPm2[bf]; sEn = sEn2[bf]; sPn = sPn2[bf]
                    sbank = ps[bf]
                    qs = sq_ * 8
                    for hh in range(2):
                        hsl = slice(64 * hh, 64 * hh + 64)
                        c0 = hh * 24
                        mm = []
                        mm.append((sbank[:, c0:c0 + 8], KTc[hsl, 0, :], sQT[hsl, qs:qs + 8]))
                        for c in range(4):
                            mm.append((sbank[:, c0 + 8 + 2 * c:c0 + 10 + 2 * c], KTc[hsl, 1 + c, :], sQT[hsl, qs + c:qs + c + 5:4]))
                        for t in range(8):
                            mm.append((sbank[:, c0 + 16 + t:c0 + 17 + t], KTc[hsl, 5 + t, :], sQT[hsl, qs + t:qs + t + 1]))
                        mm.append((sbank[0:8, 256 + hh * 8:264 + hh * 8], sKT[hsl, qs:qs + 8], sQT[hsl, qs:qs + 8]))
                        for n, (o_, l_, r_) in enumerate(mm):
                            P.op("pe", lambda e, o_=o_, l_=l_, r_=r_: e.matmul(o_, lhsT=l_, rhs=r_, start=True, stop=True, skip_group_check=True),
                                 reads=ktk + ["sQT", "sKT"], writes=[PSK(bf)], inc=(n == len(mm) - 1))
                    P.op("act", lambda e: e.activation(out=sE[:], in_=sbank[:, 0:48], func=AF.Exp, scale=0.125), reads=[PSK(bf)], writes=[("sE", bf)])
                    P.op("act", lambda e: e.activation(out=sEn[:], in_=sbank[0:8, 256:272], func=AF.Exp, scale=0.125), reads=[PSK(bf)], writes=[("sEn", bf)])
                    P.op("dve", lambda e: e.tensor_tensor(out=sPm[:], in0=sE[:], in1=smask[:].rearrange("p h c -> p (h c)"), op=ALU.mult), reads=[("sE", bf), ("smask", 0), ("smask", 1)], writes=[("sPm", bf)])
                    P.op("dve", lambda e: e.tensor_tensor(out=sPn[:], in0=sEn[:], in1=smaskn[:].rearrange("p h c -> p (h c)"), op=ALU.mult), reads=[("sEn", bf), ("smaskn", 0), ("smaskn", 1)], writes=[("sPn", bf)])

                def sample_B(sq_):
                    bf = sq_ % 2
                    sVaug = sVaug2[bf]
                    sPm = sPm2[bf]; sPn = sPn2[bf]
                    qs = sq_ * 8
                    obank = ps[2 + bf]
                    P.op("dve", lambda e: e.memset(obank[:, 0:16], 0.0), writes=[PSK(2 + bf)])
                    for hh in range(2):
                        vsl = slice(64 * hh, 64 * hh + 128)
                        c0 = hh * 24
                        o0 = hh * 8
                        mm = []
                        mm.append((obank[:, o0:o0 + 8], sVaug[:, 0, vsl], sPm[:, c0:c0 + 8]))
                        for c in range(4):
                            mm.append((obank[:, o0 + c:o0 + c + 5:4], sVaug[:, 1 + c, vsl], sPm[:, c0 + 8 + 2 * c:c0 + 10 + 2 * c]))
                        for t in range(8):
                            mm.append((obank[:, o0 + t:o0 + t + 1], sVaug[:, 5 + t, vsl], sPm[:, c0 + 16 + t:c0 + 17 + t]))
                        mm.append((obank[:, o0:o0 + 8], sVnew[0:8, sq_, vsl], sPn[0:8, hh * 8:hh * 8 + 8]))
                        for n, (o_, l_, r_) in enumerate(mm):
                            P.op("pe", lambda e, o_=o_, l_=l_, r_=r_: e.matmul(o_, lhsT=l_, rhs=r_, start=False, stop=(n == len(mm) - 1), skip_group_check=True),
                                 reads=[("sV", bf, 0), ("sV", bf, 1), ("Vaug", 0), ("Vaug", 1), "Vones", "sVnones", ("sVnew", sq_), ("sPm", bf), ("sPn", bf)], writes=[PSK(2 + bf)], inc=(n == len(mm) - 1))
                    for hh in range(2):
                        numsl = slice(64 * hh, 64 * hh + 64)
                        densl = slice(64 - 64 * hh, 128 - 64 * hh)
                        o0 = hh * 8
                        P.op("dve", lambda e, numsl=numsl, densl=densl, o0=o0: e.reciprocal(out=srd[numsl, :], in_=obank[densl, o0:o0 + 8]), reads=[PSK(2 + bf)], writes=["srd"])
                        P.op("dve", lambda e, numsl=numsl, o0=o0: e.tensor_tensor(out=smixT_a[numsl, hp, qs:qs + 8], in0=obank[numsl, o0:o0 + 8], in1=srd[numsl, :], op=ALU.mult),
                             reads=[PSK(2 + bf), "srd"], writes=[("smixT_a", hp)])

                P.op("pool", lambda e: e.memset(Vaug[:, 16:29, 64:128], 1.0), writes=["Vones"])
                sample_load(0)
                sample_A1(0)
                sample_load(1)
                sample_A(0)
                sample_A1(1)
                sample_load(2)
                sample_B(0)
                sample_A(1)
                sample_A1(2)
                sample_load(3)
                sample_B(1)
                sample_A(2)
                sample_A1(3)
                sample_B(2)
                sample_A(3)
                sample_B(3)
                P.op("dve", lambda e: e.tensor_copy(out=Vaug[:, 16:32, 64:128], in_=flag[:, 0:1].unsqueeze(1).broadcast_to([128, 16, 64])), reads=["flag"], writes=["Vones"])
                P.op("act", lambda e: e.activation(out=sqs[:], in_=smixT_a[:, hp, :], func=AF.Square), reads=[("smixT_a", hp)], writes=["sqs"])
                P.op("pe", lambda e: e.matmul(ps[7][0:32, 32:33], lhsT=sqs[:], rhs=onesb[:], start=True, stop=True), reads=["sqs", "onesb"], writes=[PSK(7)])
                P.op("dve", lambda e: e.tensor_tensor(out=sssq_a[:], in0=ps[7][0:32, 32:33], in1=sssq_a[:], op=ALU.add), reads=[PSK(7), "sssq_a"], writes=["sssq_a"])
                stop_at(f"a{hp}")

            for hp in range(NPAIR):
                attn_pair(hp)
            if "mixT_a" in dbg:
                P.dma("pool", dbg_out["mixT_a"].rearrange("p (c t) -> p c t", c=8), mixT_a[:], reads=[("mixT_a", hp, hh) for hp in range(8) for hh in range(2)])
            if "ssq_a" in dbg:
                P.dma("sp", dbg_out["ssq_a"], ssq_a[:], reads=["ssq_a"])
            if "smixT_a" in dbg:
                P.dma("pool", dbg_out["smixT_a"].rearrange("p (c t) -> p c t", c=8), smixT_a[:], reads=[("smixT_a", hp) for hp in range(8)])
            P.flush()
        xsc.close()

        AXX = mybir.AxisListType.X
        hdn_scr = nc.dram_tensor("hdn_scr", [T + 32, D], F32, kind="Internal").ap()
        hdnT_all = sb(es, "hdnT_all", [128, 8, T], BF16)
        shdnT = sb(es, "shdnT", [128, 8, 32], BF16)

        def rstd_from_ssq(dst, src, nparts, ncols, tmpc, key_src, key_dst):
            P.op("pool", lambda e: e.tensor_scalar(out=dst[0:nparts, 0:ncols], in0=src, scalar1=1.0 / 1024.0, scalar2=RMS_EPS, op0=ALU.mult, op1=ALU.add),
                 reads=[key_src], writes=[key_dst])
            P.op("pool", lambda e: e.tensor_tensor(out=dst[0:nparts, 0:ncols], in0=dst[0:nparts, 0:ncols], in1=tmpc[0:nparts, 0:ncols], op=ALU.pow),
                 reads=[key_dst, "mhalf"], writes=[key_dst])

        bst = sb(es, "bst", [128, 2, 6], F32)
        mv = sb(es, "mv", [128, 2], F32)
        lnr = sb(es, "lnr", [128, 1], F32)
        lnb = sb(es, "lnb", [128, 1], F32)
        lnh = sb(es, "lnh", [128, 1], F32)
        P.op("pool", lambda e: e.memset(lnh[:], -0.5), writes=["lnh"])

        def layer_norm(src_t, dst_t, nt, gbc, bbc, skey, dkey, gk, bk):
            skeys = skey if isinstance(skey, list) else [skey]
            for h2 in range(2):
                P.op("dve", lambda e, h2=h2: e.bn_stats(out=bst[0:nt, h2, :], in_=src_t[0:nt, h2 * 512:(h2 + 1) * 512]), reads=skeys, writes=[("bst", h2)])
            P.op("dve", lambda e: e.bn_aggr(out=mv[0:nt, :], in_=bst[0:nt, :, :].rearrange("p a s -> p (a s)")), reads=[("bst", 0), ("bst", 1)], writes=["mv"])
            P.op("pool", lambda e: e.tensor_scalar(out=lnr[0:nt, :], in0=mv[0:nt, 1:2], scalar1=LN_EPS, scalar2=None, op0=ALU.add), reads=["mv"], writes=["lnr"])
            P.op("pool", lambda e: e.tensor_tensor(out=lnr[0:nt, :], in0=lnr[0:nt, :], in1=lnh[0:nt, :], op=ALU.pow), reads=["lnr", "lnh"], writes=["lnr"])
            P.op("dve", lambda e: e.scalar_tensor_tensor(out=lnb[0:nt, :], in0=mv[0:nt, 0:1], scalar=-1.0, in1=lnr[0:nt, :], op0=ALU.mult, op1=ALU.mult),
                 reads=["mv", "lnr"], writes=["lnb"])
            P.op("act", lambda e: e.activation(out=dst_t[0:nt, :], in_=src_t[0:nt, :], func=AF.Identity, scale=lnr[0:nt, 0:1], bias=lnb[0:nt, 0:1]),
                 reads=skeys + ["lnr", "lnb"], writes=[dkey])
            P.op("pool", lambda e: e.tensor_tensor(out=dst_t[0:nt, :], in0=dst_t[0:nt, :], in1=gbc[0:nt, :], op=ALU.mult), reads=[dkey, gk], writes=[dkey])
            P.op("pool", lambda e: e.tensor_tensor(out=dst_t[0:nt, :], in0=dst_t[0:nt, :], in1=bbc[0:nt, :], op=ALU.add), reads=[dkey, bk], writes=[dkey])


        with ExitStack() as ph:
            Wo = sb(ph, "Wo", [128, 16, D], BF16)
            gcol = sb(ph, "gcol", [128, 16], F32)
            g1_bc = sb(ph, "g1_bc", [128, D], F32)
            b1_bc = sb(ph, "b1_bc", [128, D], F32)
            mhalf = sb(ph, "mhalf", [128, NB], F32)
            rs_a = sb(ph, "rs_a", [128, NB], F32)
            rs_s = sb(ph, "rs_s", [128, NB], F32)
            sst = sb(ph, "sst", [128, NB], F32)
            xt = [sb(ph, f"xt{i}", [128, D], F32) for i in range(3)]
            pre = [sb(ph, f"pre{i}", [128, D], F32) for i in range(3)]
            hdn = [sb(ph, f"hdn{i}", [128, D], F32) for i in range(3)]

            for c in range(16):
                P.dma("pool", Wo[:, c, :], w_out[c * 128:(c + 1) * 128, :], writes=[("Wo", c)])
            P.dma("sp", gcol[:, 0:8], attn_g.rearrange("(c p) -> p c", p=128), writes=["gcol"], allow_slow_non_contiguous=True)
            P.dma("sp", gcol[:, 8:16], ssm_g.rearrange("(c p) -> p c", p=128), writes=["gcol"], allow_slow_non_contiguous=True)
            P.dma("sp", g1_bc[:], ln1_g.partition_broadcast(128), writes=["g1_bc"])
            P.dma("sp", b1_bc[:], ln1_b.partition_broadcast(128), writes=["b1_bc"])
            for c in range(16):
                P.op("dve", lambda e, c=c: e.tensor_scalar(out=Wo[:, c, :], in0=Wo[:, c, :], scalar1=gcol[:, c:c + 1], scalar2=None, op0=ALU.mult),
                     reads=[("Wo", c), "gcol"], writes=[("Wo", c)])
            P.op("pool", lambda e: e.memset(mhalf[:], -0.5), writes=["mhalf"])
            P.op("dve", lambda e: e.tensor_reduce(out=sst[:], in_=ssq_s[:], axis=AXX, op=ALU.add),
                 reads=[("ssq_s", c, g) for g in range(4) for c in range(NB)], writes=["sst"])
            rstd_from_ssq(rs_a, ssq_a[:], 128, NB, mhalf, "ssq_a", "rs_a")
            rstd_from_ssq(rs_s, sst[:], 128, NB, mhalf, "sst", "rs_s")
            allWo = [("Wo", c) for c in range(16)]
            mixa_keys = [("mixT_a", hp, hh) for hp in range(8) for hh in range(2)]

            def outproj_block(tb, nt, mA, mS, tcol, akeys, skeys, rsa_col, rss_col, xrows, hT_dst, hT_col, scr_rows, hkey, rak="rs_a", rsk="rs_s"):
                b = tb % 3
                P.dma("sp", xt[b][0:nt, :], xrows, writes=[("xt", b)])
                for half in range(2):
                    for c in range(8):
                        P.op("pe", lambda e, c=c, half=half: e.matmul(ps[half][0:nt, :], lhsT=mA[:, c, tcol:tcol + nt], rhs=Wo[:, c, half * 512:(half + 1) * 512],
                                                                      start=(c == 0), stop=(c == 7)),
                             reads=akeys + [("Wo", c)], writes=[PSK(half)], inc=(c == 7))
                    for c in range(8):
                        P.op("pe", lambda e, c=c, half=half: e.matmul(ps[2 + half][0:nt, :], lhsT=mS[:, c, tcol:tcol + nt], rhs=Wo[:, 8 + c, half * 512:(half + 1) * 512],
                                                                      start=(c == 0), stop=(c == 7)),
                             reads=skeys + [("Wo", 8 + c)], writes=[PSK(2 + half)], inc=(c == 7))
                for half in range(2):
                    hs_ = slice(half * 512, (half + 1) * 512)
                    P.op("act", lambda e, half=half, hs_=hs_: e.activation(out=pre[b][0:nt, hs_], in_=ps[half][0:nt, :], func=AF.Identity, scale=rsa_col),
                         reads=[PSK(half), rak], writes=[("pre", b, half)])
                    P.op("dve", lambda e, hs_=hs_: e.scalar_tensor_tensor(out=pre[b][0:nt, hs_], in0=xt[b][0:nt, hs_], scalar=ALPHA, in1=pre[b][0:nt, hs_], op0=ALU.mult, op1=ALU.add),
                         reads=[("xt", b), ("pre", b, half)], writes=[("pre", b, half)])
                    P.op("dve", lambda e, half=half, hs_=hs_: e.scalar_tensor_tensor(out=pre[b][0:nt, hs_], in0=ps[2 + half][0:nt, :], scalar=rss_col, in1=pre[b][0:nt, hs_],
                                                                                     op0=ALU.mult, op1=ALU.add),
                         reads=[PSK(2 + half), rsk, ("pre", b, half)], writes=[("pre", b, half)])
                layer_norm(pre[b], hdn[b], nt, g1_bc, b1_bc, [("pre", b, 0), ("pre", b, 1)], ("hdn", b), "g1_bc", "b1_bc")
                P.dma("sp", scr_rows, hdn[b][0:nt, :], reads=[("hdn", b)], writes=["hdn_scr"])

                def back():
                    back_part(tb, nt, b, hT_dst, hT_col, hkey)
                return back

            def back_part(tb, nt, b, hT_dst, hT_col, hkey):
                for kc in range(8):
                    pt = 4 + kc // 4
                    P.op("pe", lambda e, kc=kc, pt=pt: e.transpose(out=ps[pt][:, (kc % 4) * 128:(kc % 4) * 128 + nt], in_=hdn[b][0:nt, kc * 128:(kc + 1) * 128], identity=ident[0:nt, 0:nt]),
                         reads=[("hdn", b), "ident"], writes=[PSK(pt)], inc=(kc % 4 == 3))
                P.op("act", lambda e: e.activation(out=hT_dst[:, 0:4, hT_col:hT_col + nt], in_=ps[4][:, :].rearrange("p (c t) -> p c t", c=4)[:, :, 0:nt], func=AF.Copy),
                     reads=[PSK(4)], writes=[(hkey, tb, 0)])
                P.op("dve", lambda e: e.tensor_copy(out=hT_dst[:, 4:8, hT_col:hT_col + nt], in_=ps[5][:, :].rearrange("p (c t) -> p c t", c=4)[:, :, 0:nt]),
                     reads=[PSK(5)], writes=[(hkey, tb, 1)])

            pendq = []
            for tb in range(NB):
                bk = outproj_block(tb, 128, mixT_a, mixT_s, tb * 128, mixa_keys, [("mixT_s", g, tb) for g in range(4)],
                                   rs_a[:, tb:tb + 1], rs_s[:, tb:tb + 1], xo[tb * 128:(tb + 1) * 128, :], hdnT_all, tb * 128,
                                   hdn_scr[tb * 128:(tb + 1) * 128, :], "hdnT")
                pendq.append(bk)
                if len(pendq) > 2:
                    pendq.pop(0)()
                stop_at(f"o{tb}")
            srs_a = sb(ph, "srs_a", [32, 1], F32)
            srs_s = sb(ph, "srs_s", [32, 1], F32)
            ssst = sb(ph, "ssst", [32, 1], F32)
            P.op("dve", lambda e: e.tensor_reduce(out=ssst[:], in_=sssq_s[:], axis=AXX, op=ALU.add), reads=[("sssq_s", g) for g in range(4)], writes=["ssst"])
            rstd_from_ssq(srs_a, sssq_a[:], 32, 1, mhalf, "sssq_a", "srs_a")
            rstd_from_ssq(srs_s, ssst[:], 32, 1, mhalf, "ssst", "srs_s")
            bk = outproj_block(16, 32, smixT_a, smixT_s, 0, [("smixT_a", hp) for hp in range(8)], [("smixT_s", g) for g in range(4)],
                               srs_a[:, 0:1], srs_s[:, 0:1], xs_d, shdnT, 0, hdn_scr[T:T + 32, :], "shdnT", rak="srs_a", rsk="srs_s")
            for f_ in pendq:
                f_()
            bk()
            if "shdn" in dbg:
                P.flush()
                P.dma("sp", dbg_out["shdn"], hdn_scr[T:T + 32, :], reads=["hdn_scr"])
            if "hdn" in dbg:
                P.flush()
                P.dma("sp", dbg_out["hdn"], hdn_scr[0:T, :], reads=["hdn_scr"])
            P.flush()

        msc.close()
        w_gate_v = w_gate.rearrange("(kc p) n -> p kc n", p=128)
        w_up_v = w_up.rearrange("(kc p) n -> p kc n", p=128)
        with ExitStack() as ph:
            Wd = sb(ph, "Wd", [128, NF, D], BF16)
            g2_bc = sb(ph, "g2_bc", [128, D], F32)
            b2_bc = sb(ph, "b2_bc", [128, D], F32)
            Wgu = [sb(ph, f"Wgu{i}", [128, 2, 8, 128], BF16) for i in range(3)]
            hTf = sb(ph, "hTf", [128, NF, 1056], BF16)
            sg = [sb(ph, f"sg{i}", [128, 512], BF16) for i in range(4)]
            hres = [sb(ph, f"hres{i}", [128, D], F32) for i in range(2)]
            ypre = [sb(ph, f"ypre{i}", [128, D], F32) for i in range(2)]
            yout = [sb(ph, f"yout{i}", [128, D], F32) for i in range(2)]
            wd_done = [False]
            P.dma("sp", g2_bc[:], ln2_g.partition_broadcast(128), writes=["g2_bc"])
            P.dma("sp", b2_bc[:], ln2_b.partition_broadcast(128), writes=["b2_bc"])
            wn = [0]
            bn = [0]

            def ffn_tile(segs, hkeys, blocks, tag):
                offs = []
                o_ = 0
                for (_, _, n_) in segs:
                    offs.append(o_); o_ += n_
                for f in range(NF):
                    wi = wn[0] % 3; wn[0] += 1
                    par = f % 2
                    P.dma("pool", Wgu[wi][:, 0, :, :], w_gate_v[:, :, f * 128:(f + 1) * 128], writes=[("Wgu", wi, 0)])
                    P.dma("pool", Wgu[wi][:, 1, :, :], w_up_v[:, :, f * 128:(f + 1) * 128], writes=[("Wgu", wi, 1)])
                    if not wd_done[0]:
                        P.dma("pool", Wd[:, f, :], w_down[f * 128:(f + 1) * 128, :], writes=[("Wd", f)])
                    for si, (hsrc, hcol, n_) in enumerate(segs):
                        if len(segs) <= 2:
                            pa, pb, ca, cb_ = 4 * par + si, 4 * par + 2 + si, 0, 0
                            sgi = 2 * par + si
                        elif si < 2:
                            pa, pb, ca, cb_ = 2 * si, 2 * si + 1, 0, 0
                            sgi = si
                        else:
                            pa, pb, ca, cb_ = 4, 4, 0, 32
                            sgi = 2
                        for gu, pp, cc in ((0, pa, ca), (1, pb, cb_)):
                            for kc in range(8):
                                P.op("pe", lambda e, kc=kc, gu=gu, pp=pp, cc=cc, wi=wi, hsrc=hsrc, hcol=hcol, n_=n_: e.matmul(
                                    ps[pp][:, cc:cc + n_], lhsT=Wgu[wi][:, gu, kc, :], rhs=hsrc[:, kc, hcol:hcol + n_], start=(kc == 0), stop=(kc == 7), skip_group_check=True),
                                    reads=[("Wgu", wi, gu)] + hkeys, writes=[PSK(pp)], inc=(kc == 7))
                        P.op("act", lambda e, pa=pa, ca=ca, sgi=sgi, n_=n_: e.activation(out=sg[sgi][:, 0:n_], in_=ps[pa][:, ca:ca + n_], func=AF.Silu), reads=[PSK(pa)], writes=[("sg", sgi)])
                        o2 = offs[si]
                        P.op("dve", lambda e, pb=pb, cb_=cb_, sgi=sgi, f=f, n_=n_, o2=o2: e.tensor_tensor(out=hTf[:, f, o2:o2 + n_], in0=ps[pb][:, cb_:cb_ + n_], in1=sg[sgi][:, 0:n_], op=ALU.mult),
                             reads=[PSK(pb), ("sg", sgi)], writes=[("hTf", f)])
                wd_done[0] = True
                for (col, nt, scr_rows, y_rows) in blocks:
                    b = bn[0] % 2; bn[0] += 1
                    P.dma("sp", hres[b][0:nt, :], scr_rows, reads=["hdn_scr"], writes=[("hres", b)])
                    for half in range(2):
                        pp = 4 + 2 * b + half
                        for f in range(NF):
                            P.op("pe", lambda e, f=f, half=half, pp=pp, col=col, nt=nt: e.matmul(ps[pp][0:nt, :], lhsT=hTf[:, f, col:col + nt], rhs=Wd[:, f, half * 512:(half + 1) * 512],
                                                                                                start=(f == 0), stop=(f == NF - 1)),
                                 reads=[("hTf", f), ("Wd", f)], writes=[PSK(pp)], inc=(f == NF - 1))
                        P.op("dve", lambda e, half=half, pp=pp, b=b, nt=nt: e.scalar_tensor_tensor(out=ypre[b][0:nt, half * 512:(half + 1) * 512], in0=hres[b][0:nt, half * 512:(half + 1) * 512],
                                                                                                 scalar=ALPHA, in1=ps[pp][0:nt, :], op0=ALU.mult, op1=ALU.add),
                             reads=[("hres", b), PSK(pp)], writes=[("ypre", b)])
                    layer_norm(ypre[b], yout[b], nt, g2_bc, b2_bc, ("ypre", b), ("yout", b), "g2_bc", "b2_bc")
                    P.dma("sp", y_rows, yout[b][0:nt, :], reads=[("yout", b)])
                stop_at(f"f{tag}")

            for j in range(T // 1024):
                segs_ = [(hdnT_all, j * 1024, 512), (hdnT_all, j * 1024 + 512, 512)]
                keys_ = [("hdnT", tb, k) for tb in range(8 * j, 8 * j + 8) for k in range(2)]
                blocks_ = [(i * 128, 128, hdn_scr[j * 1024 + i * 128:j * 1024 + (i + 1) * 128, :], y_o[j * 1024 + i * 128:j * 1024 + (i + 1) * 128, :]) for i in range(8)]
                if j == T // 1024 - 1:
                    segs_.append((shdnT, 0, 32))
                    keys_ += [("shdnT", 16, 0), ("shdnT", 16, 1)]
                    blocks_.append((1024, 32, hdn_scr[T:T + 32, :], y_s))
                ffn_tile(segs_, keys_, blocks_, j)
            P.flush()
    return nc


def make_in_maps(inputs):
    f = lambda a: np.ascontiguousarray(np.asarray(a, dtype=np.float32))
    xpr = f(inputs["x_prompt"]); xsm = f(inputs["x_sample"])
    ckw = f(inputs["cache_k_win"])[0]; cvw = f(inputs["cache_v_win"])[0]
    sss = f(inputs["state_ssm"])[0]; scv = f(inputs["state_conv"])[0]
    common = {
        "w_in": f(inputs["w_in"])[0], "conv_w": f(inputs["conv_w"])[0], "conv_b": f(inputs["conv_b"])[0],
        "dt_bias": f(inputs["dt_bias"])[0], "a_log": f(inputs["a_log"])[0], "d_skip": f(inputs["d_skip"])[0],
        "attn_g": f(inputs["attn_norm_g"])[0], "ssm_g": f(inputs["ssm_norm_g"])[0], "w_out": f(inputs["w_out"])[0],
        "ln1_g": f(inputs["ln1_g"])[0], "ln1_b": f(inputs["ln1_b"])[0], "w_gate": f(inputs["w_gate"])[0],
        "w_up": f(inputs["w_up"])[0], "w_down": f(inputs["w_down"])[0], "ln2_g": f(inputs["ln2_g"])[0], "ln2_b": f(inputs["ln2_b"])[0],
    }
    maps = []
    for c in range(8):
        b, hf = c // 2, c % 2
        m = dict(common)
        m["xo"] = np.ascontiguousarray(xpr[b, hf * T:(hf + 1) * T])
        m["xp"] = np.ascontiguousarray(xpr[b, 0:T]) if hf == 1 else np.zeros((T, D), np.float32)
        m["flag"] = np.full((128, 1), float(hf), np.float32)
        m["xs"] = np.ascontiguousarray(xsm[4 * c:4 * c + 4].reshape(32, D))
        m["ck"] = np.ascontiguousarray(ckw[4 * c:4 * c + 4].reshape(4, 2048, D))
        m["cv"] = np.ascontiguousarray(cvw[4 * c:4 * c + 4].reshape(4, 2048, D))
        m["sssm"] = np.ascontiguousarray(sss[4 * c:4 * c + 4].reshape(4, 1024, 128))
        m["sconv"] = np.ascontiguousarray(scv[4 * c:4 * c + 4].reshape(12, 2048))
        maps.append(m)
    return maps


def kernel(**inputs):
    nc = build()
    maps = make_in_maps(inputs)
    res = run_bass_kernel_spmd(nc, maps, core_ids=list(range(8)))
    R = res.results
    y_p = np.zeros((4, 4096, D), np.float32)
    kp = np.zeros((1, 4, 2048, 16, 64), np.float32); vp = np.zeros_like(kp)
    sp_ = np.zeros((1, 4, 16, 64, 128), np.float32); cp = np.zeros((1, 4, 3, 2048), np.float32)
    y_s = np.zeros((32, 8, D), np.float32)
    kn = np.zeros((1, 32, 8, 16, 64), np.float32); vn = np.zeros_like(kn)
    ss = np.zeros((1, 32, 16, 64, 128), np.float32); cs = np.zeros((1, 32, 3, 2048), np.float32)
    for c in range(8):
        b, hf = c // 2, c % 2
        r = R[c]
        y_p[b, hf * T:(hf + 1) * T] = r["y_o"]
        if hf == 1:
            kp[0, b] = r["k_o"].reshape(2048, 16, 64); vp[0, b] = r["v_o"].reshape(2048, 16, 64)
            sp_[0, b] = r["ssm_o"].reshape(16, 64, 128); cp[0, b] = r["conv_o"]
        y_s[4 * c:4 * c + 4] = r["y_s"].reshape(4, 8, D)
        kn[0, 4 * c:4 * c + 4] = r["kn_s"].reshape(4, 8, 16, 64); vn[0, 4 * c:4 * c + 4] = r["vn_s"].reshape(4, 8, 16, 64)
        ss[0, 4 * c:4 * c + 4] = r["ssm_s"].reshape(4, 16, 64, 128); cs[0, 4 * c:4 * c + 4] = r["conv_s"].reshape(4, 3, 2048)
    return (y_p, y_s, kp, vp, sp_, cp, kn, vn, ss, cs)
```
